# Optimizing a Trainium2 kernel written in Bass

```python
import jax, jax.numpy as jnp
from jax import lax
import numpy as np

D_MODEL = 1024
BATCH = 4
SEQ = 8192
DEPTH = 2

CHUNK = 64
N_MIXERS = 2
N_MEM = 256
XATTN_HEADS = 4
XATTN_HEAD_DIM = 64
XATTN_WIDTH = XATTN_HEADS * XATTN_HEAD_DIM
MIX_WIDTH = D_MODEL - XATTN_WIDTH
CONV_WIDTH = 3
POOL_WINDOWS = (2, 4, 8, 16)
POOL_GROUPS = len(POOL_WINDOWS)
POOL_GROUP_DIM = MIX_WIDTH // POOL_GROUPS
D_FF = 2816
RMS_EPS = 1e-6
N_LAYERS_A = (DEPTH + 1) // 2
N_LAYERS_B = DEPTH // 2

kernel_name = "hybrid_conv_pool_memxattn_macaron"


def rms_norm(x, g):
    xf = x.astype(jnp.float32)
    y = xf * lax.rsqrt(jnp.mean(xf * xf, axis=-1, keepdims=True) + RMS_EPS)
    return (y * g.astype(jnp.float32)).astype(x.dtype)


def swiglu(h, w_gu, w_down):
    g, u = jnp.split(h @ w_gu, 2, axis=-1)
    return (jax.nn.silu(g) * u) @ w_down


def causal_short_conv(u, w):
    s = u.shape[1]
    up = jnp.pad(u, ((0, 0), (CONV_WIDTH - 1, 0), (0, 0)))
    y = w[0] * up[:, 0:s]
    for k in range(1, CONV_WIDTH):
        y = y + w[k] * up[:, k:k + s]
    return y


def multiscale_pool(p, w_group, scale):
    s = p.shape[1]
    pos = jnp.arange(s, dtype=jnp.float32)[None, :, None]
    outs = []
    for gi, win in enumerate(POOL_WINDOWS):
        pg = p[..., gi * POOL_GROUP_DIM:(gi + 1) * POOL_GROUP_DIM]
        pf = pg.astype(jnp.float32)
        csum = jnp.cumsum(pf, axis=1)
        cfull = jnp.pad(csum, ((0, 0), (1, 0), (0, 0)))
        lower = jnp.pad(cfull[:, :s - win + 1], ((0, 0), (win - 1, 0), (0, 0)))
        count = jnp.minimum(pos + 1.0, float(win))
        mean = (csum - lower) / count
        outs.append(((mean - pf).astype(p.dtype)) @ w_group[gi])
    return jnp.concatenate(outs, axis=-1) * scale


def mem_cross_attn(q, mem_h, w_kv):
    b, s, _ = q.shape
    k, v = jnp.split(mem_h @ w_kv, 2, axis=-1)
    qh = q.reshape(b, s, XATTN_HEADS, XATTN_HEAD_DIM)
    kh = k.reshape(b, N_MEM, XATTN_HEADS, XATTN_HEAD_DIM)
    vh = v.reshape(b, N_MEM, XATTN_HEADS, XATTN_HEAD_DIM)
    scores = jnp.einsum('bshd,bmhd->bhsm', qh, kh).astype(jnp.float32) * (XATTN_HEAD_DIM ** -0.5)
    probs = jax.nn.softmax(scores, axis=-1).astype(q.dtype)
    out = jnp.einsum('bhsm,bmhd->bshd', probs, vh)
    return out.reshape(b, s, XATTN_WIDTH)


def setup_inputs(seed: int = 0) -> dict:
    key = jax.random.key(seed)
    ks = jax.random.split(key, 24)
    f32 = jnp.float32

    def nrm(k, shape, scale):
        return jax.random.normal(k, shape, f32) * scale

    def gain(k, shape):
        return 1.0 + 0.05 * jax.random.normal(k, shape, f32)

    return {
        "x": jax.random.normal(ks[0], (BATCH, SEQ, D_MODEL), f32),
        "mem": jax.random.normal(ks[1], (BATCH, N_MEM, D_MODEL), f32),
        "ffn1_norm": gain(ks[2], (DEPTH, D_MODEL)),
        "ffn1_w_gu": nrm(ks[3], (DEPTH, D_MODEL, 2 * D_FF), D_MODEL ** -0.5),
        "ffn1_w_down": nrm(ks[4], (DEPTH, D_FF, D_MODEL), D_FF ** -0.5),
        "mix_norm": gain(ks[5], (DEPTH, D_MODEL)),
        "mem_norm": gain(ks[6], (DEPTH, D_MODEL)),
        "w_kv": nrm(ks[7], (DEPTH, D_MODEL, 2 * XATTN_WIDTH), D_MODEL ** -0.5),
        "w_out": nrm(ks[8], (DEPTH, D_MODEL, D_MODEL), D_MODEL ** -0.5),
        "conv_w_in": nrm(ks[9], (N_LAYERS_A, D_MODEL, 3 * MIX_WIDTH + XATTN_WIDTH), D_MODEL ** -0.5),
        "conv_w": nrm(ks[10], (N_LAYERS_A, CONV_WIDTH, MIX_WIDTH), CONV_WIDTH ** -0.5),
        "pool_w_in": nrm(ks[11], (N_LAYERS_B, D_MODEL, MIX_WIDTH + XATTN_WIDTH), D_MODEL ** -0.5),
        "pool_w_group": nrm(ks[12], (N_LAYERS_B, POOL_GROUPS, POOL_GROUP_DIM, POOL_GROUP_DIM), POOL_GROUP_DIM ** -0.5),
        "pool_scale": 1.0 + 0.1 * jax.random.normal(ks[13], (N_LAYERS_B, MIX_WIDTH), f32),
        "ffn2_norm": gain(ks[14], (DEPTH, D_MODEL)),
        "ffn2_w_gu": nrm(ks[15], (DEPTH, D_MODEL, 2 * D_FF), D_MODEL ** -0.5),
        "ffn2_w_down": nrm(ks[16], (DEPTH, D_FF, D_MODEL), D_FF ** -0.5),
        "final_norm": gain(ks[17], (D_MODEL,)),
    }


def reference(x, mem, ffn1_norm, ffn1_w_gu, ffn1_w_down, mix_norm, mem_norm, w_kv, w_out,
              conv_w_in, conv_w, pool_w_in, pool_w_group, pool_scale,
              ffn2_norm, ffn2_w_gu, ffn2_w_down, final_norm):
    ia = 0
    ib = 0
    for i in range(DEPTH):
        x = x + 0.5 * swiglu(rms_norm(x, ffn1_norm[i]), ffn1_w_gu[i], ffn1_w_down[i])
        h = rms_norm(x, mix_norm[i])
        mem_h = rms_norm(mem, mem_norm[i])
        if i % N_MIXERS == 0:
            z = h @ conv_w_in[ia]
            gate_b = z[..., :MIX_WIDTH]
            gate_c = z[..., MIX_WIDTH:2 * MIX_WIDTH]
            val = z[..., 2 * MIX_WIDTH:3 * MIX_WIDTH]
            q = z[..., 3 * MIX_WIDTH:]
            mix = gate_b * causal_short_conv(gate_c * val, conv_w[ia])
            ia += 1
        else:
            z = h @ pool_w_in[ib]
            mix = multiscale_pool(z[..., :MIX_WIDTH], pool_w_group[ib], pool_scale[ib])
            q = z[..., MIX_WIDTH:]
            ib += 1
        att = mem_cross_attn(q, mem_h, w_kv[i])
        x = x + jnp.concatenate([mix, att], axis=-1) @ w_out[i]
        x = x + 0.5 * swiglu(rms_norm(x, ffn2_norm[i]), ffn2_w_gu[i], ffn2_w_down[i])
    return rms_norm(x, final_norm)
```

```python
import numpy as np
import concourse.bass as bass
import concourse.mybir as mybir
from concourse.bass_utils import run_bass_kernel_spmd

F32 = mybir.dt.float32
BF16 = mybir.dt.bfloat16
AF = mybir.ActivationFunctionType
ALU = mybir.AluOpType

D = 1024
DFF = 2816
NMEM = 256
MIXW = 768
EPS = 1e-6
OWN = 1024
HALO = 32
TT = OWN + HALO
NB = 3
BS = TT // NB
NCORE = 8
NSLOT = 7
SLOT_ELEMS = 2816
PADL = 16
PW = PADL + TT

C_FFN1 = 0
C_MIX = 16
C_MEM = 32
C_FFN2 = 48
C_FIN = 64
C_CONV = 72
C_PSC = 90
C_MASK = 98
C_CORR = 99
NCST = 164

ENGS = ("pe", "act", "dve", "pool", "sp")


class Buf:
    __slots__ = ("w", "r", "name")

    def __init__(self, name=""):
        self.w = {}
        self.r = {}
        self.name = name


class DmaSem:
    def __init__(self, key):
        self.key = key
        self.count = 0


class Sched:
    def __init__(self):
        self.ops = {e: [] for e in ENGS}
        self.cnt = {e: 0 for e in ENGS}
        self.waited = {e: {} for e in ENGS}

    def add(self, eng, fn, reads=(), writes=(), dma=None):
        deps = {}

        def dep(key, val, raw):
            if key == eng:
                if eng == "pe" or not raw:
                    return
                if self.cnt[eng] + 1 - val > 2:
                    return
            if val > deps.get(key, 0):
                deps[key] = val

        for b in reads:
            for k, v in b.w.items():
                dep(k, v, True)
        for b in writes:
            for k, v in b.w.items():
                dep(k, v, False)
            for k, v in b.r.items():
                dep(k, v, False)
        wd = self.waited[eng]
        waits = []
        for k, v in deps.items():
            if wd.get(k, 0) < v:
                waits.append((k, v))
                wd[k] = v
        if dma is not None:
            dma.count += 16
            tok = (dma.key, dma.count)
            inc = (dma.key, 16)
        else:
            self.cnt[eng] += 1
            tok = (eng, self.cnt[eng])
            inc = (eng, 1)
        self.ops[eng].append((waits, fn, inc))
        for b in reads:
            if b.r.get(tok[0], 0) < tok[1]:
                b.r[tok[0]] = tok[1]
        for b in writes:
            if b.w.get(tok[0], 0) < tok[1]:
                b.w[tok[0]] = tok[1]
        return tok


def build_program(NT, nstages=6, final_norm=True):
    nc = bass.Bass("TRN2", target_bir_lowering=False)
    S = Sched()
    NROWS = NT * OWN + HALO

    def din(name, shape):
        return nc.dram_tensor(name, list(shape), F32, kind="ExternalInput").ap()

    x_d = din("x", (NROWS, D))
    mem_d = din("mem", (NMEM, D))
    ident_d = din("ident", (128, 128))
    cst_d = din("cst", (128, NCST))
    w_gu_d = [din("ffn1_w_gu", (2, D, 2 * DFF)), din("ffn2_w_gu", (2, D, 2 * DFF))]
    w_dn_d = [din("ffn1_w_down", (2, DFF, D)), din("ffn2_w_down", (2, DFF, D))]
    w_kv_d = din("w_kv", (2, D, 512))
    w_out_d = din("w_out", (2, D, D))
    conv_in_d = din("conv_w_in", (1, D, 2560))
    pool_in_d = din("pool_w_in", (1, D, D))
    pool_g_d = din("pool_w_group", (1, 4, 192, 192))
    y_d = nc.dram_tensor("y", [NT * OWN, D], F32, kind="ExternalOutput").ap()

    sb = nc.alloc_sbuf_tensor
    xT = sb("xT", [128, 8, TT], F32)
    hT = sb("hT", [128, 8, TT], BF16)
    A = sb("A", [128, 22 * TT], BF16)
    mixb = sb("mixb", [128, 8, TT], BF16)
    attb = sb("attb", [128, 2, TT], BF16)
    qb = sb("qb", [128, 2, TT], BF16)
    slots = [sb(f"ws{i}", [128, SLOT_ELEMS], BF16) for i in range(NSLOT)]
    ebuf = [sb(f"eb{i}", [128, 4, BS], BF16) for i in range(2)]
    rcb = [sb(f"rc{i}", [128, BS], F32) for i in range(2)]
    sqb = sb("sqb", [128, 8, BS], BF16)
    msb = [sb(f"ms{i}", [128, BS], F32) for i in range(2)]
    rsb = [sb(f"rs{i}", [128, BS], F32) for i in range(2)]
    tmpb = [sb(f"tmp{i}", [128, BS], F32) for i in range(3)]
    xio = [sb(f"xio{i}", [128, D], F32) for i in range(3)]
    kT = sb("kT", [128, 2, 2, NMEM], BF16)
    vS = sb("vS", [128, 2, 2, 256], BF16)
    ident = sb("ident_s", [128, 128], F32)
    ones = sb("ones_s", [128, 128], BF16)
    cst = sb("cst_s", [128, NCST], F32)
    psum = [nc.alloc_psum_tensor(f"ps{i}", [128, 512], F32) for i in range(8)]

    def A_view(off_bf16, shape, dtype):
        n = int(np.prod(shape[1:]))
        if dtype == F32:
            ap = A[:, off_bf16:off_bf16 + 2 * n].bitcast(F32)
        else:
            ap = A[:, off_bf16:off_bf16 + n]
        if len(shape) == 3:
            ap = ap.rearrange("p (a b) -> p a b", a=shape[1])
        return ap

    aT = A_view(0, [128, 22, TT], BF16)
    yT = A_view(0, [128, 8, TT], F32)
    memT = A_view(0, [128, 8, NMEM], F32)
    ubuf = [A_view(i * 2 * (TT + 2), [128, TT + 2], F32) for i in range(2)]
    o1 = 2 * 2 * (TT + 2)
    ybuf = [A_view(o1 + i * 2 * TT, [128, TT], F32) for i in range(2)]
    pbuf = [A_view(i * 2 * PW, [128, PW], F32) for i in range(2)]
    sbufs = [A_view((2 + i) * 2 * PW, [128, PW], F32) for i in range(2)]
    o2 = 4 * 2 * PW
    dbuf = A_view(o2, [128, 8, TT], BF16)

    class NS:
        pass

    B = NS()
    B.x = [[Buf() for _ in range(NB)] for _ in range(8)]
    B.h = [Buf() for _ in range(NB)]
    B.a = [[Buf() for _ in range(NB)] for _ in range(22)]
    B.A = Buf()
    B.mix = [[Buf() for _ in range(NB)] for _ in range(8)]
    B.att = [[Buf() for _ in range(NB)] for _ in range(2)]
    B.q = [[Buf() for _ in range(NB)] for _ in range(2)]
    B.slot = [Buf() for _ in range(NSLOT)]
    B.e = [Buf() for _ in range(2)]
    B.rc = [Buf() for _ in range(2)]
    B.sq = Buf()
    B.ms = [Buf() for _ in range(2)]
    B.rs = [Buf() for _ in range(2)]
    B.tmp = [Buf() for _ in range(3)]
    B.xio = [Buf() for _ in range(3)]
    B.kv = Buf()
    B.const = Buf()
    B.ps = [Buf() for _ in range(8)]
    B.u = [Buf() for _ in range(2)]
    B.y = [Buf() for _ in range(2)]
    B.p = [Buf() for _ in range(2)]
    B.s = [Buf() for _ in range(2)]
    B.d = [Buf() for _ in range(8)]
    B.yT = [Buf() for _ in range(NB)]
    B.memT = Buf()

    sems = {}

    def new_dma_sem(name):
        sems[name] = None
        return DmaSem(name)

    slot_sem = [new_dma_sem(f"d_slot{i}") for i in range(NSLOT)]
    xio_sem = [new_dma_sem(f"d_xio{i}") for i in range(3)]
    const_sem = new_dma_sem("d_const")
    out_sems = xio_sem

    state = {"ps": 0, "tmp": 0, "nrm": 0, "xio": 0, "e": 0}

    def ps_next():
        i = state["ps"]
        state["ps"] = (i + 1) % 8
        return psum[i], B.ps[i]

    def blk(tb):
        return slice(tb * BS, (tb + 1) * BS)

    units = []
    wstate = {"emitted": 0, "next": 0}

    def sview(slot, off, P, shape):
        n = int(np.prod(shape))
        ap = slot[0:P, off:off + n]
        if len(shape) == 2:
            ap = ap.rearrange("p (a b) -> p a b", a=shape[0])
        elif len(shape) == 3:
            ap = ap.rearrange("p (a b c) -> p a b c", a=shape[0], b=shape[1])
        return ap

    def kview(dram2d, r0, nk, c0, ncol, P=128):
        return dram2d[r0:r0 + nk * P, c0:c0 + ncol].rearrange("(k p) n -> p k n", p=P)

    def emit_unit_dma(idx):
        name, parts = units[idx]
        s = idx % NSLOT
        for (dst_fn, src) in parts:
            dst = dst_fn(slots[s])

            def fn(e, dst=dst, src=src):
                return e.dma_start(out=dst, in_=src)

            S.add("pool", fn, reads=(), writes=(B.slot[s],), dma=slot_sem[s])

    def wnext(name):
        i = wstate["next"]
        assert units[i][0] == name, (units[i][0], name)
        lim = min(len(units), i + NSLOT - 1)
        while wstate["emitted"] < lim:
            emit_unit_dma(wstate["emitted"])
            wstate["emitted"] += 1
        wstate["next"] = i + 1
        s = i % NSLOT
        return slots[s], B.slot[s]

    def U(name, parts):
        units.append((name, parts))

    def list_ffn_units(l, which):
        gu = w_gu_d[which][l]
        dn = w_dn_d[which][l]
        for fp in range(11):
            U(f"G{l}{which}_{fp}", [(lambda s: sview(s, 0, 128, [8, 256]), kview(gu, 0, 8, 256 * fp, 256))])
            U(f"U{l}{which}_{fp}", [(lambda s: sview(s, 0, 128, [8, 256]), kview(gu, 0, 8, DFF + 256 * fp, 256))])
        for op in range(4):
            for kh in range(2):
                U(f"D{l}{which}_{op}_{kh}",
                  [(lambda s: sview(s, 0, 128, [11, 256]), kview(dn, kh * 1408, 11, 256 * op, 256))])

    def list_mix_units(l):
        if l == 0:
            w = conv_in_d[0]
            for c in range(6):
                U(f"GCV_{c}", [
                    (lambda s: sview(s, 0, 128, [8, 256])[:, :, 0:128], kview(w, 0, 8, MIXW + 128 * c, 128)),
                    (lambda s: sview(s, 0, 128, [8, 256])[:, :, 128:256], kview(w, 0, 8, 2 * MIXW + 128 * c, 128)),
                ])
                if c % 2 == 1:
                    U(f"GB_{c // 2}", [(lambda s: sview(s, 0, 128, [8, 256]), kview(w, 0, 8, 256 * (c // 2), 256))])
            U("Q0", [(lambda s: sview(s, 0, 128, [8, 256]), kview(w, 0, 8, 3 * MIXW, 256))])
            wo = w_out_d[0]
            for op in range(4):
                U(f"WO0_{op}", [(lambda s: sview(s, 0, 128, [8, 256]), kview(wo, 0, 8, 256 * op, 256))])
        else:
            w = pool_in_d[0]
            for g in range(4):
                U(f"PW_{g}", [(lambda s: sview(s, 0, 128, [8, 192]), kview(w, 0, 8, 192 * g, 192))])
            U("Q1", [(lambda s: sview(s, 0, 128, [8, 256]), kview(w, 0, 8, MIXW, 256))])
            pg = pool_g_d[0].rearrange("g (kc p) n -> p g kc n", p=96)
            U("PG", [(lambda s: sview(s, 0, 96, [4, 2, 192]), pg)])
            wo = w_out_d[1]
            for op in range(4):
                U(f"WO1_{op}", [
                    (lambda s: sview(s, 0, 96, [8, 256]), kview(wo, 0, 8, 256 * op, 256, P=96)),
                    (lambda s: sview(s, 2048, 128, [2, 256]), kview(wo, MIXW, 2, 256 * op, 256)),
                ])

    for l in range(2):
        U(f"KVK{l}", [(lambda s: sview(s, 0, 128, [8, 256]), kview(w_kv_d[l], 0, 8, 0, 256))])
        U(f"KVV{l}", [(lambda s: sview(s, 0, 128, [8, 256]), kview(w_kv_d[l], 0, 8, 256, 256))])
    stage_list = []
    for l in range(2):
        stage_list += [("ffn", l, 0), ("mix", l, 0), ("ffn", l, 1)]
    stage_list = stage_list[:nstages]
    for t in range(NT):
        for (kind, l, which) in stage_list:
            if kind == "ffn":
                list_ffn_units(l, which)
            else:
                list_mix_units(l)

    def mm_group(out_ap, pairs, reads, psB):
        n = len(pairs)

        def fn(e):
            ins = None
            for i, (l, r) in enumerate(pairs):
                ins = e.matmul(out_ap, l, r, start=(i == 0), stop=(i == n - 1))
            return ins

        return S.add("pe", fn, reads=reads, writes=(psB,))

    def act(out, in_, func, reads, writes, scale=None, bias=None):
        kw = {}
        if scale is not None:
            kw["scale"] = scale
        if bias is not None:
            kw["bias"] = bias

        def fn(e):
            return e.activation(out=out, in_=in_, func=func, **kw)

        return S.add("act", fn, reads=reads, writes=writes)

    def dve_tt(out, in0, in1, op, reads, writes, eng="dve"):
        def fn(e):
            return e.tensor_tensor(out=out, in0=in0, in1=in1, op=op)

        return S.add(eng, fn, reads=reads, writes=writes)

    def dve_ts(out, in0, s1, s2, op0, op1, reads, writes, eng="dve"):
        def fn(e):
            if op1 is None:
                return e.tensor_scalar(out=out, in0=in0, scalar1=s1, scalar2=None, op0=op0)
            return e.tensor_scalar(out=out, in0=in0, scalar1=s1, scalar2=s2, op0=op0, op1=op1)

        return S.add(eng, fn, reads=reads, writes=writes)

    def dve_stt(out, in0, scalar, in1, op0, op1, reads, writes):
        def fn(e):
            return e.scalar_tensor_tensor(out=out, in0=in0, scalar=scalar, in1=in1, op0=op0, op1=op1)

        return S.add("dve", fn, reads=reads, writes=writes)

    def memset(eng, ap, val, writes, reads=()):
        def fn(e):
            return e.memset(ap, val)

        return S.add(eng, fn, reads=reads, writes=writes)

    def norm_block(src_fn, sbufs_r, sl, gcol, dst_fn, dst_bufs_fn, extra=()):
        n = sl.stop - sl.start
        i2 = state["nrm"] % 2
        state["nrm"] += 1
        src = src_fn(sl)
        for hf in range(2):
            act(sqb[:, 4 * hf:4 * hf + 4, 0:n], src[:, 4 * hf:4 * hf + 4, :], AF.Square,
                reads=sbufs_r[4 * hf:4 * hf + 4], writes=(B.sq,))
        ps, psB = ps_next()
        mm_group(ps[:, 0:n], [(ones[:, :], sqb[:, c, 0:n]) for c in range(8)],
                 reads=(B.sq, B.const), psB=psB)
        act(msb[i2][:, 0:n], ps[:, 0:n], AF.Ln, reads=(psB,), writes=(B.ms[i2],), scale=1.0 / D, bias=EPS)
        act(rsb[i2][:, 0:n], msb[i2][:, 0:n], AF.Exp, reads=(B.ms[i2],), writes=(B.rs[i2],), scale=-0.5)
        for c in range(8):
            dve_stt(dst_fn(c, sl), src[:, c, :], cst[:, gcol + c:gcol + c + 1], rsb[i2][:, 0:n],
                    ALU.mult, ALU.mult,
                    reads=(sbufs_r[c], B.rs[i2], B.const) + tuple(extra), writes=dst_bufs_fn(c))

    tile_blocks = [blk(tb) for tb in range(NB)]

    def fence_A():
        for e_ in ("pe", "act", "dve", "pool"):
            if S.cnt[e_]:
                B.A.w[e_] = S.cnt[e_]

    pend = {"gcol": None, "dst": "h", "done": set()}

    def set_pend(gcol, dst="h"):
        pend["gcol"] = gcol
        pend["dst"] = dst
        pend["done"] = set()

    def need_h(tb):
        if pend["gcol"] is None or tb in pend["done"]:
            return
        pend["done"].add(tb)
        xb = [B.x[c][tb] for c in range(8)]
        if pend["dst"] == "h":
            norm_block(lambda sl: xT[:, :, sl], xb, blk(tb), pend["gcol"],
                       lambda c, sl: hT[:, c, sl], lambda c: (B.h[tb],))
        else:
            fence_A()
            norm_block(lambda sl: xT[:, :, sl], xb, blk(tb), pend["gcol"],
                       lambda c, sl: yT[:, c, sl], lambda c: (B.yT[tb],), extra=(B.A,))

    def ffn(l, which, nxt):
        fence_A()
        for fp in range(11):
            gs, gB = wnext(f"G{l}{which}_{fp}")
            us, uB = wnext(f"U{l}{which}_{fp}")
            gv = sview(gs, 0, 128, [8, 256])
            uv = sview(us, 0, 128, [8, 256])
            for tb in range(NB):
                for f in (2 * fp, 2 * fp + 1):
                    cs = slice((f % 2) * 128, (f % 2) * 128 + 128)
                    sl = blk(tb)
                    need_h(tb)
                    pg, pgB = ps_next()
                    pu, puB = ps_next()
                    mm_group(pg[:, 0:BS], [(gv[:, k, cs], hT[:, k, sl]) for k in range(8)],
                             reads=(gB, B.h[tb]), psB=pgB)
                    mm_group(pu[:, 0:BS], [(uv[:, k, cs], hT[:, k, sl]) for k in range(8)],
                             reads=(uB, B.h[tb]), psB=puB)
                    ti = state["tmp"] % 3
                    state["tmp"] += 1
                    act(tmpb[ti][:, :], pg[:, 0:BS], AF.Silu, reads=(pgB,), writes=(B.tmp[ti],))
                    dve_tt(aT[:, f, sl], tmpb[ti][:, :], pu[:, 0:BS], ALU.mult,
                           reads=(B.tmp[ti], puB, B.A), writes=(B.a[f][tb],))
                    if fp == 0 and tb == 0 and f == 0:
                        need_h(NB - 1)
        for op in range(4):
            d0, d0B = wnext(f"D{l}{which}_{op}_0")
            d1, d1B = wnext(f"D{l}{which}_{op}_1")
            dv = [sview(d0, 0, 128, [11, 256]), sview(d1, 0, 128, [11, 256])]
            for o in (2 * op, 2 * op + 1):
                cs = slice((o % 2) * 128, (o % 2) * 128 + 128)
                for tb in range(NB):
                    sl = blk(tb)
                    ps, psB = ps_next()
                    mm_group(ps[:, 0:BS], [(dv[k // 11][:, k % 11, cs], aT[:, k, sl]) for k in range(22)],
                             reads=[d0B, d1B] + [B.a[k][tb] for k in range(22)], psB=psB)
                    dve_stt(xT[:, o, sl], ps[:, 0:BS], 0.5, xT[:, o, sl], ALU.mult, ALU.add,
                            reads=(psB, B.x[o][tb]), writes=(B.x[o][tb],))
                    if o == 7:
                        if tb == 0:
                            set_pend(*nxt)
                        else:
                            need_h(tb - 1)
        need_h(NB - 2)

    def q_proj(name, ws, wB):
        wv = sview(ws, 0, 128, [8, 256])
        for j in range(2):
            cs = slice(j * 128, j * 128 + 128)
            for tb in range(NB):
                sl = blk(tb)
                ps, psB = ps_next()
                mm_group(ps[:, 0:BS], [(wv[:, k, cs], hT[:, k, sl]) for k in range(8)],
                         reads=(wB, B.h[tb]), psB=psB)
                act(qb[:, j, sl], ps[:, 0:BS], AF.Copy, reads=(psB,), writes=(B.q[j][tb],))

    def attention(l):
        for tb in range(NB):
            sl = blk(tb)
            for j in range(2):
                ei = state["e"] % 2
                state["e"] += 1
                for hh in range(2):
                    r = slice(64 * hh, 64 * hh + 64)
                    for mc in range(2):
                        ps, psB = ps_next()
                        mm_group(ps[:, 0:BS], [(kT[r, l, j, mc * 128:(mc + 1) * 128], qb[r, j, sl])],
                                 reads=(B.kv, B.q[j][tb]), psB=psB)
                        act(ebuf[ei][:, hh * 2 + mc, :], ps[:, 0:BS], AF.Exp, reads=(psB,),
                            writes=(B.e[ei],), scale=0.125)
                ppv, ppvB = ps_next()
                psm, psmB = ps_next()
                for hh in range(2):
                    r = slice(64 * hh, 64 * hh + 64)
                    h = 2 * j + hh
                    mm_group(ppv[r, 0:BS],
                             [(vS[:, l, mc, h * 64:(h + 1) * 64], ebuf[ei][:, hh * 2 + mc, :]) for mc in range(2)],
                             reads=(B.kv, B.e[ei]), psB=ppvB)
                    mm_group(psm[r, 0:BS],
                             [(ones[:, 0:64], ebuf[ei][:, hh * 2 + mc, :]) for mc in range(2)],
                             reads=(B.const, B.e[ei]), psB=psmB)

                act(rcb[ei][:, :], psm[:, 0:BS], AF.Ln, reads=(psmB,), writes=(B.rc[ei],))
                act(rcb[ei][:, :], rcb[ei][:, :], AF.Exp, reads=(B.rc[ei],), writes=(B.rc[ei],), scale=-1.0)
                dve_tt(attb[:, j, sl], ppv[:, 0:BS], rcb[ei][:, :], ALU.mult,
                       reads=(ppvB, B.rc[ei]), writes=(B.att[j][tb],))

    def w_out_stage(l, nxt):
        for op in range(4):
            ws, wB = wnext(f"WO{l}_{op}")
            if l == 0:
                wv = sview(ws, 0, 128, [8, 256])
            else:
                wa = sview(ws, 0, 96, [8, 256])
                wb_ = sview(ws, 2048, 128, [2, 256])
            for o in (2 * op, 2 * op + 1):
                cs = slice((o % 2) * 128, (o % 2) * 128 + 128)
                for tb in range(NB):
                    sl = blk(tb)
                    ps, psB = ps_next()
                    if l == 0:
                        pairs = [(wv[:, k, cs], mixb[:, k, sl]) for k in range(6)]
                        pairs += [(wv[:, 6 + j, cs], attb[:, j, sl]) for j in range(2)]
                        rd = [wB] + [B.mix[k][tb] for k in range(6)] + [B.att[j][tb] for j in range(2)]
                    else:
                        pairs = [(wa[:, k, cs], mixb[0:96, k, sl]) for k in range(8)]
                        pairs += [(wb_[:, j, cs], attb[:, j, sl]) for j in range(2)]
                        rd = [wB] + [B.mix[k][tb] for k in range(8)] + [B.att[j][tb] for j in range(2)]
                    mm_group(ps[:, 0:BS], pairs, reads=rd, psB=psB)
                    dve_tt(xT[:, o, sl], ps[:, 0:BS], xT[:, o, sl], ALU.add,
                           reads=(psB, B.x[o][tb]), writes=(B.x[o][tb],))
                    if o == 7:
                        if tb == 0:
                            set_pend(*nxt)
                        else:
                            need_h(tb - 1)
        need_h(NB - 2)

    def mixer_conv(first_tile, nxt):
        fence_A()
        for i in range(2):
            memset("dve", ubuf[i][:, 0:2], 0.0, (B.u[i],), reads=(B.A,))
        for c in range(6):
            ws, wB = wnext(f"GCV_{c}")
            wv = sview(ws, 0, 128, [8, 256])
            ui = c % 2
            for tb in range(NB):
                sl = blk(tb)
                need_h(tb)
                p1, p1B = ps_next()
                p2, p2B = ps_next()
                mm_group(p1[:, 0:BS], [(wv[:, k, 0:128], hT[:, k, sl]) for k in range(8)],
                         reads=(wB, B.h[tb]), psB=p1B)
                mm_group(p2[:, 0:BS], [(wv[:, k, 128:256], hT[:, k, sl]) for k in range(8)],
                         reads=(wB, B.h[tb]), psB=p2B)
                ti = state["tmp"] % 3
                state["tmp"] += 1
                act(tmpb[ti][:, :], p2[:, 0:BS], AF.Copy, reads=(p2B,), writes=(B.tmp[ti],))
                dve_tt(ubuf[ui][:, 2 + tb * BS:2 + (tb + 1) * BS], p1[:, 0:BS], tmpb[ti][:, :], ALU.mult,
                       reads=(p1B, B.tmp[ti], B.A), writes=(B.u[ui],))
                if c == 0 and tb == 0:
                    need_h(NB - 1)
            if first_tile:
                dve_ts(ubuf[ui][:, 2:2 + HALO], ubuf[ui][:, 2:2 + HALO], cst[:, C_MASK:C_MASK + 1], None,
                       ALU.mult, None, reads=(B.u[ui], B.const), writes=(B.u[ui],))
            cw = lambda tap: cst[:, C_CONV + 3 * c + tap:C_CONV + 3 * c + tap + 1]
            dve_ts(ybuf[ui][:, :], ubuf[ui][:, 2:2 + TT], cw(2), None, ALU.mult, None,
                   reads=(B.u[ui], B.const, B.A), writes=(B.y[ui],))
            dve_stt(ybuf[ui][:, :], ubuf[ui][:, 1:1 + TT], cw(1), ybuf[ui][:, :], ALU.mult, ALU.add,
                    reads=(B.u[ui], B.y[ui], B.const), writes=(B.y[ui],))
            dve_stt(ybuf[ui][:, :], ubuf[ui][:, 0:TT], cw(0), ybuf[ui][:, :], ALU.mult, ALU.add,
                    reads=(B.u[ui], B.y[ui], B.const), writes=(B.y[ui],))
            if c % 2 == 1:
                ws2, wB2 = wnext(f"GB_{c // 2}")
                wv2 = sview(ws2, 0, 128, [8, 256])
                for cc in (c - 1, c):
                    cs = slice((cc % 2) * 128, (cc % 2) * 128 + 128)
                    for tb in range(NB):
                        sl = blk(tb)
                        ps, psB = ps_next()
                        mm_group(ps[:, 0:BS], [(wv2[:, k, cs], hT[:, k, sl]) for k in range(8)],
                                 reads=(wB2, B.h[tb]), psB=psB)
                        dve_tt(mixb[:, cc, sl], ps[:, 0:BS], ybuf[cc % 2][:, sl], ALU.mult,
                               reads=(psB, B.y[cc % 2]), writes=(B.mix[cc][tb],))
        ws, wB = wnext("Q0")
        q_proj("Q0", ws, wB)
        attention(0)
        w_out_stage(0, nxt)

    def mixer_pool(first_tile, nxt):
        fence_A()
        for i in range(2):
            memset("dve", pbuf[i][:, 0:PADL], 0.0, (B.p[i],), reads=(B.A,))
        for g in range(4):
            ws, wB = wnext(f"PW_{g}")
            wv = sview(ws, 0, 128, [8, 192])
            win = 2 << g
            for i in (2 * g, 2 * g + 1):
                pi = i % 2
                cs = slice((i % 2) * 96, (i % 2) * 96 + 96)
                for tb in range(NB):
                    sl = blk(tb)
                    need_h(tb)
                    ps, psB = ps_next()
                    mm_group(ps[0:96, 0:BS], [(wv[:, k, cs], hT[:, k, sl]) for k in range(8)],
                             reads=(wB, B.h[tb]), psB=psB)
                    act(pbuf[pi][0:96, PADL + tb * BS:PADL + (tb + 1) * BS], ps[0:96, 0:BS], AF.Copy,
                        reads=(psB, B.A), writes=(B.p[pi],))
                P = pbuf[pi]
                if first_tile:
                    dve_ts(P[0:96, PADL:PADL + HALO], P[0:96, PADL:PADL + HALO], cst[0:96, C_MASK:C_MASK + 1],
                           None, ALU.mult, None, reads=(B.p[pi], B.const), writes=(B.p[pi],))
                cur, curB = P, B.p[pi]
                sh = 1
                si = 0
                while sh < win:
                    lo = 2 * sh - 1
                    dst, dstB = sbufs[si], B.s[si]
                    dve_tt(dst[0:96, lo:PW], cur[0:96, lo:PW], cur[0:96, lo - sh:PW - sh], ALU.add,
                           reads=(curB, B.A), writes=(dstB,))
                    cur, curB = dst, dstB
                    si ^= 1
                    sh *= 2
                if first_tile:
                    c0 = PADL + HALO
                    dve_tt(cur[0:96, c0:c0 + 16], cur[0:96, c0:c0 + 16],
                           cst[0:96, C_CORR + 16 * g:C_CORR + 16 * g + 16], ALU.mult,
                           reads=(curB, B.const), writes=(curB,))
                dve_stt(dbuf[0:96, i, :], cur[0:96, PADL:PW], 1.0 / win, P[0:96, PADL:PW], ALU.mult, ALU.subtract,
                        reads=(curB, B.p[pi], B.A), writes=(B.d[i],))
        ws, wB = wnext("Q1")
        q_proj("Q1", ws, wB)
        attention(1)
        ws, wB = wnext("PG")
        pgv = sview(ws, 0, 96, [4, 2, 192])
        for g in range(4):
            for oc in range(2):
                i = 2 * g + oc
                for tb in range(NB):
                    sl = blk(tb)
                    ps, psB = ps_next()
                    mm_group(ps[0:96, 0:BS],
                             [(pgv[:, g, kc, oc * 96:(oc + 1) * 96], dbuf[0:96, 2 * g + kc, sl]) for kc in range(2)],
                             reads=(wB, B.d[2 * g], B.d[2 * g + 1]), psB=psB)
                    act(mixb[0:96, i, sl], ps[0:96, 0:BS], AF.Identity, reads=(psB, B.const),
                        writes=(B.mix[i][tb],), scale=cst[0:96, C_PSC + i:C_PSC + i + 1])
        w_out_stage(1, nxt)

    def dma_sp(out, in_, reads, writes, sem):
        def fn(e):
            return e.dma_start(out=out, in_=in_)

        return S.add("sp", fn, reads=reads, writes=writes, dma=sem)

    dma_sp(cst[:, :], cst_d[:, :], (), (B.const,), const_sem)
    dma_sp(ident[:, :], ident_d[:, :], (), (B.const,), const_sem)
    memset("dve", ones[:, :], 1.0, (B.const,))

    def load_transpose(src_d, row0, nrows, dst_fn, dst_bufs_fn, extra=()):
        nblk = (nrows + 127) // 128
        for b_ in range(nblk):
            rows = min(128, nrows - b_ * 128)
            xi = state["xio"] % 3
            state["xio"] += 1
            dma_sp(xio[xi][0:rows, :], src_d[row0 + b_ * 128:row0 + b_ * 128 + rows, :], (), (B.xio[xi],),
                   xio_sem[xi])
            for half in range(2):
                ps, psB = ps_next()

                def fn(e, ps=ps, xi=xi, rows=rows, half=half):
                    ins = None
                    for cc in range(4):
                        c = half * 4 + cc
                        ins = e.transpose(ps[:, cc * 128:cc * 128 + rows], xio[xi][0:rows, c * 128:(c + 1) * 128],
                                          ident[0:rows, 0:rows])
                    return ins

                S.add("pe", fn, reads=(B.xio[xi], B.const), writes=(psB,))
                src = ps[:, :].rearrange("p (a b) -> p a b", a=4)[:, :, 0:rows]
                dst = dst_fn(half * 4, b_ * 128, rows)
                act(dst, src, AF.Copy, reads=(psB,) + tuple(extra), writes=dst_bufs_fn(half * 4, b_ * 128, rows))

    load_transpose(mem_d, 0, NMEM, lambda c0, t0, n: memT[:, c0:c0 + 4, t0:t0 + n],
                   lambda c0, t0, n: (B.memT,), extra=(B.A,))
    for l in range(2):
        norm_block(lambda sl: memT[:, :, sl], [B.memT] * 8, slice(0, NMEM), C_MEM + 8 * l,
                   lambda c, sl: hT[:, c, sl], lambda c: (B.h[0],))
        ws, wB = wnext(f"KVK{l}")
        wv = sview(ws, 0, 128, [8, 256])
        for j in range(2):
            ps, psB = ps_next()
            mm_group(ps[:, 0:NMEM], [(wv[:, k, j * 128:(j + 1) * 128], hT[:, k, 0:NMEM]) for k in range(8)],
                     reads=(wB, B.h[0]), psB=psB)
            act(kT[:, l, j, :], ps[:, 0:NMEM], AF.Copy, reads=(psB,), writes=(B.kv,))
        ws, wB = wnext(f"KVV{l}")
        wv = sview(ws, 0, 128, [8, 256])
        for mc in range(2):
            ps, psB = ps_next()
            mm_group(ps[:, 0:256], [(hT[:, k, mc * 128:(mc + 1) * 128], wv[:, k, :]) for k in range(8)],
                     reads=(wB, B.h[0]), psB=psB)
            act(vS[:, l, mc, :], ps[:, 0:256], AF.Copy, reads=(psB,), writes=(B.kv,))

    def xbufs(c0, t0, n):
        tbs = sorted(set([t0 // BS, (t0 + n - 1) // BS]))
        return [B.x[c][tb] for c in range(c0, c0 + 4) for tb in tbs]

    def stage_gcol(kind, l, which):
        if kind == "ffn":
            return (C_FFN1 if which == 0 else C_FFN2) + 8 * l
        return C_MIX + 8 * l

    for t in range(NT):
        if stage_list:
            set_pend(stage_gcol(*stage_list[0]))
        elif final_norm:
            set_pend(C_FIN, "y")
        else:
            set_pend(None)
        load_transpose(x_d, t * OWN, TT, lambda c0, t0, n: xT[:, c0:c0 + 4, t0:t0 + n], xbufs)
        for si, (kind, l, which) in enumerate(stage_list):
            if si + 1 < len(stage_list):
                nxt = (stage_gcol(*stage_list[si + 1]), "h")
            elif final_norm:
                nxt = (C_FIN, "y")
            else:
                nxt = (None, "h")
            if kind == "ffn":
                ffn(l, which, nxt)
            elif l == 0:
                mixer_conv(t == 0, nxt)
            else:
                mixer_pool(t == 0, nxt)
        if final_norm:
            for tb in range(NB):
                need_h(tb)
            srcT, srcB = yT, (lambda c, tb: B.yT[tb])
        else:
            srcT, srcB = xT, (lambda c, tb: B.x[c][tb])
        for ob in range(OWN // 128):
            col0 = HALO + ob * 128
            tbs = sorted(set([col0 // BS, (col0 + 127) // BS]))
            xi = state["xio"] % 3
            state["xio"] += 1
            for half in range(2):
                ps, psB = ps_next()

                def fn(e, ps=ps, half=half, col0=col0, srcT=srcT):
                    ins = None
                    for cc in range(4):
                        c = half * 4 + cc
                        ins = e.transpose(ps[:, cc * 128:(cc + 1) * 128], srcT[:, c, col0:col0 + 128], ident[:, :])
                    return ins

                rd = [srcB(c, tb) for c in range(half * 4, half * 4 + 4) for tb in tbs] + [B.const]
                S.add("pe", fn, reads=rd, writes=(psB,))
                act(xio[xi][:, half * 512:(half + 1) * 512], ps[:, :], AF.Copy, reads=(psB,), writes=(B.xio[xi],))
            dma_sp(y_d[t * OWN + ob * 128:t * OWN + (ob + 1) * 128, :], xio[xi][:, :], (B.xio[xi],), (),
                   xio_sem[xi])
    assert wstate["next"] == len(units), (wstate["next"], len(units))

    for k in list(sems.keys()):
        sems[k] = nc.alloc_semaphore(k)
    for e in ("pe", "act", "dve", "pool"):
        sems[e] = nc.alloc_semaphore("t_" + e)

    def replay(name, eng):
        for waits, fn, inc in S.ops[name]:
            for k, v in waits:
                eng.wait_ge(sems[k], v)
            ins = fn(eng)
            ins.then_inc(sems[inc[0]], inc[1])

    with nc.Block() as block:
        @block.tensor
        def _(e):
            replay("pe", e)

        @block.scalar
        def _(e):
            replay("act", e)

        @block.vector
        def _(e):
            replay("dve", e)

        @block.gpsimd
        def _(e):
            replay("pool", e)

        @block.sync
        def _(e):
            replay("sp", e)
            for ds in out_sems:
                if ds.count:
                    e.wait_ge(sems[ds.key], ds.count)

    stats = {k: len(v) for k, v in S.ops.items()}
    return nc, stats


_CACHE = {}


def _get_program(NT, nstages=6, final_norm=True):
    key = (NT, nstages, final_norm)
    if key not in _CACHE:
        _CACHE[key] = build_program(NT, nstages, final_norm)[0]
    return _CACHE[key]


def _const_pack(inp, mask_val, corr_on):
    c = np.zeros((128, NCST), np.float32)

    def put(col, vec2d):
        for i in range(vec2d.shape[0]):
            c[:, col + 8 * i:col + 8 * i + 8] = vec2d[i].reshape(8, 128).T

    put(C_FFN1, inp["ffn1_norm"])
    put(C_MIX, inp["mix_norm"])
    put(C_MEM, inp["mem_norm"])
    put(C_FFN2, inp["ffn2_norm"])
    put(C_FIN, inp["final_norm"][None, :])
    cw = inp["conv_w"][0]
    for ch in range(6):
        for tap in range(3):
            c[:, C_CONV + 3 * ch + tap] = cw[tap, ch * 128:(ch + 1) * 128]
    ps = inp["pool_scale"][0]
    for i in range(8):
        c[0:96, C_PSC + i] = ps[i * 96:(i + 1) * 96]
    c[:, C_MASK] = mask_val
    for g in range(4):
        w = 2 << g
        for t in range(16):
            c[:, C_CORR + 16 * g + t] = (float(w) / min(t + 1, w)) if corr_on else 1.0
    return c


def _prep_inputs(inputs):
    inp = {k: np.ascontiguousarray(np.asarray(v), dtype=np.float32) for k, v in inputs.items()}
    return inp


_WKEYS = ["ffn1_w_gu", "ffn2_w_gu", "ffn1_w_down", "ffn2_w_down", "w_kv", "w_out",
          "conv_w_in", "pool_w_in", "pool_w_group"]


def _core_x(inp, core):
    b, half = core // 2, core % 2
    x = inp["x"]
    xs = np.zeros((4096 + HALO, D), np.float32)
    if half == 0:
        xs[HALO:] = x[b, 0:4096]
    else:
        xs[:] = x[b, 4096 - HALO:8192]
    return xs


def kernel(**inputs):
    inp = _prep_inputs(inputs)
    NT = 4
    nc = _get_program(NT)
    ident = np.eye(128, dtype=np.float32)
    in_maps = []
    for core in range(NCORE):
        b, half = core // 2, core % 2
        m = {"x": _core_x(inp, core), "mem": inp["mem"][b], "ident": ident,
             "cst": _const_pack(inp, 0.0 if half == 0 else 1.0, half == 0)}
        for k in _WKEYS:
            m[k] = inp[k]
        in_maps.append(m)
    res = run_bass_kernel_spmd(nc, in_maps, core_ids=list(range(NCORE)))
    out = np.empty((4, 8192, D), np.float32)
    for core in range(NCORE):
        b, half = core // 2, core % 2
        out[b, half * 4096:(half + 1) * 4096] = res.results[core]["y"]
    return out
```

```python
import numpy as np
import concourse.bass as bass
import concourse.mybir as mybir
from concourse.bass_utils import run_bass_kernel_spmd

F32 = mybir.dt.float32
BF16 = mybir.dt.bfloat16
AF = mybir.ActivationFunctionType
ALU = mybir.AluOpType

D = 1024
DFF = 2816
NMEM = 256
MIXW = 768
EPS = 1e-6
OWN = 1024
HALO = 32
TT = OWN + HALO
NB = 3
BS = TT // NB
NCORE = 8
NSLOT = 7
SLOT_ELEMS = 2816
PADL = 16
PW = PADL + TT

C_FFN1 = 0
C_MIX = 16
C_MEM = 32
C_FFN2 = 48
C_FIN = 64
C_CONV = 72
C_PSC = 90
C_MASK = 98
C_CORR = 99
NCST = 164

ENGS = ("pe", "act", "dve", "pool", "sp")


class Buf:
    __slots__ = ("w", "r", "name")

    def __init__(self, name=""):
        self.w = {}
        self.r = {}
        self.name = name


class DmaSem:
    def __init__(self, key):
        self.key = key
        self.count = 0


class Sched:
    def __init__(self):
        self.ops = {e: [] for e in ENGS}
        self.cnt = {e: 0 for e in ENGS}
        self.waited = {e: {} for e in ENGS}

    def add(self, eng, fn, reads=(), writes=(), dma=None):
        deps = {}

        def dep(key, val, raw):
            if key == eng:
                if eng == "pe" or not raw:
                    return
            if val > deps.get(key, 0):
                deps[key] = val

        for b in reads:
            for k, v in b.w.items():
                dep(k, v, True)
        for b in writes:
            for k, v in b.w.items():
                dep(k, v, False)
            for k, v in b.r.items():
                dep(k, v, False)
        wd = self.waited[eng]
        waits = []
        for k, v in deps.items():
            if wd.get(k, 0) < v:
                waits.append((k, v))
                wd[k] = v
        if dma is not None:
            dma.count += 16
            tok = (dma.key, dma.count)
            inc = (dma.key, 16)
        else:
            self.cnt[eng] += 1
            tok = (eng, self.cnt[eng])
            inc = (eng, 1)
        self.ops[eng].append((waits, fn, inc))
        for b in reads:
            if b.r.get(tok[0], 0) < tok[1]:
                b.r[tok[0]] = tok[1]
        for b in writes:
            if b.w.get(tok[0], 0) < tok[1]:
                b.w[tok[0]] = tok[1]
        return tok


def build_program(NT, nstages=6, final_norm=True):
    nc = bass.Bass("TRN2", target_bir_lowering=False)
    S = Sched()
    NROWS = NT * OWN + HALO

    def din(name, shape):
        return nc.dram_tensor(name, list(shape), F32, kind="ExternalInput").ap()

    x_d = din("x", (NROWS, D))
    mem_d = din("mem", (NMEM, D))
    ident_d = din("ident", (128, 128))
    cst_d = din("cst", (128, NCST))
    w_gu_d = [din("ffn1_w_gu", (2, D, 2 * DFF)), din("ffn2_w_gu", (2, D, 2 * DFF))]
    w_dn_d = [din("ffn1_w_down", (2, DFF, D)), din("ffn2_w_down", (2, DFF, D))]
    w_kv_d = din("w_kv", (2, D, 512))
    w_out_d = din("w_out", (2, D, D))
    conv_in_d = din("conv_w_in", (1, D, 2560))
    pool_in_d = din("pool_w_in", (1, D, D))
    pool_g_d = din("pool_w_group", (1, 4, 192, 192))
    y_d = nc.dram_tensor("y", [NT * OWN, D], F32, kind="ExternalOutput").ap()

    sb = nc.alloc_sbuf_tensor
    xT = sb("xT", [128, 8, TT], F32)
    hT = sb("hT", [128, 8, TT], BF16)
    A = sb("A", [128, 22 * TT], BF16)
    mixb = sb("mixb", [128, 8, TT], BF16)
    attb = sb("attb", [128, 2, TT], BF16)
    qb = sb("qb", [128, 2, TT], BF16)
    slots = [sb(f"ws{i}", [128, SLOT_ELEMS], BF16) for i in range(NSLOT)]
    ebuf = [sb(f"eb{i}", [128, 4, BS], BF16) for i in range(2)]
    rcb = [sb(f"rc{i}", [128, BS], F32) for i in range(2)]
    sqb = sb("sqb", [128, 8, BS], BF16)
    msb = [sb(f"ms{i}", [128, BS], F32) for i in range(2)]
    rsb = [sb(f"rs{i}", [128, BS], F32) for i in range(2)]
    tmpb = [sb(f"tmp{i}", [128, BS], F32) for i in range(3)]
    xio = [sb(f"xio{i}", [128, D], F32) for i in range(3)]
    kT = sb("kT", [128, 2, 2, NMEM], BF16)
    vS = sb("vS", [128, 2, 2, 256], BF16)
    ident = sb("ident_s", [128, 128], F32)
    ones = sb("ones_s", [128, 128], BF16)
    cst = sb("cst_s", [128, NCST], F32)
    psum = [nc.alloc_psum_tensor(f"ps{i}", [128, 512], F32) for i in range(8)]

    def A_view(off_bf16, shape, dtype):
        n = int(np.prod(shape[1:]))
        if dtype == F32:
            ap = A[:, off_bf16:off_bf16 + 2 * n].bitcast(F32)
        else:
            ap = A[:, off_bf16:off_bf16 + n]
        if len(shape) == 3:
            ap = ap.rearrange("p (a b) -> p a b", a=shape[1])
        return ap

    aT = A_view(0, [128, 22, TT], BF16)
    yT = A_view(0, [128, 8, TT], F32)
    memT = A_view(0, [128, 8, NMEM], F32)
    ubuf = [A_view(i * 2 * (TT + 2), [128, TT + 2], F32) for i in range(2)]
    o1 = 2 * 2 * (TT + 2)
    ybuf = [A_view(o1 + i * 2 * TT, [128, TT], F32) for i in range(2)]
    pbuf = [A_view(i * 2 * PW, [128, PW], F32) for i in range(2)]
    sbufs = [A_view((2 + i) * 2 * PW, [128, PW], F32) for i in range(2)]
    o2 = 4 * 2 * PW
    dbuf = A_view(o2, [128, 8, TT], BF16)

    class NS:
        pass

    B = NS()
    B.x = [[Buf() for _ in range(NB)] for _ in range(8)]
    B.h = [Buf() for _ in range(NB)]
    B.a = [[Buf() for _ in range(NB)] for _ in range(22)]
    B.A = Buf()
    B.mix = [[Buf() for _ in range(NB)] for _ in range(8)]
    B.att = [[Buf() for _ in range(NB)] for _ in range(2)]
    B.q = [[Buf() for _ in range(NB)] for _ in range(2)]
    B.slot = [Buf() for _ in range(NSLOT)]
    B.e = [Buf() for _ in range(2)]
    B.rc = [Buf() for _ in range(2)]
    B.sq = Buf()
    B.ms = [Buf() for _ in range(2)]
    B.rs = [Buf() for _ in range(2)]
    B.tmp = [Buf() for _ in range(3)]
    B.xio = [Buf() for _ in range(3)]
    B.kv = Buf()
    B.const = Buf()
    B.ps = [Buf() for _ in range(8)]
    B.u = [Buf() for _ in range(2)]
    B.y = [Buf() for _ in range(2)]
    B.p = [Buf() for _ in range(2)]
    B.s = [Buf() for _ in range(2)]
    B.d = [Buf() for _ in range(8)]
    B.yT = [Buf() for _ in range(NB)]
    B.memT = Buf()

    sems = {}

    def new_dma_sem(name):
        sems[name] = None
        return DmaSem(name)

    slot_sem = [new_dma_sem(f"d_slot{i}") for i in range(NSLOT)]
    xio_sem = [new_dma_sem(f"d_xio{i}") for i in range(3)]
    const_sem = new_dma_sem("d_const")
    out_sems = xio_sem

    state = {"ps": 0, "tmp": 0, "nrm": 0, "xio": 0, "e": 0}

    def ps_next():
        i = state["ps"]
        state["ps"] = (i + 1) % 8
        return psum[i], B.ps[i]

    def blk(tb):
        return slice(tb * BS, (tb + 1) * BS)

    units = []
    wstate = {"emitted": 0, "next": 0}

    def sview(slot, off, P, shape):
        n = int(np.prod(shape))
        ap = slot[0:P, off:off + n]
        if len(shape) == 2:
            ap = ap.rearrange("p (a b) -> p a b", a=shape[0])
        elif len(shape) == 3:
            ap = ap.rearrange("p (a b c) -> p a b c", a=shape[0], b=shape[1])
        return ap

    def kview(dram2d, r0, nk, c0, ncol, P=128):
        return dram2d[r0:r0 + nk * P, c0:c0 + ncol].rearrange("(k p) n -> p k n", p=P)

    def emit_unit_dma(idx):
        name, parts = units[idx]
        s = idx % NSLOT
        for (dst_fn, src) in parts:
            dst = dst_fn(slots[s])

            def fn(e, dst=dst, src=src):
                return e.dma_start(out=dst, in_=src)

            S.add("pool", fn, reads=(), writes=(B.slot[s],), dma=slot_sem[s])

    def wnext(name):
        i = wstate["next"]
        assert units[i][0] == name, (units[i][0], name)
        lim = min(len(units), i + NSLOT - 1)
        while wstate["emitted"] < lim:
            emit_unit_dma(wstate["emitted"])
            wstate["emitted"] += 1
        wstate["next"] = i + 1
        s = i % NSLOT
        return slots[s], B.slot[s]

    def U(name, parts):
        units.append((name, parts))

    def list_ffn_units(l, which):
        gu = w_gu_d[which][l]
        dn = w_dn_d[which][l]
        for fp in range(11):
            U(f"G{l}{which}_{fp}", [(lambda s: sview(s, 0, 128, [8, 256]), kview(gu, 0, 8, 256 * fp, 256))])
            U(f"U{l}{which}_{fp}", [(lambda s: sview(s, 0, 128, [8, 256]), kview(gu, 0, 8, DFF + 256 * fp, 256))])
        for op in range(4):
            for kh in range(2):
                U(f"D{l}{which}_{op}_{kh}",
                  [(lambda s: sview(s, 0, 128, [11, 256]), kview(dn, kh * 1408, 11, 256 * op, 256))])

    def list_mix_units(l):
        if l == 0:
            w = conv_in_d[0]
            for c in range(6):
                U(f"GCV_{c}", [
                    (lambda s: sview(s, 0, 128, [8, 256])[:, :, 0:128], kview(w, 0, 8, MIXW + 128 * c, 128)),
                    (lambda s: sview(s, 0, 128, [8, 256])[:, :, 128:256], kview(w, 0, 8, 2 * MIXW + 128 * c, 128)),
                ])
                if c % 2 == 1:
                    U(f"GB_{c // 2}", [(lambda s: sview(s, 0, 128, [8, 256]), kview(w, 0, 8, 256 * (c // 2), 256))])
            U("Q0", [(lambda s: sview(s, 0, 128, [8, 256]), kview(w, 0, 8, 3 * MIXW, 256))])
            wo = w_out_d[0]
            for op in range(4):
                U(f"WO0_{op}", [(lambda s: sview(s, 0, 128, [8, 256]), kview(wo, 0, 8, 256 * op, 256))])
        else:
            w = pool_in_d[0]
            for g in range(4):
                U(f"PW_{g}", [(lambda s: sview(s, 0, 128, [8, 192]), kview(w, 0, 8, 192 * g, 192))])
            U("Q1", [(lambda s: sview(s, 0, 128, [8, 256]), kview(w, 0, 8, MIXW, 256))])
            pg = pool_g_d[0].rearrange("g (kc p) n -> p g kc n", p=96)
            U("PG", [(lambda s: sview(s, 0, 96, [4, 2, 192]), pg)])
            wo = w_out_d[1]
            for op in range(4):
                U(f"WO1_{op}", [
                    (lambda s: sview(s, 0, 96, [8, 256]), kview(wo, 0, 8, 256 * op, 256, P=96)),
                    (lambda s: sview(s, 2048, 128, [2, 256]), kview(wo, MIXW, 2, 256 * op, 256)),
                ])

    for l in range(2):
        U(f"KVK{l}", [(lambda s: sview(s, 0, 128, [8, 256]), kview(w_kv_d[l], 0, 8, 0, 256))])
        U(f"KVV{l}", [(lambda s: sview(s, 0, 128, [8, 256]), kview(w_kv_d[l], 0, 8, 256, 256))])
    stage_list = []
    for l in range(2):
        stage_list += [("ffn", l, 0), ("mix", l, 0), ("ffn", l, 1)]
    stage_list = stage_list[:nstages]
    for t in range(NT):
        for (kind, l, which) in stage_list:
            if kind == "ffn":
                list_ffn_units(l, which)
            else:
                list_mix_units(l)

    def mm_group(out_ap, pairs, reads, psB):
        n = len(pairs)

        def fn(e):
            ins = None
            for i, (l, r) in enumerate(pairs):
                ins = e.matmul(out_ap, l, r, start=(i == 0), stop=(i == n - 1))
            return ins

        return S.add("pe", fn, reads=reads, writes=(psB,))

    def act(out, in_, func, reads, writes, scale=None, bias=None):
        kw = {}
        if scale is not None:
            kw["scale"] = scale
        if bias is not None:
            kw["bias"] = bias

        def fn(e):
            return e.activation(out=out, in_=in_, func=func, **kw)

        return S.add("act", fn, reads=reads, writes=writes)

    def dve_tt(out, in0, in1, op, reads, writes, eng="dve"):
        def fn(e):
            return e.tensor_tensor(out=out, in0=in0, in1=in1, op=op)

        return S.add(eng, fn, reads=reads, writes=writes)

    def dve_ts(out, in0, s1, s2, op0, op1, reads, writes, eng="dve"):
        def fn(e):
            if op1 is None:
                return e.tensor_scalar(out=out, in0=in0, scalar1=s1, scalar2=None, op0=op0)
            return e.tensor_scalar(out=out, in0=in0, scalar1=s1, scalar2=s2, op0=op0, op1=op1)

        return S.add(eng, fn, reads=reads, writes=writes)

    def dve_stt(out, in0, scalar, in1, op0, op1, reads, writes):
        def fn(e):
            return e.scalar_tensor_tensor(out=out, in0=in0, scalar=scalar, in1=in1, op0=op0, op1=op1)

        return S.add("dve", fn, reads=reads, writes=writes)

    def memset(eng, ap, val, writes, reads=()):
        def fn(e):
            return e.memset(ap, val)

        return S.add(eng, fn, reads=reads, writes=writes)

    def norm_block(src_fn, sbufs_r, sl, gcol, dst_fn, dst_bufs_fn, extra=()):
        n = sl.stop - sl.start
        i2 = state["nrm"] % 2
        state["nrm"] += 1
        src = src_fn(sl)
        for hf in range(2):
            act(sqb[:, 4 * hf:4 * hf + 4, 0:n], src[:, 4 * hf:4 * hf + 4, :], AF.Square,
                reads=sbufs_r[4 * hf:4 * hf + 4], writes=(B.sq,))
        ps, psB = ps_next()
        mm_group(ps[:, 0:n], [(ones[:, :], sqb[:, c, 0:n]) for c in range(8)],
                 reads=(B.sq, B.const), psB=psB)
        act(msb[i2][:, 0:n], ps[:, 0:n], AF.Ln, reads=(psB,), writes=(B.ms[i2],), scale=1.0 / D, bias=EPS)
        act(rsb[i2][:, 0:n], msb[i2][:, 0:n], AF.Exp, reads=(B.ms[i2],), writes=(B.rs[i2],), scale=-0.5)
        for c in range(8):
            dve_stt(dst_fn(c, sl), src[:, c, :], cst[:, gcol + c:gcol + c + 1], rsb[i2][:, 0:n],
                    ALU.mult, ALU.mult,
                    reads=(sbufs_r[c], B.rs[i2], B.const) + tuple(extra), writes=dst_bufs_fn(c))

    tile_blocks = [blk(tb) for tb in range(NB)]

    def fence_A():
        for e_ in ("pe", "act", "dve", "pool"):
            if S.cnt[e_]:
                B.A.w[e_] = S.cnt[e_]

    pend = {"gcol": None, "dst": "h", "done": set()}

    def set_pend(gcol, dst="h"):
        pend["gcol"] = gcol
        pend["dst"] = dst
        pend["done"] = set()

    def need_h(tb):
        if pend["gcol"] is None or tb in pend["done"]:
            return
        pend["done"].add(tb)
        xb = [B.x[c][tb] for c in range(8)]
        if pend["dst"] == "h":
            norm_block(lambda sl: xT[:, :, sl], xb, blk(tb), pend["gcol"],
                       lambda c, sl: hT[:, c, sl], lambda c: (B.h[tb],))
        else:
            fence_A()
            norm_block(lambda sl: xT[:, :, sl], xb, blk(tb), pend["gcol"],
                       lambda c, sl: yT[:, c, sl], lambda c: (B.yT[tb],), extra=(B.A,))

    def ffn(l, which, nxt):
        fence_A()
        for fp in range(11):
            gs, gB = wnext(f"G{l}{which}_{fp}")
            us, uB = wnext(f"U{l}{which}_{fp}")
            gv = sview(gs, 0, 128, [8, 256])
            uv = sview(us, 0, 128, [8, 256])
            for tb in range(NB):
                for f in (2 * fp, 2 * fp + 1):
                    cs = slice((f % 2) * 128, (f % 2) * 128 + 128)
                    sl = blk(tb)
                    need_h(tb)
                    pg, pgB = ps_next()
                    pu, puB = ps_next()
                    mm_group(pg[:, 0:BS], [(gv[:, k, cs], hT[:, k, sl]) for k in range(8)],
                             reads=(gB, B.h[tb]), psB=pgB)
                    mm_group(pu[:, 0:BS], [(uv[:, k, cs], hT[:, k, sl]) for k in range(8)],
                             reads=(uB, B.h[tb]), psB=puB)
                    ti = state["tmp"] % 3
                    state["tmp"] += 1
                    act(tmpb[ti][:, :], pg[:, 0:BS], AF.Silu, reads=(pgB,), writes=(B.tmp[ti],))
                    dve_tt(aT[:, f, sl], tmpb[ti][:, :], pu[:, 0:BS], ALU.mult,
                           reads=(B.tmp[ti], puB, B.A), writes=(B.a[f][tb],))
                    if fp == 0 and tb == 0 and f == 0:
                        need_h(NB - 1)
        for op in range(4):
            d0, d0B = wnext(f"D{l}{which}_{op}_0")
            d1, d1B = wnext(f"D{l}{which}_{op}_1")
            dv = [sview(d0, 0, 128, [11, 256]), sview(d1, 0, 128, [11, 256])]
            order = ([(o, tb) for o in (2 * op, 2 * op + 1) for tb in range(NB)] if op < 3 else
                     [(o, tb) for tb in range(NB) for o in (2 * op, 2 * op + 1)])
            for (o, tb) in order:
                cs = slice((o % 2) * 128, (o % 2) * 128 + 128)
                sl = blk(tb)
                ps, psB = ps_next()
                mm_group(ps[:, 0:BS], [(dv[k // 11][:, k % 11, cs], aT[:, k, sl]) for k in range(22)],
                         reads=[d0B, d1B] + [B.a[k][tb] for k in range(22)], psB=psB)
                dve_stt(xT[:, o, sl], ps[:, 0:BS], 0.5, xT[:, o, sl], ALU.mult, ALU.add,
                        reads=(psB, B.x[o][tb]), writes=(B.x[o][tb],))
                if o == 7:
                    if tb == 0:
                        set_pend(*nxt)
                    else:
                        need_h(tb - 1)
        need_h(NB - 2)

    def q_proj(name, ws, wB):
        wv = sview(ws, 0, 128, [8, 256])
        for j in range(2):
            cs = slice(j * 128, j * 128 + 128)
            for tb in range(NB):
                sl = blk(tb)
                ps, psB = ps_next()
                mm_group(ps[:, 0:BS], [(wv[:, k, cs], hT[:, k, sl]) for k in range(8)],
                         reads=(wB, B.h[tb]), psB=psB)
                act(qb[:, j, sl], ps[:, 0:BS], AF.Copy, reads=(psB,), writes=(B.q[j][tb],))

    def attention(l):
        steps = [(tb, j) for tb in range(NB) for j in range(2)]
        eis = {}

        def scores(tb, j):
            sl = blk(tb)
            ei = state["e"] % 2
            state["e"] += 1
            eis[(tb, j)] = ei
            for hh in range(2):
                r = slice(64 * hh, 64 * hh + 64)
                for mc in range(2):
                    ps, psB = ps_next()
                    mm_group(ps[:, 0:BS], [(kT[r, l, j, mc * 128:(mc + 1) * 128], qb[r, j, sl])],
                             reads=(B.kv, B.q[j][tb]), psB=psB)
                    act(ebuf[ei][:, hh * 2 + mc, :], ps[:, 0:BS], AF.Exp, reads=(psB,),
                        writes=(B.e[ei],), scale=0.125)

        def pv(tb, j):
            sl = blk(tb)
            ei = eis[(tb, j)]
            ppv, ppvB = ps_next()
            psm, psmB = ps_next()
            for hh in range(2):
                r = slice(64 * hh, 64 * hh + 64)
                h = 2 * j + hh
                mm_group(ppv[r, 0:BS],
                         [(vS[:, l, mc, h * 64:(h + 1) * 64], ebuf[ei][:, hh * 2 + mc, :]) for mc in range(2)],
                         reads=(B.kv, B.e[ei]), psB=ppvB)
                mm_group(psm[r, 0:BS],
                         [(ones[:, 0:64], ebuf[ei][:, hh * 2 + mc, :]) for mc in range(2)],
                         reads=(B.const, B.e[ei]), psB=psmB)
            act(rcb[ei][:, :], psm[:, 0:BS], AF.Ln, reads=(psmB,), writes=(B.rc[ei],))
            act(rcb[ei][:, :], rcb[ei][:, :], AF.Exp, reads=(B.rc[ei],), writes=(B.rc[ei],), scale=-1.0)
            dve_tt(attb[:, j, sl], ppv[:, 0:BS], rcb[ei][:, :], ALU.mult,
                   reads=(ppvB, B.rc[ei]), writes=(B.att[j][tb],))

        scores(*steps[0])
        for i, st in enumerate(steps):
            if i + 1 < len(steps):
                scores(*steps[i + 1])
            pv(*st)

    def w_out_stage(l, nxt):
        for op in range(4):
            ws, wB = wnext(f"WO{l}_{op}")
            if l == 0:
                wv = sview(ws, 0, 128, [8, 256])
            else:
                wa = sview(ws, 0, 96, [8, 256])
                wb_ = sview(ws, 2048, 128, [2, 256])
            order = ([(o, tb) for o in (2 * op, 2 * op + 1) for tb in range(NB)] if op < 3 else
                     [(o, tb) for tb in range(NB) for o in (2 * op, 2 * op + 1)])
            for (o, tb) in order:
                cs = slice((o % 2) * 128, (o % 2) * 128 + 128)
                if True:
                    sl = blk(tb)
                    ps, psB = ps_next()
                    if l == 0:
                        pairs = [(wv[:, k, cs], mixb[:, k, sl]) for k in range(6)]
                        pairs += [(wv[:, 6 + j, cs], attb[:, j, sl]) for j in range(2)]
                        rd = [wB] + [B.mix[k][tb] for k in range(6)] + [B.att[j][tb] for j in range(2)]
                    else:
                        pairs = [(wa[:, k, cs], mixb[0:96, k, sl]) for k in range(8)]
                        pairs += [(wb_[:, j, cs], attb[:, j, sl]) for j in range(2)]
                        rd = [wB] + [B.mix[k][tb] for k in range(8)] + [B.att[j][tb] for j in range(2)]
                    mm_group(ps[:, 0:BS], pairs, reads=rd, psB=psB)
                    dve_tt(xT[:, o, sl], ps[:, 0:BS], xT[:, o, sl], ALU.add,
                           reads=(psB, B.x[o][tb]), writes=(B.x[o][tb],))
                    if o == 7:
                        if tb == 0:
                            set_pend(*nxt)
                        else:
                            need_h(tb - 1)
        need_h(NB - 2)

    def mixer_conv(first_tile, nxt):
        fence_A()
        for i in range(2):
            memset("dve", ubuf[i][:, 0:2], 0.0, (B.u[i],), reads=(B.A,))
        for c in range(6):
            ws, wB = wnext(f"GCV_{c}")
            wv = sview(ws, 0, 128, [8, 256])
            ui = c % 2
            for tb in range(NB):
                sl = blk(tb)
                need_h(tb)
                p1, p1B = ps_next()
                p2, p2B = ps_next()
                mm_group(p1[:, 0:BS], [(wv[:, k, 0:128], hT[:, k, sl]) for k in range(8)],
                         reads=(wB, B.h[tb]), psB=p1B)
                mm_group(p2[:, 0:BS], [(wv[:, k, 128:256], hT[:, k, sl]) for k in range(8)],
                         reads=(wB, B.h[tb]), psB=p2B)
                ti = state["tmp"] % 3
                state["tmp"] += 1
                act(tmpb[ti][:, :], p2[:, 0:BS], AF.Copy, reads=(p2B,), writes=(B.tmp[ti],))
                dve_tt(ubuf[ui][:, 2 + tb * BS:2 + (tb + 1) * BS], p1[:, 0:BS], tmpb[ti][:, :], ALU.mult,
                       reads=(p1B, B.tmp[ti], B.A), writes=(B.u[ui],))
                if c == 0 and tb == 0:
                    need_h(NB - 1)
            if first_tile:
                dve_ts(ubuf[ui][:, 2:2 + HALO], ubuf[ui][:, 2:2 + HALO], cst[:, C_MASK:C_MASK + 1], None,
                       ALU.mult, None, reads=(B.u[ui], B.const), writes=(B.u[ui],))
            cw = lambda tap: cst[:, C_CONV + 3 * c + tap:C_CONV + 3 * c + tap + 1]
            dve_ts(ybuf[ui][:, :], ubuf[ui][:, 2:2 + TT], cw(2), None, ALU.mult, None,
                   reads=(B.u[ui], B.const, B.A), writes=(B.y[ui],))
            dve_stt(ybuf[ui][:, :], ubuf[ui][:, 1:1 + TT], cw(1), ybuf[ui][:, :], ALU.mult, ALU.add,
                    reads=(B.u[ui], B.y[ui], B.const), writes=(B.y[ui],))
            dve_stt(ybuf[ui][:, :], ubuf[ui][:, 0:TT], cw(0), ybuf[ui][:, :], ALU.mult, ALU.add,
                    reads=(B.u[ui], B.y[ui], B.const), writes=(B.y[ui],))
            if c % 2 == 1:
                ws2, wB2 = wnext(f"GB_{c // 2}")
                wv2 = sview(ws2, 0, 128, [8, 256])
                for cc in (c - 1, c):
                    cs = slice((cc % 2) * 128, (cc % 2) * 128 + 128)
                    for tb in range(NB):
                        sl = blk(tb)
                        ps, psB = ps_next()
                        mm_group(ps[:, 0:BS], [(wv2[:, k, cs], hT[:, k, sl]) for k in range(8)],
                                 reads=(wB2, B.h[tb]), psB=psB)
                        dve_tt(mixb[:, cc, sl], ps[:, 0:BS], ybuf[cc % 2][:, sl], ALU.mult,
                               reads=(psB, B.y[cc % 2]), writes=(B.mix[cc][tb],))
        ws, wB = wnext("Q0")
        q_proj("Q0", ws, wB)
        attention(0)
        w_out_stage(0, nxt)

    def mixer_pool(first_tile, nxt):
        fence_A()
        for i in range(2):
            memset("dve", pbuf[i][:, 0:PADL], 0.0, (B.p[i],), reads=(B.A,))
        for g in range(4):
            ws, wB = wnext(f"PW_{g}")
            wv = sview(ws, 0, 128, [8, 192])
            win = 2 << g
            for i in (2 * g, 2 * g + 1):
                pi = i % 2
                cs = slice((i % 2) * 96, (i % 2) * 96 + 96)
                for tb in range(NB):
                    sl = blk(tb)
                    need_h(tb)
                    ps, psB = ps_next()
                    mm_group(ps[0:96, 0:BS], [(wv[:, k, cs], hT[:, k, sl]) for k in range(8)],
                             reads=(wB, B.h[tb]), psB=psB)
                    act(pbuf[pi][0:96, PADL + tb * BS:PADL + (tb + 1) * BS], ps[0:96, 0:BS], AF.Copy,
                        reads=(psB, B.A), writes=(B.p[pi],))
                P = pbuf[pi]
                if first_tile:
                    dve_ts(P[0:96, PADL:PADL + HALO], P[0:96, PADL:PADL + HALO], cst[0:96, C_MASK:C_MASK + 1],
                           None, ALU.mult, None, reads=(B.p[pi], B.const), writes=(B.p[pi],))
                cur, curB = P, B.p[pi]
                sh = 1
                si = 0
                while sh < win:
                    lo = 2 * sh - 1
                    dst, dstB = sbufs[si], B.s[si]
                    dve_tt(dst[0:96, lo:PW], cur[0:96, lo:PW], cur[0:96, lo - sh:PW - sh], ALU.add,
                           reads=(curB, B.A), writes=(dstB,))
                    cur, curB = dst, dstB
                    si ^= 1
                    sh *= 2
                if first_tile:
                    c0 = PADL + HALO
                    dve_tt(cur[0:96, c0:c0 + 16], cur[0:96, c0:c0 + 16],
                           cst[0:96, C_CORR + 16 * g:C_CORR + 16 * g + 16], ALU.mult,
                           reads=(curB, B.const), writes=(curB,))
                dve_stt(dbuf[0:96, i, :], cur[0:96, PADL:PW], 1.0 / win, P[0:96, PADL:PW], ALU.mult, ALU.subtract,
                        reads=(curB, B.p[pi], B.A), writes=(B.d[i],))
        ws, wB = wnext("Q1")
        q_proj("Q1", ws, wB)
        attention(1)
        ws, wB = wnext("PG")
        pgv = sview(ws, 0, 96, [4, 2, 192])
        for g in range(4):
            for oc in range(2):
                i = 2 * g + oc
                for tb in range(NB):
                    sl = blk(tb)
                    ps, psB = ps_next()
                    mm_group(ps[0:96, 0:BS],
                             [(pgv[:, g, kc, oc * 96:(oc + 1) * 96], dbuf[0:96, 2 * g + kc, sl]) for kc in range(2)],
                             reads=(wB, B.d[2 * g], B.d[2 * g + 1]), psB=psB)
                    act(mixb[0:96, i, sl], ps[0:96, 0:BS], AF.Identity, reads=(psB, B.const),
                        writes=(B.mix[i][tb],), scale=cst[0:96, C_PSC + i:C_PSC + i + 1])
        w_out_stage(1, nxt)

    def dma_sp(out, in_, reads, writes, sem):
        def fn(e):
            return e.dma_start(out=out, in_=in_)

        return S.add("sp", fn, reads=reads, writes=writes, dma=sem)

    dma_sp(cst[:, :], cst_d[:, :], (), (B.const,), const_sem)
    dma_sp(ident[:, :], ident_d[:, :], (), (B.const,), const_sem)
    memset("dve", ones[:, :], 1.0, (B.const,))

    def load_transpose(src_d, row0, nrows, dst_fn, dst_bufs_fn, extra=()):
        nblk = (nrows + 127) // 128
        for b_ in range(nblk):
            rows = min(128, nrows - b_ * 128)
            xi = state["xio"] % 3
            state["xio"] += 1
            dma_sp(xio[xi][0:rows, :], src_d[row0 + b_ * 128:row0 + b_ * 128 + rows, :], (), (B.xio[xi],),
                   xio_sem[xi])
            for half in range(2):
                ps, psB = ps_next()

                def fn(e, ps=ps, xi=xi, rows=rows, half=half):
                    ins = None
                    for cc in range(4):
                        c = half * 4 + cc
                        ins = e.transpose(ps[:, cc * 128:cc * 128 + rows], xio[xi][0:rows, c * 128:(c + 1) * 128],
                                          ident[0:rows, 0:rows])
                    return ins

                S.add("pe", fn, reads=(B.xio[xi], B.const), writes=(psB,))
                src = ps[:, :].rearrange("p (a b) -> p a b", a=4)[:, :, 0:rows]
                dst = dst_fn(half * 4, b_ * 128, rows)
                act(dst, src, AF.Copy, reads=(psB,) + tuple(extra), writes=dst_bufs_fn(half * 4, b_ * 128, rows))

    load_transpose(mem_d, 0, NMEM, lambda c0, t0, n: memT[:, c0:c0 + 4, t0:t0 + n],
                   lambda c0, t0, n: (B.memT,), extra=(B.A,))
    for l in range(2):
        norm_block(lambda sl: memT[:, :, sl], [B.memT] * 8, slice(0, NMEM), C_MEM + 8 * l,
                   lambda c, sl: hT[:, c, sl], lambda c: (B.h[0],))
        ws, wB = wnext(f"KVK{l}")
        wv = sview(ws, 0, 128, [8, 256])
        for j in range(2):
            ps, psB = ps_next()
            mm_group(ps[:, 0:NMEM], [(wv[:, k, j * 128:(j + 1) * 128], hT[:, k, 0:NMEM]) for k in range(8)],
                     reads=(wB, B.h[0]), psB=psB)
            act(kT[:, l, j, :], ps[:, 0:NMEM], AF.Copy, reads=(psB,), writes=(B.kv,))
        ws, wB = wnext(f"KVV{l}")
        wv = sview(ws, 0, 128, [8, 256])
        for mc in range(2):
            ps, psB = ps_next()
            mm_group(ps[:, 0:256], [(hT[:, k, mc * 128:(mc + 1) * 128], wv[:, k, :]) for k in range(8)],
                     reads=(wB, B.h[0]), psB=psB)
            act(vS[:, l, mc, :], ps[:, 0:256], AF.Copy, reads=(psB,), writes=(B.kv,))

    def xbufs(c0, t0, n):
        tbs = sorted(set([t0 // BS, (t0 + n - 1) // BS]))
        return [B.x[c][tb] for c in range(c0, c0 + 4) for tb in tbs]

    def stage_gcol(kind, l, which):
        if kind == "ffn":
            return (C_FFN1 if which == 0 else C_FFN2) + 8 * l
        return C_MIX + 8 * l

    for t in range(NT):
        if stage_list:
            set_pend(stage_gcol(*stage_list[0]))
        elif final_norm:
            set_pend(C_FIN, "y")
        else:
            set_pend(None)
        load_transpose(x_d, t * OWN, TT, lambda c0, t0, n: xT[:, c0:c0 + 4, t0:t0 + n], xbufs)
        for si, (kind, l, which) in enumerate(stage_list):
            if si + 1 < len(stage_list):
                nxt = (stage_gcol(*stage_list[si + 1]), "h")
            elif final_norm:
                nxt = (C_FIN, "y")
            else:
                nxt = (None, "h")
            if kind == "ffn":
                ffn(l, which, nxt)
            elif l == 0:
                mixer_conv(t == 0, nxt)
            else:
                mixer_pool(t == 0, nxt)
        if final_norm:
            for tb in range(NB):
                need_h(tb)
            srcT, srcB = yT, (lambda c, tb: B.yT[tb])
        else:
            srcT, srcB = xT, (lambda c, tb: B.x[c][tb])
        for ob in range(OWN // 128):
            col0 = HALO + ob * 128
            tbs = sorted(set([col0 // BS, (col0 + 127) // BS]))
            xi = state["xio"] % 3
            state["xio"] += 1
            for half in range(2):
                ps, psB = ps_next()

                def fn(e, ps=ps, half=half, col0=col0, srcT=srcT):
                    ins = None
                    for cc in range(4):
                        c = half * 4 + cc
                        ins = e.transpose(ps[:, cc * 128:(cc + 1) * 128], srcT[:, c, col0:col0 + 128], ident[:, :])
                    return ins

                rd = [srcB(c, tb) for c in range(half * 4, half * 4 + 4) for tb in tbs] + [B.const]
                S.add("pe", fn, reads=rd, writes=(psB,))
                act(xio[xi][:, half * 512:(half + 1) * 512], ps[:, :], AF.Copy, reads=(psB,), writes=(B.xio[xi],))
            dma_sp(y_d[t * OWN + ob * 128:t * OWN + (ob + 1) * 128, :], xio[xi][:, :], (B.xio[xi],), (),
                   xio_sem[xi])
    assert wstate["next"] == len(units), (wstate["next"], len(units))

    for k in list(sems.keys()):
        sems[k] = nc.alloc_semaphore(k)
    for e in ("pe", "act", "dve", "pool"):
        sems[e] = nc.alloc_semaphore("t_" + e)

    def replay(name, eng):
        for waits, fn, inc in S.ops[name]:
            for k, v in waits:
                eng.wait_ge(sems[k], v)
            ins = fn(eng)
            ins.then_inc(sems[inc[0]], inc[1])

    with nc.Block() as block:
        @block.tensor
        def _(e):
            replay("pe", e)

        @block.scalar
        def _(e):
            replay("act", e)

        @block.vector
        def _(e):
            replay("dve", e)

        @block.gpsimd
        def _(e):
            replay("pool", e)

        @block.sync
        def _(e):
            replay("sp", e)
            for ds in out_sems:
                if ds.count:
                    e.wait_ge(sems[ds.key], ds.count)

    stats = {k: len(v) for k, v in S.ops.items()}
    return nc, stats


_CACHE = {}


def _get_program(NT, nstages=6, final_norm=True):
    key = (NT, nstages, final_norm)
    if key not in _CACHE:
        _CACHE[key] = build_program(NT, nstages, final_norm)[0]
    return _CACHE[key]


def _const_pack(inp, mask_val, corr_on):
    c = np.zeros((128, NCST), np.float32)

    def put(col, vec2d):
        for i in range(vec2d.shape[0]):
            c[:, col + 8 * i:col + 8 * i + 8] = vec2d[i].reshape(8, 128).T

    put(C_FFN1, inp["ffn1_norm"])
    put(C_MIX, inp["mix_norm"])
    put(C_MEM, inp["mem_norm"])
    put(C_FFN2, inp["ffn2_norm"])
    put(C_FIN, inp["final_norm"][None, :])
    cw = inp["conv_w"][0]
    for ch in range(6):
        for tap in range(3):
            c[:, C_CONV + 3 * ch + tap] = cw[tap, ch * 128:(ch + 1) * 128]
    ps = inp["pool_scale"][0]
    for i in range(8):
        c[0:96, C_PSC + i] = ps[i * 96:(i + 1) * 96]
    c[:, C_MASK] = mask_val
    for g in range(4):
        w = 2 << g
        for t in range(16):
            c[:, C_CORR + 16 * g + t] = (float(w) / min(t + 1, w)) if corr_on else 1.0
    return c


def _prep_inputs(inputs):
    inp = {k: np.ascontiguousarray(np.asarray(v), dtype=np.float32) for k, v in inputs.items()}
    return inp


_WKEYS = ["ffn1_w_gu", "ffn2_w_gu", "ffn1_w_down", "ffn2_w_down", "w_kv", "w_out",
          "conv_w_in", "pool_w_in", "pool_w_group"]


def _core_x(inp, core):
    b, half = core // 2, core % 2
    x = inp["x"]
    xs = np.zeros((4096 + HALO, D), np.float32)
    if half == 0:
        xs[HALO:] = x[b, 0:4096]
    else:
        xs[:] = x[b, 4096 - HALO:8192]
    return xs


def kernel(**inputs):
    inp = _prep_inputs(inputs)
    NT = 4
    nc = _get_program(NT)
    ident = np.eye(128, dtype=np.float32)
    in_maps = []
    for core in range(NCORE):
        b, half = core // 2, core % 2
        m = {"x": _core_x(inp, core), "mem": inp["mem"][b], "ident": ident,
             "cst": _const_pack(inp, 0.0 if half == 0 else 1.0, half == 0)}
        for k in _WKEYS:
            m[k] = inp[k]
        in_maps.append(m)
    res = run_bass_kernel_spmd(nc, in_maps, core_ids=list(range(NCORE)))
    out = np.empty((4, 8192, D), np.float32)
    for core in range(NCORE):
        b, half = core // 2, core % 2
        out[b, half * 4096:(half + 1) * 4096] = res.results[core]["y"]
    return out
```

```python
import numpy as np
import concourse.bass as bass
import concourse.mybir as mybir
from concourse.bass_utils import run_bass_kernel_spmd

F32 = mybir.dt.float32
BF16 = mybir.dt.bfloat16
AF = mybir.ActivationFunctionType
ALU = mybir.AluOpType

D = 1024
DFF = 2816
NMEM = 256
MIXW = 768
EPS = 1e-6
OWN = 1024
HALO = 20
TT = OWN + HALO
NB = 3
BS = TT // NB
NCORE = 8
NSLOT = 7
SLOT_ELEMS = 2816
PADL = 16
PW = PADL + TT

C_FFN1 = 0
C_MIX = 16
C_MEM = 32
C_FFN2 = 48
C_FIN = 64
C_CONV = 72
C_PSC = 90
C_MASK = 98
C_CORR = 99
NCST = 164

ENGS = ("pe", "act", "dve", "pool", "sp")


class Buf:
    __slots__ = ("w", "r", "name")

    def __init__(self, name=""):
        self.w = {}
        self.r = {}
        self.name = name


class DmaSem:
    def __init__(self, key):
        self.key = key
        self.count = 0


class Sched:
    def __init__(self):
        self.ops = {e: [] for e in ENGS}
        self.cnt = {e: 0 for e in ENGS}
        self.waited = {e: {} for e in ENGS}

    def add(self, eng, fn, reads=(), writes=(), dma=None):
        deps = {}

        def dep(key, val, raw):
            if key == eng:
                if eng == "pe" or not raw:
                    return
            if val > deps.get(key, 0):
                deps[key] = val

        for b in reads:
            for k, v in b.w.items():
                dep(k, v, True)
        for b in writes:
            for k, v in b.w.items():
                dep(k, v, False)
            for k, v in b.r.items():
                dep(k, v, False)
        wd = self.waited[eng]
        waits = []
        for k, v in deps.items():
            if wd.get(k, 0) < v:
                waits.append((k, v))
                wd[k] = v
        if dma is not None:
            dma.count += 16
            tok = (dma.key, dma.count)
            inc = (dma.key, 16)
        else:
            self.cnt[eng] += 1
            tok = (eng, self.cnt[eng])
            inc = (eng, 1)
        self.ops[eng].append((waits, fn, inc))
        for b in reads:
            if b.r.get(tok[0], 0) < tok[1]:
                b.r[tok[0]] = tok[1]
        for b in writes:
            if b.w.get(tok[0], 0) < tok[1]:
                b.w[tok[0]] = tok[1]
        return tok


def build_program(NT, nstages=6, final_norm=True):
    nc = bass.Bass("TRN2", target_bir_lowering=False)
    S = Sched()
    NROWS = NT * OWN + HALO

    def din(name, shape):
        return nc.dram_tensor(name, list(shape), F32, kind="ExternalInput").ap()

    x_d = din("x", (NROWS, D))
    mem_d = din("mem", (NMEM, D))
    ident_d = din("ident", (128, 128))
    cst_d = din("cst", (128, NCST))
    w_gu_d = [din("ffn1_w_gu", (2, D, 2 * DFF)), din("ffn2_w_gu", (2, D, 2 * DFF))]
    w_dn_d = [din("ffn1_w_down", (2, DFF, D)), din("ffn2_w_down", (2, DFF, D))]
    w_kv_d = din("w_kv", (2, D, 512))
    w_out_d = din("w_out", (2, D, D))
    conv_in_d = din("conv_w_in", (1, D, 2560))
    pool_in_d = din("pool_w_in", (1, D, D))
    pool_g_d = din("pool_w_group", (1, 4, 192, 192))
    y_d = nc.dram_tensor("y", [NT * OWN, D], F32, kind="ExternalOutput").ap()

    sb = nc.alloc_sbuf_tensor
    xT = sb("xT", [128, 8, TT], F32)
    hT = sb("hT", [128, 8, TT], BF16)
    A = sb("A", [128, 22 * TT], BF16)
    mixb = sb("mixb", [128, 8, TT], BF16)
    attb = sb("attb", [128, 2, TT], BF16)
    qb = sb("qb", [128, 2, TT], BF16)
    slots = [sb(f"ws{i}", [128, SLOT_ELEMS], BF16) for i in range(NSLOT)]
    ebuf = [sb(f"eb{i}", [128, 4, BS], BF16) for i in range(2)]
    rcb = [sb(f"rc{i}", [128, BS], F32) for i in range(2)]
    sqb = sb("sqb", [128, 8, BS], BF16)
    msb = [sb(f"ms{i}", [128, BS], F32) for i in range(2)]
    rsb = [sb(f"rs{i}", [128, BS], F32) for i in range(2)]
    tmpb = [sb(f"tmp{i}", [128, BS], F32) for i in range(3)]
    xio = [sb(f"xio{i}", [128, D], F32) for i in range(3)]
    xo = [sb(f"xo{i}", [128, D], F32) for i in range(2)]
    kT = sb("kT", [128, 2, 2, NMEM], BF16)
    vS = sb("vS", [128, 2, 2, 256], BF16)
    ident = sb("ident_s", [128, 128], F32)
    ones = sb("ones_s", [128, 128], BF16)
    cst = sb("cst_s", [128, NCST], F32)
    psum = [nc.alloc_psum_tensor(f"ps{i}", [128, 512], F32) for i in range(8)]

    def A_view(off_bf16, shape, dtype):
        n = int(np.prod(shape[1:]))
        if dtype == F32:
            ap = A[:, off_bf16:off_bf16 + 2 * n].bitcast(F32)
        else:
            ap = A[:, off_bf16:off_bf16 + n]
        if len(shape) == 3:
            ap = ap.rearrange("p (a b) -> p a b", a=shape[1])
        return ap

    aT = A_view(0, [128, 22, TT], BF16)
    yT = A_view(0, [128, 8, TT], F32)
    memT = A_view(0, [128, 8, NMEM], F32)
    ubuf = [A_view(i * 2 * (TT + 2), [128, TT + 2], F32) for i in range(2)]
    o1 = 2 * 2 * (TT + 2)
    ybuf = [A_view(o1 + i * 2 * TT, [128, TT], F32) for i in range(2)]
    pbuf = [A_view(i * 2 * PW, [128, PW], F32) for i in range(4)]
    sbufs = [A_view((4 + i) * 2 * PW, [128, PW], F32) for i in range(2)]
    o2 = 6 * 2 * PW
    dbuf = A_view(o2, [128, 8, TT], BF16)

    class NS:
        pass

    B = NS()
    B.x = [[Buf() for _ in range(NB)] for _ in range(8)]
    B.h = [Buf() for _ in range(NB)]
    B.a = [[Buf() for _ in range(NB)] for _ in range(22)]
    B.A = Buf()
    B.mix = [[Buf() for _ in range(NB)] for _ in range(8)]
    B.att = [[Buf() for _ in range(NB)] for _ in range(2)]
    B.q = [[Buf() for _ in range(NB)] for _ in range(2)]
    B.slot = [Buf() for _ in range(NSLOT)]
    B.e = [Buf() for _ in range(2)]
    B.rc = [Buf() for _ in range(2)]
    B.sq = Buf()
    B.ms = [Buf() for _ in range(2)]
    B.rs = [Buf() for _ in range(2)]
    B.tmp = [Buf() for _ in range(3)]
    B.xio = [Buf() for _ in range(3)]
    B.xo = [Buf() for _ in range(2)]
    B.kv = Buf()
    B.const = Buf()
    B.ps = [Buf() for _ in range(8)]
    B.u = [Buf() for _ in range(2)]
    B.y = [Buf() for _ in range(2)]
    B.p = [Buf() for _ in range(4)]
    B.s = [Buf() for _ in range(2)]
    B.d = [Buf() for _ in range(8)]
    B.yT = [Buf() for _ in range(NB)]
    B.memT = Buf()

    sems = {}

    def new_dma_sem(name):
        sems[name] = None
        return DmaSem(name)

    slot_sem = [new_dma_sem(f"d_slot{i}") for i in range(NSLOT)]
    xio_sem = [new_dma_sem(f"d_xio{i}") for i in range(3)]
    const_sem = new_dma_sem("d_const")
    xo_sem = [new_dma_sem(f"d_xo{i}") for i in range(2)]
    out_sems = xo_sem

    state = {"ps": 0, "tmp": 0, "nrm": 0, "xio": 0, "xo": 0, "e": 0}

    def ps_next():
        i = state["ps"]
        state["ps"] = (i + 1) % 8
        return psum[i], B.ps[i]

    def blk(tb):
        return slice(tb * BS, (tb + 1) * BS)

    units = []
    wstate = {"emitted": 0, "next": 0}

    def sview(slot, off, P, shape):
        n = int(np.prod(shape))
        ap = slot[0:P, off:off + n]
        if len(shape) == 2:
            ap = ap.rearrange("p (a b) -> p a b", a=shape[0])
        elif len(shape) == 3:
            ap = ap.rearrange("p (a b c) -> p a b c", a=shape[0], b=shape[1])
        return ap

    def kview(dram2d, r0, nk, c0, ncol, P=128):
        return dram2d[r0:r0 + nk * P, c0:c0 + ncol].rearrange("(k p) n -> p k n", p=P)

    def emit_unit_dma(idx):
        name, parts = units[idx]
        s = idx % NSLOT
        for (dst_fn, src) in parts:
            dst = dst_fn(slots[s])

            def fn(e, dst=dst, src=src):
                return e.dma_start(out=dst, in_=src)

            S.add("pool", fn, reads=(), writes=(B.slot[s],), dma=slot_sem[s])

    def wnext(name):
        i = wstate["next"]
        assert units[i][0] == name, (units[i][0], name)
        lim = min(len(units), i + NSLOT - 1)
        while wstate["emitted"] < lim:
            emit_unit_dma(wstate["emitted"])
            wstate["emitted"] += 1
        wstate["next"] = i + 1
        s = i % NSLOT
        return slots[s], B.slot[s]

    def U(name, parts):
        units.append((name, parts))

    def list_ffn_units(l, which):
        gu = w_gu_d[which][l]
        dn = w_dn_d[which][l]
        for fp in range(11):
            U(f"G{l}{which}_{fp}", [(lambda s: sview(s, 0, 128, [8, 256]), kview(gu, 0, 8, 256 * fp, 256))])
            U(f"U{l}{which}_{fp}", [(lambda s: sview(s, 0, 128, [8, 256]), kview(gu, 0, 8, DFF + 256 * fp, 256))])
        for op in range(4):
            for kh in range(2):
                U(f"D{l}{which}_{op}_{kh}",
                  [(lambda s: sview(s, 0, 128, [11, 256]), kview(dn, kh * 1408, 11, 256 * op, 256))])

    def list_mix_units(l):
        if l == 0:
            w = conv_in_d[0]
            for c in range(6):
                U(f"GCV_{c}", [
                    (lambda s: sview(s, 0, 128, [8, 256])[:, :, 0:128], kview(w, 0, 8, MIXW + 128 * c, 128)),
                    (lambda s: sview(s, 0, 128, [8, 256])[:, :, 128:256], kview(w, 0, 8, 2 * MIXW + 128 * c, 128)),
                ])
                if c % 2 == 1:
                    U(f"GB_{c // 2}", [(lambda s: sview(s, 0, 128, [8, 256]), kview(w, 0, 8, 256 * (c // 2), 256))])
            U("Q0", [(lambda s: sview(s, 0, 128, [8, 256]), kview(w, 0, 8, 3 * MIXW, 256))])
            wo = w_out_d[0]
            for op in range(4):
                U(f"WO0_{op}", [(lambda s: sview(s, 0, 128, [8, 256]), kview(wo, 0, 8, 256 * op, 256))])
        else:
            w = pool_in_d[0]
            for g in range(4):
                U(f"PW_{g}", [(lambda s: sview(s, 0, 128, [8, 192]), kview(w, 0, 8, 192 * g, 192))])
            U("Q1", [(lambda s: sview(s, 0, 128, [8, 256]), kview(w, 0, 8, MIXW, 256))])
            pg = pool_g_d[0].rearrange("g (kc p) n -> p g kc n", p=96)
            U("PG", [(lambda s: sview(s, 0, 96, [4, 2, 192]), pg)])
            wo = w_out_d[1]
            for op in range(4):
                U(f"WO1_{op}", [
                    (lambda s: sview(s, 0, 96, [8, 256]), kview(wo, 0, 8, 256 * op, 256, P=96)),
                    (lambda s: sview(s, 2048, 128, [2, 256]), kview(wo, MIXW, 2, 256 * op, 256)),
                ])

    for l in range(2):
        U(f"KVK{l}", [(lambda s: sview(s, 0, 128, [8, 256]), kview(w_kv_d[l], 0, 8, 0, 256))])
        U(f"KVV{l}", [(lambda s: sview(s, 0, 128, [8, 256]), kview(w_kv_d[l], 0, 8, 256, 256))])
    stage_list = []
    for l in range(2):
        stage_list += [("ffn", l, 0), ("mix", l, 0), ("ffn", l, 1)]
    stage_list = stage_list[:nstages]
    for t in range(NT):
        for (kind, l, which) in stage_list:
            if kind == "ffn":
                list_ffn_units(l, which)
            else:
                list_mix_units(l)

    def mm_group(out_ap, pairs, reads, psB):
        n = len(pairs)

        def fn(e):
            ins = None
            for i, (l, r) in enumerate(pairs):
                ins = e.matmul(out_ap, l, r, start=(i == 0), stop=(i == n - 1))
            return ins

        return S.add("pe", fn, reads=reads, writes=(psB,))

    def act(out, in_, func, reads, writes, scale=None, bias=None):
        kw = {}
        if scale is not None:
            kw["scale"] = scale
        if bias is not None:
            kw["bias"] = bias

        def fn(e):
            return e.activation(out=out, in_=in_, func=func, **kw)

        return S.add("act", fn, reads=reads, writes=writes)

    def dve_tt(out, in0, in1, op, reads, writes, eng="dve"):
        def fn(e):
            return e.tensor_tensor(out=out, in0=in0, in1=in1, op=op)

        return S.add(eng, fn, reads=reads, writes=writes)

    def dve_ts(out, in0, s1, s2, op0, op1, reads, writes, eng="dve"):
        def fn(e):
            if op1 is None:
                return e.tensor_scalar(out=out, in0=in0, scalar1=s1, scalar2=None, op0=op0)
            return e.tensor_scalar(out=out, in0=in0, scalar1=s1, scalar2=s2, op0=op0, op1=op1)

        return S.add(eng, fn, reads=reads, writes=writes)

    def dve_stt(out, in0, scalar, in1, op0, op1, reads, writes):
        def fn(e):
            return e.scalar_tensor_tensor(out=out, in0=in0, scalar=scalar, in1=in1, op0=op0, op1=op1)

        return S.add("dve", fn, reads=reads, writes=writes)

    def memset(eng, ap, val, writes, reads=()):
        def fn(e):
            return e.memset(ap, val)

        return S.add(eng, fn, reads=reads, writes=writes)

    def norm_block(src_fn, sbufs_r, sl, gcol, dst_fn, dst_bufs_fn, extra=()):
        n = sl.stop - sl.start
        i2 = state["nrm"] % 2
        state["nrm"] += 1
        src = src_fn(sl)
        for hf in range(2):
            act(sqb[:, 4 * hf:4 * hf + 4, 0:n], src[:, 4 * hf:4 * hf + 4, :], AF.Square,
                reads=sbufs_r[4 * hf:4 * hf + 4], writes=(B.sq,))
        ps, psB = ps_next()
        mm_group(ps[:, 0:n], [(ones[:, :], sqb[:, c, 0:n]) for c in range(8)],
                 reads=(B.sq, B.const), psB=psB)
        act(msb[i2][:, 0:n], ps[:, 0:n], AF.Ln, reads=(psB,), writes=(B.ms[i2],), scale=1.0 / D, bias=EPS)
        act(rsb[i2][:, 0:n], msb[i2][:, 0:n], AF.Exp, reads=(B.ms[i2],), writes=(B.rs[i2],), scale=-0.5)
        for c in range(8):
            dve_stt(dst_fn(c, sl), src[:, c, :], cst[:, gcol + c:gcol + c + 1], rsb[i2][:, 0:n],
                    ALU.mult, ALU.mult,
                    reads=(sbufs_r[c], B.rs[i2], B.const) + tuple(extra), writes=dst_bufs_fn(c))

    tile_blocks = [blk(tb) for tb in range(NB)]

    def fence_A():
        for e_ in ("pe", "act", "dve", "pool"):
            if S.cnt[e_]:
                B.A.w[e_] = S.cnt[e_]

    pend = {"gcol": None, "dst": "h", "done": set()}

    def set_pend(gcol, dst="h"):
        pend["gcol"] = gcol
        pend["dst"] = dst
        pend["done"] = set()

    def need_h(tb):
        if pend["gcol"] is None or tb in pend["done"]:
            return
        pend["done"].add(tb)
        xb = [B.x[c][tb] for c in range(8)]
        if pend["dst"] == "h":
            norm_block(lambda sl: xT[:, :, sl], xb, blk(tb), pend["gcol"],
                       lambda c, sl: hT[:, c, sl], lambda c: (B.h[tb],))
        else:
            fence_A()
            norm_block(lambda sl: xT[:, :, sl], xb, blk(tb), pend["gcol"],
                       lambda c, sl: yT[:, c, sl], lambda c: (B.yT[tb],), extra=(B.A,))

    def ffn(l, which, nxt):
        fence_A()
        for fp in range(11):
            gs, gB = wnext(f"G{l}{which}_{fp}")
            us, uB = wnext(f"U{l}{which}_{fp}")
            gv = sview(gs, 0, 128, [8, 256])
            uv = sview(us, 0, 128, [8, 256])
            for tb in range(NB):
                for f in (2 * fp, 2 * fp + 1):
                    cs = slice((f % 2) * 128, (f % 2) * 128 + 128)
                    sl = blk(tb)
                    need_h(tb)
                    pg, pgB = ps_next()
                    pu, puB = ps_next()
                    mm_group(pg[:, 0:BS], [(gv[:, k, cs], hT[:, k, sl]) for k in range(8)],
                             reads=(gB, B.h[tb]), psB=pgB)
                    mm_group(pu[:, 0:BS], [(uv[:, k, cs], hT[:, k, sl]) for k in range(8)],
                             reads=(uB, B.h[tb]), psB=puB)
                    ti = state["tmp"] % 3
                    state["tmp"] += 1
                    act(tmpb[ti][:, :], pg[:, 0:BS], AF.Silu, reads=(pgB,), writes=(B.tmp[ti],))
                    dve_tt(aT[:, f, sl], tmpb[ti][:, :], pu[:, 0:BS], ALU.mult,
                           reads=(B.tmp[ti], puB, B.A), writes=(B.a[f][tb],))
                    if fp == 0 and tb == 0 and f == 0:
                        need_h(NB - 1)
        for op in range(4):
            d0, d0B = wnext(f"D{l}{which}_{op}_0")
            d1, d1B = wnext(f"D{l}{which}_{op}_1")
            dv = [sview(d0, 0, 128, [11, 256]), sview(d1, 0, 128, [11, 256])]
            order = ([(o, tb) for o in (2 * op, 2 * op + 1) for tb in range(NB)] if op < 3 else
                     [(o, tb) for tb in range(NB) for o in (2 * op, 2 * op + 1)])
            for (o, tb) in order:
                cs = slice((o % 2) * 128, (o % 2) * 128 + 128)
                sl = blk(tb)
                ps, psB = ps_next()
                mm_group(ps[:, 0:BS], [(dv[k // 11][:, k % 11, cs], aT[:, k, sl]) for k in range(22)],
                         reads=[d0B, d1B] + [B.a[k][tb] for k in range(22)], psB=psB)
                dve_stt(xT[:, o, sl], ps[:, 0:BS], 0.5, xT[:, o, sl], ALU.mult, ALU.add,
                        reads=(psB, B.x[o][tb]), writes=(B.x[o][tb],))
                if o == 7:
                    if tb == 0:
                        set_pend(*nxt)
                    else:
                        need_h(tb - 1)
        need_h(NB - 2)

    def q_proj(name, ws, wB):
        wv = sview(ws, 0, 128, [8, 256])
        for j in range(2):
            cs = slice(j * 128, j * 128 + 128)
            for tb in range(NB):
                sl = blk(tb)
                ps, psB = ps_next()
                mm_group(ps[:, 0:BS], [(wv[:, k, cs], hT[:, k, sl]) for k in range(8)],
                         reads=(wB, B.h[tb]), psB=psB)
                act(qb[:, j, sl], ps[:, 0:BS], AF.Copy, reads=(psB,), writes=(B.q[j][tb],))

    def attention(l):
        steps = [(tb, j) for tb in range(NB) for j in range(2)]
        eis = {}

        def scores(tb, j):
            sl = blk(tb)
            ei = state["e"] % 2
            state["e"] += 1
            eis[(tb, j)] = ei
            for hh in range(2):
                r = slice(64 * hh, 64 * hh + 64)
                for mc in range(2):
                    ps, psB = ps_next()
                    mm_group(ps[:, 0:BS], [(kT[r, l, j, mc * 128:(mc + 1) * 128], qb[r, j, sl])],
                             reads=(B.kv, B.q[j][tb]), psB=psB)
                    act(ebuf[ei][:, hh * 2 + mc, :], ps[:, 0:BS], AF.Exp, reads=(psB,),
                        writes=(B.e[ei],), scale=0.125)

        def pv(tb, j):
            sl = blk(tb)
            ei = eis[(tb, j)]
            ppv, ppvB = ps_next()
            psm, psmB = ps_next()
            for hh in range(2):
                r = slice(64 * hh, 64 * hh + 64)
                h = 2 * j + hh
                mm_group(ppv[r, 0:BS],
                         [(vS[:, l, mc, h * 64:(h + 1) * 64], ebuf[ei][:, hh * 2 + mc, :]) for mc in range(2)],
                         reads=(B.kv, B.e[ei]), psB=ppvB)
                mm_group(psm[r, 0:BS],
                         [(ones[:, 0:64], ebuf[ei][:, hh * 2 + mc, :]) for mc in range(2)],
                         reads=(B.const, B.e[ei]), psB=psmB)
            act(rcb[ei][:, :], psm[:, 0:BS], AF.Ln, reads=(psmB,), writes=(B.rc[ei],))
            act(rcb[ei][:, :], rcb[ei][:, :], AF.Exp, reads=(B.rc[ei],), writes=(B.rc[ei],), scale=-1.0)
            dve_tt(attb[:, j, sl], ppv[:, 0:BS], rcb[ei][:, :], ALU.mult,
                   reads=(ppvB, B.rc[ei]), writes=(B.att[j][tb],))

        scores(*steps[0])
        for i, st in enumerate(steps):
            if i + 1 < len(steps):
                scores(*steps[i + 1])
            pv(*st)

    def w_out_stage(l, nxt):
        for op in range(4):
            ws, wB = wnext(f"WO{l}_{op}")
            if l == 0:
                wv = sview(ws, 0, 128, [8, 256])
            else:
                wa = sview(ws, 0, 96, [8, 256])
                wb_ = sview(ws, 2048, 128, [2, 256])
            order = ([(o, tb) for o in (2 * op, 2 * op + 1) for tb in range(NB)] if op < 3 else
                     [(o, tb) for tb in range(NB) for o in (2 * op, 2 * op + 1)])
            for (o, tb) in order:
                cs = slice((o % 2) * 128, (o % 2) * 128 + 128)
                if True:
                    sl = blk(tb)
                    ps, psB = ps_next()
                    if l == 0:
                        pairs = [(wv[:, k, cs], mixb[:, k, sl]) for k in range(6)]
                        pairs += [(wv[:, 6 + j, cs], attb[:, j, sl]) for j in range(2)]
                        rd = [wB] + [B.mix[k][tb] for k in range(6)] + [B.att[j][tb] for j in range(2)]
                    else:
                        pairs = [(wa[:, k, cs], mixb[0:96, k, sl]) for k in range(8)]
                        pairs += [(wb_[:, j, cs], attb[:, j, sl]) for j in range(2)]
                        rd = [wB] + [B.mix[k][tb] for k in range(8)] + [B.att[j][tb] for j in range(2)]
                    mm_group(ps[:, 0:BS], pairs, reads=rd, psB=psB)
                    dve_tt(xT[:, o, sl], ps[:, 0:BS], xT[:, o, sl], ALU.add,
                           reads=(psB, B.x[o][tb]), writes=(B.x[o][tb],))
                    if o == 7:
                        if tb == 0:
                            set_pend(*nxt)
                        else:
                            need_h(tb - 1)
        need_h(NB - 2)

    def mixer_conv(first_tile, nxt):
        fence_A()
        for i in range(2):
            memset("dve", ubuf[i][:, 0:2], 0.0, (B.u[i],), reads=(B.A,))
        for c in range(6):
            ws, wB = wnext(f"GCV_{c}")
            wv = sview(ws, 0, 128, [8, 256])
            ui = c % 2
            for tb in range(NB):
                sl = blk(tb)
                need_h(tb)
                p1, p1B = ps_next()
                p2, p2B = ps_next()
                mm_group(p1[:, 0:BS], [(wv[:, k, 0:128], hT[:, k, sl]) for k in range(8)],
                         reads=(wB, B.h[tb]), psB=p1B)
                mm_group(p2[:, 0:BS], [(wv[:, k, 128:256], hT[:, k, sl]) for k in range(8)],
                         reads=(wB, B.h[tb]), psB=p2B)
                ti = state["tmp"] % 3
                state["tmp"] += 1
                act(tmpb[ti][:, :], p2[:, 0:BS], AF.Copy, reads=(p2B,), writes=(B.tmp[ti],))
                dve_tt(ubuf[ui][:, 2 + tb * BS:2 + (tb + 1) * BS], p1[:, 0:BS], tmpb[ti][:, :], ALU.mult,
                       reads=(p1B, B.tmp[ti], B.A), writes=(B.u[ui],))
                if c == 0 and tb == 0:
                    need_h(NB - 1)
            if first_tile:
                dve_ts(ubuf[ui][:, 2:2 + HALO], ubuf[ui][:, 2:2 + HALO], cst[:, C_MASK:C_MASK + 1], None,
                       ALU.mult, None, reads=(B.u[ui], B.const), writes=(B.u[ui],))
            cw = lambda tap: cst[:, C_CONV + 3 * c + tap:C_CONV + 3 * c + tap + 1]
            dve_ts(ybuf[ui][:, :], ubuf[ui][:, 2:2 + TT], cw(2), None, ALU.mult, None,
                   reads=(B.u[ui], B.const, B.A), writes=(B.y[ui],))
            dve_stt(ybuf[ui][:, :], ubuf[ui][:, 1:1 + TT], cw(1), ybuf[ui][:, :], ALU.mult, ALU.add,
                    reads=(B.u[ui], B.y[ui], B.const), writes=(B.y[ui],))
            dve_stt(ybuf[ui][:, :], ubuf[ui][:, 0:TT], cw(0), ybuf[ui][:, :], ALU.mult, ALU.add,
                    reads=(B.u[ui], B.y[ui], B.const), writes=(B.y[ui],))
            if c % 2 == 1:
                ws2, wB2 = wnext(f"GB_{c // 2}")
                wv2 = sview(ws2, 0, 128, [8, 256])
                for cc in (c - 1, c):
                    cs = slice((cc % 2) * 128, (cc % 2) * 128 + 128)
                    for tb in range(NB):
                        sl = blk(tb)
                        ps, psB = ps_next()
                        mm_group(ps[:, 0:BS], [(wv2[:, k, cs], hT[:, k, sl]) for k in range(8)],
                                 reads=(wB2, B.h[tb]), psB=psB)
                        dve_tt(mixb[:, cc, sl], ps[:, 0:BS], ybuf[cc % 2][:, sl], ALU.mult,
                               reads=(psB, B.y[cc % 2]), writes=(B.mix[cc][tb],))
        ws, wB = wnext("Q0")
        q_proj("Q0", ws, wB)
        attention(0)
        w_out_stage(0, nxt)

    def mixer_pool(first_tile, nxt):
        fence_A()
        for i in range(4):
            memset("dve", pbuf[i][:, 0:PADL], 0.0, (B.p[i],), reads=(B.A,))
        for gp in range(2):
            wvs = []
            for g in (2 * gp, 2 * gp + 1):
                ws, wB = wnext(f"PW_{g}")
                wvs.append((sview(ws, 0, 128, [8, 192]), wB))
            chunks = [4 * gp + ii for ii in range(4)]
            for tb in range(NB):
                sl = blk(tb)
                need_h(tb)
                for i in chunks:
                    wv, wB = wvs[(i // 2) % 2]
                    pi = i % 4
                    cs = slice((i % 2) * 96, (i % 2) * 96 + 96)
                    ps, psB = ps_next()
                    mm_group(ps[0:96, 0:BS], [(wv[:, k, cs], hT[:, k, sl]) for k in range(8)],
                             reads=(wB, B.h[tb]), psB=psB)
                    act(pbuf[pi][0:96, PADL + tb * BS:PADL + (tb + 1) * BS], ps[0:96, 0:BS], AF.Copy,
                        reads=(psB, B.A), writes=(B.p[pi],))
                    if gp == 0 and tb == 0 and i == chunks[1]:
                        need_h(NB - 1)
            for i in chunks:
                g = i // 2
                win = 2 << g
                pi = i % 4
                P = pbuf[pi]
                if first_tile:
                    dve_ts(P[0:96, PADL:PADL + HALO], P[0:96, PADL:PADL + HALO], cst[0:96, C_MASK:C_MASK + 1],
                           None, ALU.mult, None, reads=(B.p[pi], B.const), writes=(B.p[pi],))
                cur, curB = P, B.p[pi]
                sh = 1
                si = 0
                while sh < win:
                    lo = 2 * sh - 1
                    dst, dstB = sbufs[si], B.s[si]
                    dve_tt(dst[0:96, lo:PW], cur[0:96, lo:PW], cur[0:96, lo - sh:PW - sh], ALU.add,
                           reads=(curB, B.A), writes=(dstB,))
                    cur, curB = dst, dstB
                    si ^= 1
                    sh *= 2
                if first_tile:
                    c0 = PADL + HALO
                    dve_tt(cur[0:96, c0:c0 + 16], cur[0:96, c0:c0 + 16],
                           cst[0:96, C_CORR + 16 * g:C_CORR + 16 * g + 16], ALU.mult,
                           reads=(curB, B.const), writes=(curB,))
                dve_stt(dbuf[0:96, i, :], cur[0:96, PADL:PW], 1.0 / win, P[0:96, PADL:PW], ALU.mult, ALU.subtract,
                        reads=(curB, B.p[pi], B.A), writes=(B.d[i],))
        ws, wB = wnext("Q1")
        q_proj("Q1", ws, wB)
        attention(1)
        ws, wB = wnext("PG")
        pgv = sview(ws, 0, 96, [4, 2, 192])
        for g in range(4):
            for oc in range(2):
                i = 2 * g + oc
                for tb in range(NB):
                    sl = blk(tb)
                    ps, psB = ps_next()
                    mm_group(ps[0:96, 0:BS],
                             [(pgv[:, g, kc, oc * 96:(oc + 1) * 96], dbuf[0:96, 2 * g + kc, sl]) for kc in range(2)],
                             reads=(wB, B.d[2 * g], B.d[2 * g + 1]), psB=psB)
                    act(mixb[0:96, i, sl], ps[0:96, 0:BS], AF.Identity, reads=(psB, B.const),
                        writes=(B.mix[i][tb],), scale=cst[0:96, C_PSC + i:C_PSC + i + 1])
        w_out_stage(1, nxt)

    def dma_sp(out, in_, reads, writes, sem):
        def fn(e):
            return e.dma_start(out=out, in_=in_)

        return S.add("sp", fn, reads=reads, writes=writes, dma=sem)

    dma_sp(cst[:, :], cst_d[:, :], (), (B.const,), const_sem)
    dma_sp(ident[:, :], ident_d[:, :], (), (B.const,), const_sem)
    memset("dve", ones[:, :], 1.0, (B.const,))

    def load_transpose(src_d, row0, nrows, dst_fn, dst_bufs_fn, extra=()):
        nblk = (nrows + 127) // 128
        for b_ in range(nblk):
            rows = min(128, nrows - b_ * 128)
            xi = state["xio"] % 3
            state["xio"] += 1
            dma_sp(xio[xi][0:rows, :], src_d[row0 + b_ * 128:row0 + b_ * 128 + rows, :], (), (B.xio[xi],),
                   xio_sem[xi])
            for half in range(2):
                ps, psB = ps_next()

                def fn(e, ps=ps, xi=xi, rows=rows, half=half):
                    ins = None
                    for cc in range(4):
                        c = half * 4 + cc
                        ins = e.transpose(ps[:, cc * 128:cc * 128 + rows], xio[xi][0:rows, c * 128:(c + 1) * 128],
                                          ident[0:rows, 0:rows])
                    return ins

                S.add("pe", fn, reads=(B.xio[xi], B.const), writes=(psB,))
                src = ps[:, :].rearrange("p (a b) -> p a b", a=4)[:, :, 0:rows]
                dst = dst_fn(half * 4, b_ * 128, rows)
                act(dst, src, AF.Copy, reads=(psB,) + tuple(extra), writes=dst_bufs_fn(half * 4, b_ * 128, rows))

    load_transpose(mem_d, 0, NMEM, lambda c0, t0, n: memT[:, c0:c0 + 4, t0:t0 + n],
                   lambda c0, t0, n: (B.memT,), extra=(B.A,))
    for l in range(2):
        norm_block(lambda sl: memT[:, :, sl], [B.memT] * 8, slice(0, NMEM), C_MEM + 8 * l,
                   lambda c, sl: hT[:, c, sl], lambda c: (B.h[0],))
        ws, wB = wnext(f"KVK{l}")
        wv = sview(ws, 0, 128, [8, 256])
        for j in range(2):
            ps, psB = ps_next()
            mm_group(ps[:, 0:NMEM], [(wv[:, k, j * 128:(j + 1) * 128], hT[:, k, 0:NMEM]) for k in range(8)],
                     reads=(wB, B.h[0]), psB=psB)
            act(kT[:, l, j, :], ps[:, 0:NMEM], AF.Copy, reads=(psB,), writes=(B.kv,))
        ws, wB = wnext(f"KVV{l}")
        wv = sview(ws, 0, 128, [8, 256])
        for mc in range(2):
            ps, psB = ps_next()
            mm_group(ps[:, 0:256], [(hT[:, k, mc * 128:(mc + 1) * 128], wv[:, k, :]) for k in range(8)],
                     reads=(wB, B.h[0]), psB=psB)
            act(vS[:, l, mc, :], ps[:, 0:256], AF.Copy, reads=(psB,), writes=(B.kv,))

    def xbufs(c0, t0, n):
        tbs = sorted(set([t0 // BS, (t0 + n - 1) // BS]))
        return [B.x[c][tb] for c in range(c0, c0 + 4) for tb in tbs]

    def stage_gcol(kind, l, which):
        if kind == "ffn":
            return (C_FFN1 if which == 0 else C_FFN2) + 8 * l
        return C_MIX + 8 * l

    NIB = (TT + 127) // 128
    NOB = OWN // 128
    in_slot = {}

    def in_load(t, b_):
        rows = min(128, TT - b_ * 128)
        xi = state["xio"] % 3
        state["xio"] += 1
        in_slot[(t, b_)] = xi
        r0 = t * OWN + b_ * 128
        dma_sp(xio[xi][0:rows, :], x_d[r0:r0 + rows, :], (), (B.xio[xi],), xio_sem[xi])

    def in_xpose(t, b_):
        rows = min(128, TT - b_ * 128)
        xi = in_slot[(t, b_)]
        for half in range(2):
            ps, psB = ps_next()

            def fn(e, ps=ps, xi=xi, rows=rows, half=half):
                ins = None
                for cc in range(4):
                    c = half * 4 + cc
                    ins = e.transpose(ps[:, cc * 128:cc * 128 + rows], xio[xi][0:rows, c * 128:(c + 1) * 128],
                                      ident[0:rows, 0:rows])
                return ins

            S.add("pe", fn, reads=(B.xio[xi], B.const), writes=(psB,))
            src = ps[:, :].rearrange("p (a b) -> p a b", a=4)[:, :, 0:rows]
            act(xT[:, half * 4:half * 4 + 4, b_ * 128:b_ * 128 + rows], src, AF.Copy, reads=(psB,),
                writes=xbufs(half * 4, b_ * 128, rows))

    def out_block(t, ob, srcT, srcB):
        col0 = HALO + ob * 128
        tbs = sorted(set([col0 // BS, (col0 + 127) // BS]))
        xi = state["xo"] % 2
        state["xo"] += 1
        for half in range(2):
            ps, psB = ps_next()

            def fn(e, ps=ps, half=half, col0=col0, srcT=srcT):
                ins = None
                for cc in range(4):
                    c = half * 4 + cc
                    ins = e.transpose(ps[:, cc * 128:(cc + 1) * 128], srcT[:, c, col0:col0 + 128], ident[:, :])
                return ins

            rd = [srcB(c, tb) for c in range(half * 4, half * 4 + 4) for tb in tbs] + [B.const]
            S.add("pe", fn, reads=rd, writes=(psB,))
            act(xo[xi][:, half * 512:(half + 1) * 512], ps[:, :], AF.Copy, reads=(psB,), writes=(B.xo[xi],))

        def fn2(e, dst=y_d[t * OWN + ob * 128:t * OWN + (ob + 1) * 128, :], src=xo[xi][:, :]):
            return e.dma_start(out=dst, in_=src)

        S.add("act", fn2, reads=(B.xo[xi],), writes=(), dma=xo_sem[xi])

    if final_norm:
        srcT, srcB = yT, (lambda c, tb: B.yT[tb])
    else:
        srcT, srcB = xT, (lambda c, tb: B.x[c][tb])

    def first_pend():
        if stage_list:
            set_pend(stage_gcol(*stage_list[0]))
        elif final_norm:
            set_pend(C_FIN, "y")
        else:
            set_pend(None)

    for b_ in range(3):
        in_load(0, b_)
    for t in range(NT):
        seq = []
        outs = [("out", ob) for ob in range(NOB)] if t > 0 else []
        for b_ in range(NIB):
            seq.append(("in", b_))
            if b_ == 1:
                seq.append(("flush", 0))
            if b_ >= 1 and outs:
                seq.append(outs.pop(0))
            if b_ == 4:
                seq.append(("norm", 0))
            if b_ == 7:
                seq.append(("norm", 1))
        seq += outs
        for (kind, v) in seq:
            if kind == "in":
                in_xpose(t, v)
                if v + 3 < NIB:
                    in_load(t, v + 3)
                elif t + 1 < NT:
                    in_load(t + 1, v + 3 - NIB)
            elif kind == "flush":
                if t > 0 and final_norm:
                    for tb in range(NB):
                        need_h(tb)
                first_pend()
            elif kind == "out":
                out_block(t - 1, v, srcT, srcB)
            elif kind == "norm":
                if not (not stage_list and final_norm):
                    need_h(v)
        for si, (kind, l, which) in enumerate(stage_list):
            if si + 1 < len(stage_list):
                nxt = (stage_gcol(*stage_list[si + 1]), "h")
            elif final_norm:
                nxt = (C_FIN, "y")
            else:
                nxt = (None, "h")
            if kind == "ffn":
                ffn(l, which, nxt)
            elif l == 0:
                mixer_conv(t == 0, nxt)
            else:
                mixer_pool(t == 0, nxt)
    if final_norm:
        for tb in range(NB):
            need_h(tb)
    for ob in range(NOB):
        out_block(NT - 1, ob, srcT, srcB)
    assert wstate["next"] == len(units), (wstate["next"], len(units))

    for k in list(sems.keys()):
        sems[k] = nc.alloc_semaphore(k)
    for e in ("pe", "act", "dve", "pool"):
        sems[e] = nc.alloc_semaphore("t_" + e)

    def replay(name, eng):
        for waits, fn, inc in S.ops[name]:
            for k, v in waits:
                eng.wait_ge(sems[k], v)
            ins = fn(eng)
            ins.then_inc(sems[inc[0]], inc[1])

    with nc.Block() as block:
        @block.tensor
        def _(e):
            replay("pe", e)

        @block.scalar
        def _(e):
            replay("act", e)

        @block.vector
        def _(e):
            replay("dve", e)

        @block.gpsimd
        def _(e):
            replay("pool", e)

        @block.sync
        def _(e):
            replay("sp", e)
            for ds in out_sems:
                if ds.count:
                    e.wait_ge(sems[ds.key], ds.count)

    stats = {k: len(v) for k, v in S.ops.items()}
    return nc, stats


_CACHE = {}


def _get_program(NT, nstages=6, final_norm=True):
    key = (NT, nstages, final_norm)
    if key not in _CACHE:
        _CACHE[key] = build_program(NT, nstages, final_norm)[0]
    return _CACHE[key]


def _const_pack(inp, mask_val, corr_on):
    c = np.zeros((128, NCST), np.float32)

    def put(col, vec2d):
        for i in range(vec2d.shape[0]):
            c[:, col + 8 * i:col + 8 * i + 8] = vec2d[i].reshape(8, 128).T

    put(C_FFN1, inp["ffn1_norm"])
    put(C_MIX, inp["mix_norm"])
    put(C_MEM, inp["mem_norm"])
    put(C_FFN2, inp["ffn2_norm"])
    put(C_FIN, inp["final_norm"][None, :])
    cw = inp["conv_w"][0]
    for ch in range(6):
        for tap in range(3):
            c[:, C_CONV + 3 * ch + tap] = cw[tap, ch * 128:(ch + 1) * 128]
    ps = inp["pool_scale"][0]
    for i in range(8):
        c[0:96, C_PSC + i] = ps[i * 96:(i + 1) * 96]
    c[:, C_MASK] = mask_val
    for g in range(4):
        w = 2 << g
        for t in range(16):
            c[:, C_CORR + 16 * g + t] = (float(w) / min(t + 1, w)) if corr_on else 1.0
    return c


def _prep_inputs(inputs):
    inp = {k: np.ascontiguousarray(np.asarray(v), dtype=np.float32) for k, v in inputs.items()}
    return inp


_WKEYS = ["ffn1_w_gu", "ffn2_w_gu", "ffn1_w_down", "ffn2_w_down", "w_kv", "w_out",
          "conv_w_in", "pool_w_in", "pool_w_group"]


def _core_x(inp, core):
    b, half = core // 2, core % 2
    x = inp["x"]
    xs = np.zeros((4096 + HALO, D), np.float32)
    if half == 0:
        xs[HALO:] = x[b, 0:4096]
    else:
        xs[:] = x[b, 4096 - HALO:8192]
    return xs


def kernel(**inputs):
    inp = _prep_inputs(inputs)
    NT = 4
    nc = _get_program(NT)
    ident = np.eye(128, dtype=np.float32)
    in_maps = []
    for core in range(NCORE):
        b, half = core // 2, core % 2
        m = {"x": _core_x(inp, core), "mem": inp["mem"][b], "ident": ident,
             "cst": _const_pack(inp, 0.0 if half == 0 else 1.0, half == 0)}
        for k in _WKEYS:
            m[k] = inp[k]
        in_maps.append(m)
    res = run_bass_kernel_spmd(nc, in_maps, core_ids=list(range(NCORE)))
    out = np.empty((4, 8192, D), np.float32)
    for core in range(NCORE):
        b, half = core // 2, core % 2
        out[b, half * 4096:(half + 1) * 4096] = res.results[core]["y"]
    return out
```

```python
import numpy as np
import concourse.bass as bass
import concourse.mybir as mybir
from concourse.bass_utils import run_bass_kernel_spmd

F32 = mybir.dt.float32
BF16 = mybir.dt.bfloat16
AF = mybir.ActivationFunctionType
ALU = mybir.AluOpType

D = 1024
DFF = 2816
NMEM = 256
MIXW = 768
EPS = 1e-6
OWN = 1024
HALO = 20
TT = OWN + HALO
NB = 3
BS = TT // NB
NCORE = 8
NSLOT = 7
SLOT_ELEMS = 2816
PADL = 16
PW = PADL + TT

C_FFN1 = 0
C_MIX = 16
C_MEM = 32
C_FFN2 = 48
C_FIN = 64
C_CONV = 72
C_PSC = 90
C_MASK = 98
C_CORR = 99
NCST = 164

ENGS = ("pe", "act", "dve", "pool", "sp")


class Buf:
    __slots__ = ("w", "r", "name")

    def __init__(self, name=""):
        self.w = {}
        self.r = {}
        self.name = name


class DmaSem:
    def __init__(self, key):
        self.key = key
        self.count = 0


class Sched:
    def __init__(self):
        self.ops = {e: [] for e in ENGS}
        self.cnt = {e: 0 for e in ENGS}
        self.waited = {e: {} for e in ENGS}

    def add(self, eng, fn, reads=(), writes=(), dma=None):
        deps = {}

        def dep(key, val, raw):
            if key == eng:
                if eng == "pe":
                    return
            if val > deps.get(key, 0):
                deps[key] = val

        for b in reads:
            for k, v in b.w.items():
                dep(k, v, True)
        for b in writes:
            for k, v in b.w.items():
                dep(k, v, False)
            for k, v in b.r.items():
                dep(k, v, False)
        wd = self.waited[eng]
        waits = []
        for k, v in deps.items():
            if wd.get(k, 0) < v:
                waits.append((k, v))
                wd[k] = v
        if dma is not None:
            dma.count += 16
            tok = (dma.key, dma.count)
            inc = (dma.key, 16)
        else:
            self.cnt[eng] += 1
            tok = (eng, self.cnt[eng])
            inc = (eng, 1)
        self.ops[eng].append((waits, fn, inc))
        for b in reads:
            if b.r.get(tok[0], 0) < tok[1]:
                b.r[tok[0]] = tok[1]
        for b in writes:
            if b.w.get(tok[0], 0) < tok[1]:
                b.w[tok[0]] = tok[1]
        return tok


def build_program(NT, nstages=6, final_norm=True):
    nc = bass.Bass("TRN2", target_bir_lowering=False)
    S = Sched()
    NROWS = NT * OWN + HALO

    def din(name, shape):
        return nc.dram_tensor(name, list(shape), F32, kind="ExternalInput").ap()

    x_d = din("x", (NROWS, D))
    mem_d = din("mem", (NMEM, D))
    ident_d = din("ident", (128, 128))
    cst_d = din("cst", (128, NCST))
    w_gu_d = [din("ffn1_w_gu", (2, D, 2 * DFF)), din("ffn2_w_gu", (2, D, 2 * DFF))]
    w_dn_d = [din("ffn1_w_down", (2, DFF, D)), din("ffn2_w_down", (2, DFF, D))]
    w_kv_d = din("w_kv", (2, D, 512))
    w_out_d = din("w_out", (2, D, D))
    conv_in_d = din("conv_w_in", (1, D, 2560))
    pool_in_d = din("pool_w_in", (1, D, D))
    pool_g_d = din("pool_w_group", (1, 4, 192, 192))
    y_d = nc.dram_tensor("y", [NT * OWN, D], F32, kind="ExternalOutput").ap()

    sb = nc.alloc_sbuf_tensor
    xT = sb("xT", [128, 8, TT], F32)
    hT = sb("hT", [128, 8, TT], BF16)
    A = sb("A", [128, 22 * TT], BF16)
    mixb = sb("mixb", [128, 8, TT], BF16)
    attb = sb("attb", [128, 2, TT], BF16)
    qb = sb("qb", [128, 2, TT], BF16)
    slots = [sb(f"ws{i}", [128, SLOT_ELEMS], BF16) for i in range(NSLOT)]
    ebuf = [sb(f"eb{i}", [128, 4, BS], BF16) for i in range(2)]
    rcb = [sb(f"rc{i}", [128, BS], F32) for i in range(2)]
    sqb = sb("sqb", [128, 8, BS], BF16)
    msb = [sb(f"ms{i}", [128, BS], F32) for i in range(2)]
    rsb = [sb(f"rs{i}", [128, BS], F32) for i in range(2)]
    tmpb = [sb(f"tmp{i}", [128, BS], F32) for i in range(3)]
    xio = [sb(f"xio{i}", [128, D], F32) for i in range(3)]
    xo = [sb(f"xo{i}", [128, D], F32) for i in range(2)]
    kT = sb("kT", [128, 2, 2, NMEM], BF16)
    vS = sb("vS", [128, 2, 2, 256], BF16)
    ident = sb("ident_s", [128, 128], F32)
    ones = sb("ones_s", [128, 128], BF16)
    cst = sb("cst_s", [128, NCST], F32)
    psum = [nc.alloc_psum_tensor(f"ps{i}", [128, 512], F32) for i in range(8)]

    def A_view(off_bf16, shape, dtype):
        n = int(np.prod(shape[1:]))
        if dtype == F32:
            ap = A[:, off_bf16:off_bf16 + 2 * n].bitcast(F32)
        else:
            ap = A[:, off_bf16:off_bf16 + n]
        if len(shape) == 3:
            ap = ap.rearrange("p (a b) -> p a b", a=shape[1])
        return ap

    aT = A_view(0, [128, 22, TT], BF16)
    yT = A_view(0, [128, 8, TT], F32)
    memT = A_view(0, [128, 8, NMEM], F32)
    ubuf = [A_view(i * 2 * (TT + 2), [128, TT + 2], F32) for i in range(2)]
    o1 = 2 * 2 * (TT + 2)
    ybuf = [A_view(o1 + i * 2 * TT, [128, TT], F32) for i in range(2)]
    pbuf = [A_view(i * 2 * PW, [128, PW], F32) for i in range(4)]
    sbufs = [A_view((4 + i) * 2 * PW, [128, PW], F32) for i in range(2)]
    o2 = 6 * 2 * PW
    dbuf = A_view(o2, [128, 8, TT], BF16)

    class NS:
        pass

    B = NS()
    B.x = [[Buf() for _ in range(NB)] for _ in range(8)]
    B.h = [[Buf() for _ in range(NB)] for _ in range(8)]
    B.a = [[Buf() for _ in range(NB)] for _ in range(22)]
    B.A = Buf()
    B.mix = [[Buf() for _ in range(NB)] for _ in range(8)]
    B.att = [[Buf() for _ in range(NB)] for _ in range(2)]
    B.q = [[Buf() for _ in range(NB)] for _ in range(2)]
    B.slot = [Buf() for _ in range(NSLOT)]
    B.e = [[Buf() for _ in range(4)] for _ in range(2)]
    B.rc = [Buf() for _ in range(2)]
    B.sq = [Buf() for _ in range(2)]
    B.ms = [Buf() for _ in range(2)]
    B.rs = [Buf() for _ in range(2)]
    B.tmp = [Buf() for _ in range(3)]
    B.xio = [Buf() for _ in range(3)]
    B.xo = [[Buf() for _ in range(2)] for _ in range(2)]
    B.kv = Buf()
    B.const = Buf()
    B.ps = [Buf() for _ in range(8)]
    B.u = [[Buf() for _ in range(NB)] for _ in range(2)]
    B.upad = [Buf() for _ in range(2)]
    B.y = [Buf() for _ in range(2)]
    B.p = [[Buf() for _ in range(NB)] for _ in range(4)]
    B.ppad = [Buf() for _ in range(4)]
    B.s = [Buf() for _ in range(2)]
    B.d = [Buf() for _ in range(8)]
    B.yT = [[Buf() for _ in range(NB)] for _ in range(8)]
    B.memT = Buf()

    def hB(tb):
        return [B.h[c][tb] for c in range(8)]

    sems = {}

    def new_dma_sem(name):
        sems[name] = None
        return DmaSem(name)

    slot_sem = [new_dma_sem(f"d_slot{i}") for i in range(NSLOT)]
    xio_sem = [new_dma_sem(f"d_xio{i}") for i in range(3)]
    const_sem = new_dma_sem("d_const")
    xo_sem = [new_dma_sem(f"d_xo{i}") for i in range(2)]
    out_sems = xo_sem

    state = {"ps": 0, "tmp": 0, "nrm": 0, "xio": 0, "xo": 0, "e": 0}

    def ps_next():
        i = state["ps"]
        state["ps"] = (i + 1) % 8
        return psum[i], B.ps[i]

    def blk(tb):
        return slice(tb * BS, (tb + 1) * BS)

    units = []
    wstate = {"emitted": 0, "next": 0}

    def sview(slot, off, P, shape):
        n = int(np.prod(shape))
        ap = slot[0:P, off:off + n]
        if len(shape) == 2:
            ap = ap.rearrange("p (a b) -> p a b", a=shape[0])
        elif len(shape) == 3:
            ap = ap.rearrange("p (a b c) -> p a b c", a=shape[0], b=shape[1])
        return ap

    def kview(dram2d, r0, nk, c0, ncol, P=128):
        return dram2d[r0:r0 + nk * P, c0:c0 + ncol].rearrange("(k p) n -> p k n", p=P)

    def emit_unit_dma(idx):
        name, parts = units[idx]
        s = idx % NSLOT
        for (dst_fn, src) in parts:
            dst = dst_fn(slots[s])

            def fn(e, dst=dst, src=src):
                return e.dma_start(out=dst, in_=src)

            S.add("pool", fn, reads=(), writes=(B.slot[s],), dma=slot_sem[s])

    def wnext(name):
        i = wstate["next"]
        assert units[i][0] == name, (units[i][0], name)
        lim = min(len(units), i + NSLOT - 1)
        while wstate["emitted"] < lim:
            emit_unit_dma(wstate["emitted"])
            wstate["emitted"] += 1
        wstate["next"] = i + 1
        s = i % NSLOT
        return slots[s], B.slot[s]

    def U(name, parts):
        units.append((name, parts))

    def list_ffn_units(l, which):
        gu = w_gu_d[which][l]
        dn = w_dn_d[which][l]
        for fp in range(11):
            U(f"G{l}{which}_{fp}", [(lambda s: sview(s, 0, 128, [8, 256]), kview(gu, 0, 8, 256 * fp, 256))])
            U(f"U{l}{which}_{fp}", [(lambda s: sview(s, 0, 128, [8, 256]), kview(gu, 0, 8, DFF + 256 * fp, 256))])
        for op in range(4):
            for kh in range(2):
                U(f"D{l}{which}_{op}_{kh}",
                  [(lambda s: sview(s, 0, 128, [11, 256]), kview(dn, kh * 1408, 11, 256 * op, 256))])

    def list_mix_units(l):
        if l == 0:
            w = conv_in_d[0]
            for c in range(6):
                U(f"GCV_{c}", [
                    (lambda s: sview(s, 0, 128, [8, 256])[:, :, 0:128], kview(w, 0, 8, MIXW + 128 * c, 128)),
                    (lambda s: sview(s, 0, 128, [8, 256])[:, :, 128:256], kview(w, 0, 8, 2 * MIXW + 128 * c, 128)),
                ])
                if c % 2 == 1:
                    U(f"GB_{c // 2}", [(lambda s: sview(s, 0, 128, [8, 256]), kview(w, 0, 8, 256 * (c // 2), 256))])
            U("Q0", [(lambda s: sview(s, 0, 128, [8, 256]), kview(w, 0, 8, 3 * MIXW, 256))])
            wo = w_out_d[0]
            for op in range(4):
                U(f"WO0_{op}", [(lambda s: sview(s, 0, 128, [8, 256]), kview(wo, 0, 8, 256 * op, 256))])
        else:
            w = pool_in_d[0]
            for g in range(4):
                U(f"PW_{g}", [(lambda s: sview(s, 0, 128, [8, 192]), kview(w, 0, 8, 192 * g, 192))])
            U("Q1", [(lambda s: sview(s, 0, 128, [8, 256]), kview(w, 0, 8, MIXW, 256))])
            pg = pool_g_d[0].rearrange("g (kc p) n -> p g kc n", p=96)
            U("PG", [(lambda s: sview(s, 0, 96, [4, 2, 192]), pg)])
            wo = w_out_d[1]
            for op in range(4):
                U(f"WO1_{op}", [
                    (lambda s: sview(s, 0, 96, [8, 256]), kview(wo, 0, 8, 256 * op, 256, P=96)),
                    (lambda s: sview(s, 2048, 128, [2, 256]), kview(wo, MIXW, 2, 256 * op, 256)),
                ])

    for l in range(2):
        U(f"KVK{l}", [(lambda s: sview(s, 0, 128, [8, 256]), kview(w_kv_d[l], 0, 8, 0, 256))])
        U(f"KVV{l}", [(lambda s: sview(s, 0, 128, [8, 256]), kview(w_kv_d[l], 0, 8, 256, 256))])
    stage_list = []
    for l in range(2):
        stage_list += [("ffn", l, 0), ("mix", l, 0), ("ffn", l, 1)]
    stage_list = stage_list[:nstages]
    for t in range(NT):
        for (kind, l, which) in stage_list:
            if kind == "ffn":
                list_ffn_units(l, which)
            else:
                list_mix_units(l)

    def mm_group(out_ap, pairs, reads, psB):
        n = len(pairs)

        def fn(e):
            ins = None
            for i, (l, r) in enumerate(pairs):
                ins = e.matmul(out_ap, l, r, start=(i == 0), stop=(i == n - 1))
            return ins

        return S.add("pe", fn, reads=reads, writes=(psB,))

    def act(out, in_, func, reads, writes, scale=None, bias=None):
        kw = {}
        if scale is not None:
            kw["scale"] = scale
        if bias is not None:
            kw["bias"] = bias

        def fn(e):
            return e.activation(out=out, in_=in_, func=func, **kw)

        return S.add("act", fn, reads=reads, writes=writes)

    def dve_tt(out, in0, in1, op, reads, writes, eng="dve"):
        def fn(e):
            return e.tensor_tensor(out=out, in0=in0, in1=in1, op=op)

        return S.add(eng, fn, reads=reads, writes=writes)

    def dve_ts(out, in0, s1, s2, op0, op1, reads, writes, eng="dve"):
        def fn(e):
            if op1 is None:
                return e.tensor_scalar(out=out, in0=in0, scalar1=s1, scalar2=None, op0=op0)
            return e.tensor_scalar(out=out, in0=in0, scalar1=s1, scalar2=s2, op0=op0, op1=op1)

        return S.add(eng, fn, reads=reads, writes=writes)

    def dve_stt(out, in0, scalar, in1, op0, op1, reads, writes):
        def fn(e):
            return e.scalar_tensor_tensor(out=out, in0=in0, scalar=scalar, in1=in1, op0=op0, op1=op1)

        return S.add("dve", fn, reads=reads, writes=writes)

    def memset(eng, ap, val, writes, reads=()):
        def fn(e):
            return e.memset(ap, val)

        return S.add(eng, fn, reads=reads, writes=writes)

    def norm_block(src_fn, sbufs_r, sl, gcol, dst_fn, dst_bufs_fn, extra=()):
        n = sl.stop - sl.start
        i2 = state["nrm"] % 2
        state["nrm"] += 1
        src = src_fn(sl)
        for hf in range(2):
            act(sqb[:, 4 * hf:4 * hf + 4, 0:n], src[:, 4 * hf:4 * hf + 4, :], AF.Square,
                reads=sbufs_r[4 * hf:4 * hf + 4], writes=(B.sq[hf],))
        ps, psB = ps_next()
        mm_group(ps[:, 0:n], [(ones[:, :], sqb[:, c, 0:n]) for c in range(8)],
                 reads=(B.sq[0], B.sq[1], B.const), psB=psB)
        act(msb[i2][:, 0:n], ps[:, 0:n], AF.Ln, reads=(psB,), writes=(B.ms[i2],), scale=1.0 / D, bias=EPS)
        act(rsb[i2][:, 0:n], msb[i2][:, 0:n], AF.Exp, reads=(B.ms[i2],), writes=(B.rs[i2],), scale=-0.5)
        for c in range(8):
            dve_stt(dst_fn(c, sl), src[:, c, :], cst[:, gcol + c:gcol + c + 1], rsb[i2][:, 0:n],
                    ALU.mult, ALU.mult,
                    reads=(sbufs_r[c], B.rs[i2], B.const) + tuple(extra), writes=dst_bufs_fn(c))

    tile_blocks = [blk(tb) for tb in range(NB)]

    def fence_A():
        for e_ in ("pe", "act", "dve", "pool"):
            if S.cnt[e_]:
                B.A.w[e_] = S.cnt[e_]

    pend = {"gcol": None, "dst": "h", "done": set()}

    def set_pend(gcol, dst="h"):
        pend["gcol"] = gcol
        pend["dst"] = dst
        pend["done"] = set()

    def need_h(tb):
        if pend["gcol"] is None or tb in pend["done"]:
            return
        pend["done"].add(tb)
        xb = [B.x[c][tb] for c in range(8)]
        if pend["dst"] == "h":
            norm_block(lambda sl: xT[:, :, sl], xb, blk(tb), pend["gcol"],
                       lambda c, sl: hT[:, c, sl], lambda c: (B.h[c][tb],))
        else:
            fence_A()
            norm_block(lambda sl: xT[:, :, sl], xb, blk(tb), pend["gcol"],
                       lambda c, sl: yT[:, c, sl], lambda c: (B.yT[c][tb],), extra=(B.A,))

    def ffn(l, which, nxt):
        fence_A()
        for fp in range(11):
            gs, gB = wnext(f"G{l}{which}_{fp}")
            us, uB = wnext(f"U{l}{which}_{fp}")
            gv = sview(gs, 0, 128, [8, 256])
            uv = sview(us, 0, 128, [8, 256])
            for tb in range(NB):
                for f in (2 * fp, 2 * fp + 1):
                    cs = slice((f % 2) * 128, (f % 2) * 128 + 128)
                    sl = blk(tb)
                    need_h(tb)
                    pg, pgB = ps_next()
                    pu, puB = ps_next()
                    mm_group(pg[:, 0:BS], [(gv[:, k, cs], hT[:, k, sl]) for k in range(8)],
                             reads=[gB] + hB(tb), psB=pgB)
                    mm_group(pu[:, 0:BS], [(uv[:, k, cs], hT[:, k, sl]) for k in range(8)],
                             reads=[uB] + hB(tb), psB=puB)
                    ti = state["tmp"] % 3
                    state["tmp"] += 1
                    act(tmpb[ti][:, :], pg[:, 0:BS], AF.Silu, reads=(pgB,), writes=(B.tmp[ti],))
                    dve_tt(aT[:, f, sl], tmpb[ti][:, :], pu[:, 0:BS], ALU.mult,
                           reads=(B.tmp[ti], puB, B.A), writes=(B.a[f][tb],))
                    if fp == 0 and tb == 0 and f == 0:
                        need_h(NB - 1)
        for op in range(4):
            d0, d0B = wnext(f"D{l}{which}_{op}_0")
            d1, d1B = wnext(f"D{l}{which}_{op}_1")
            dv = [sview(d0, 0, 128, [11, 256]), sview(d1, 0, 128, [11, 256])]
            order = ([(o, tb) for o in (2 * op, 2 * op + 1) for tb in range(NB)] if op < 3 else
                     [(o, tb) for tb in range(NB) for o in (2 * op, 2 * op + 1)])
            for (o, tb) in order:
                cs = slice((o % 2) * 128, (o % 2) * 128 + 128)
                sl = blk(tb)
                ps, psB = ps_next()
                mm_group(ps[:, 0:BS], [(dv[k // 11][:, k % 11, cs], aT[:, k, sl]) for k in range(22)],
                         reads=[d0B, d1B] + [B.a[k][tb] for k in range(22)], psB=psB)
                dve_stt(xT[:, o, sl], ps[:, 0:BS], 0.5, xT[:, o, sl], ALU.mult, ALU.add,
                        reads=(psB, B.x[o][tb]), writes=(B.x[o][tb],))
                if o == 7:
                    if tb == 0:
                        set_pend(*nxt)
                    else:
                        need_h(tb - 1)
        need_h(NB - 2)

    def q_proj(name, ws, wB):
        wv = sview(ws, 0, 128, [8, 256])
        for j in range(2):
            cs = slice(j * 128, j * 128 + 128)
            for tb in range(NB):
                sl = blk(tb)
                ps, psB = ps_next()
                mm_group(ps[:, 0:BS], [(wv[:, k, cs], hT[:, k, sl]) for k in range(8)],
                         reads=[wB] + hB(tb), psB=psB)
                act(qb[:, j, sl], ps[:, 0:BS], AF.Copy, reads=(psB,), writes=(B.q[j][tb],))

    def attention(l):
        steps = [(tb, j) for tb in range(NB) for j in range(2)]
        eis = {}

        def scores(tb, j):
            sl = blk(tb)
            ei = state["e"] % 2
            state["e"] += 1
            eis[(tb, j)] = ei
            for hh in range(2):
                r = slice(64 * hh, 64 * hh + 64)
                for mc in range(2):
                    ps, psB = ps_next()
                    mm_group(ps[:, 0:BS], [(kT[r, l, j, mc * 128:(mc + 1) * 128], qb[r, j, sl])],
                             reads=(B.kv, B.q[j][tb]), psB=psB)
                    act(ebuf[ei][:, hh * 2 + mc, :], ps[:, 0:BS], AF.Exp, reads=(psB,),
                        writes=(B.e[ei][hh * 2 + mc],), scale=0.125)

        def pv(tb, j):
            sl = blk(tb)
            ei = eis[(tb, j)]
            ppv, ppvB = ps_next()
            psm, psmB = ps_next()
            for hh in range(2):
                r = slice(64 * hh, 64 * hh + 64)
                h = 2 * j + hh
                mm_group(ppv[r, 0:BS],
                         [(vS[:, l, mc, h * 64:(h + 1) * 64], ebuf[ei][:, hh * 2 + mc, :]) for mc in range(2)],
                         reads=[B.kv] + B.e[ei][2 * hh:2 * hh + 2], psB=ppvB)
                mm_group(psm[r, 0:BS],
                         [(ones[:, 0:64], ebuf[ei][:, hh * 2 + mc, :]) for mc in range(2)],
                         reads=[B.const] + B.e[ei][2 * hh:2 * hh + 2], psB=psmB)
            act(rcb[ei][:, :], psm[:, 0:BS], AF.Ln, reads=(psmB,), writes=(B.rc[ei],))
            act(rcb[ei][:, :], rcb[ei][:, :], AF.Exp, reads=(B.rc[ei],), writes=(B.rc[ei],), scale=-1.0)
            dve_tt(attb[:, j, sl], ppv[:, 0:BS], rcb[ei][:, :], ALU.mult,
                   reads=(ppvB, B.rc[ei]), writes=(B.att[j][tb],))

        scores(*steps[0])
        for i, st in enumerate(steps):
            if i + 1 < len(steps):
                scores(*steps[i + 1])
            pv(*st)

    def w_out_stage(l, nxt):
        for op in range(4):
            ws, wB = wnext(f"WO{l}_{op}")
            if l == 0:
                wv = sview(ws, 0, 128, [8, 256])
            else:
                wa = sview(ws, 0, 96, [8, 256])
                wb_ = sview(ws, 2048, 128, [2, 256])
            order = ([(o, tb) for o in (2 * op, 2 * op + 1) for tb in range(NB)] if op < 3 else
                     [(o, tb) for tb in range(NB) for o in (2 * op, 2 * op + 1)])
            for (o, tb) in order:
                cs = slice((o % 2) * 128, (o % 2) * 128 + 128)
                if True:
                    sl = blk(tb)
                    ps, psB = ps_next()
                    if l == 0:
                        pairs = [(wv[:, k, cs], mixb[:, k, sl]) for k in range(6)]
                        pairs += [(wv[:, 6 + j, cs], attb[:, j, sl]) for j in range(2)]
                        rd = [wB] + [B.mix[k][tb] for k in range(6)] + [B.att[j][tb] for j in range(2)]
                    else:
                        pairs = [(wa[:, k, cs], mixb[0:96, k, sl]) for k in range(8)]
                        pairs += [(wb_[:, j, cs], attb[:, j, sl]) for j in range(2)]
                        rd = [wB] + [B.mix[k][tb] for k in range(8)] + [B.att[j][tb] for j in range(2)]
                    mm_group(ps[:, 0:BS], pairs, reads=rd, psB=psB)
                    dve_tt(xT[:, o, sl], ps[:, 0:BS], xT[:, o, sl], ALU.add,
                           reads=(psB, B.x[o][tb]), writes=(B.x[o][tb],))
                    if o == 7:
                        if tb == 0:
                            set_pend(*nxt)
                        else:
                            need_h(tb - 1)
        need_h(NB - 2)

    def mixer_conv(first_tile, nxt):
        fence_A()
        for i in range(2):
            memset("dve", ubuf[i][:, 0:2], 0.0, (B.upad[i],), reads=(B.A,))
        for c in range(6):
            ws, wB = wnext(f"GCV_{c}")
            wv = sview(ws, 0, 128, [8, 256])
            ui = c % 2
            for tb in range(NB):
                sl = blk(tb)
                need_h(tb)
                p1, p1B = ps_next()
                p2, p2B = ps_next()
                mm_group(p1[:, 0:BS], [(wv[:, k, 0:128], hT[:, k, sl]) for k in range(8)],
                         reads=[wB] + hB(tb), psB=p1B)
                mm_group(p2[:, 0:BS], [(wv[:, k, 128:256], hT[:, k, sl]) for k in range(8)],
                         reads=[wB] + hB(tb), psB=p2B)
                ti = state["tmp"] % 3
                state["tmp"] += 1
                act(tmpb[ti][:, :], p2[:, 0:BS], AF.Copy, reads=(p2B,), writes=(B.tmp[ti],))
                dve_tt(ubuf[ui][:, 2 + tb * BS:2 + (tb + 1) * BS], p1[:, 0:BS], tmpb[ti][:, :], ALU.mult,
                       reads=(p1B, B.tmp[ti], B.A), writes=(B.u[ui][tb],))
                if c == 0 and tb == 0:
                    need_h(NB - 1)
            if first_tile:
                dve_ts(ubuf[ui][:, 2:2 + HALO], ubuf[ui][:, 2:2 + HALO], cst[:, C_MASK:C_MASK + 1], None,
                       ALU.mult, None, reads=(B.u[ui][0], B.const), writes=(B.u[ui][0],))
            cw = lambda tap: cst[:, C_CONV + 3 * c + tap:C_CONV + 3 * c + tap + 1]
            dve_ts(ybuf[ui][:, :], ubuf[ui][:, 2:2 + TT], cw(2), None, ALU.mult, None,
                   reads=B.u[ui] + [B.upad[ui], B.const, B.A], writes=(B.y[ui],))
            dve_stt(ybuf[ui][:, :], ubuf[ui][:, 1:1 + TT], cw(1), ybuf[ui][:, :], ALU.mult, ALU.add,
                    reads=B.u[ui] + [B.upad[ui], B.y[ui], B.const], writes=(B.y[ui],))
            dve_stt(ybuf[ui][:, :], ubuf[ui][:, 0:TT], cw(0), ybuf[ui][:, :], ALU.mult, ALU.add,
                    reads=B.u[ui] + [B.upad[ui], B.y[ui], B.const], writes=(B.y[ui],))
            if c % 2 == 1:
                ws2, wB2 = wnext(f"GB_{c // 2}")
                wv2 = sview(ws2, 0, 128, [8, 256])
                for cc in (c - 1, c):
                    cs = slice((cc % 2) * 128, (cc % 2) * 128 + 128)
                    for tb in range(NB):
                        sl = blk(tb)
                        ps, psB = ps_next()
                        mm_group(ps[:, 0:BS], [(wv2[:, k, cs], hT[:, k, sl]) for k in range(8)],
                                 reads=[wB2] + hB(tb), psB=psB)
                        dve_tt(mixb[:, cc, sl], ps[:, 0:BS], ybuf[cc % 2][:, sl], ALU.mult,
                               reads=(psB, B.y[cc % 2]), writes=(B.mix[cc][tb],))
        ws, wB = wnext("Q0")
        q_proj("Q0", ws, wB)
        attention(0)
        w_out_stage(0, nxt)

    def mixer_pool(first_tile, nxt):
        fence_A()
        for i in range(4):
            memset("dve", pbuf[i][:, 0:PADL], 0.0, (B.ppad[i],), reads=(B.A,))
        for gp in range(2):
            wvs = []
            for g in (2 * gp, 2 * gp + 1):
                ws, wB = wnext(f"PW_{g}")
                wvs.append((sview(ws, 0, 128, [8, 192]), wB))
            chunks = [4 * gp + ii for ii in range(4)]
            for tb in range(NB):
                sl = blk(tb)
                need_h(tb)
                for i in chunks:
                    wv, wB = wvs[(i // 2) % 2]
                    pi = i % 4
                    cs = slice((i % 2) * 96, (i % 2) * 96 + 96)
                    ps, psB = ps_next()
                    mm_group(ps[0:96, 0:BS], [(wv[:, k, cs], hT[:, k, sl]) for k in range(8)],
                             reads=[wB] + hB(tb), psB=psB)
                    act(pbuf[pi][0:96, PADL + tb * BS:PADL + (tb + 1) * BS], ps[0:96, 0:BS], AF.Copy,
                        reads=(psB, B.A), writes=(B.p[pi][tb],))
                    if gp == 0 and tb == 0 and i == chunks[1]:
                        need_h(NB - 1)
            for i in chunks:
                g = i // 2
                win = 2 << g
                pi = i % 4
                P = pbuf[pi]
                if first_tile:
                    dve_ts(P[0:96, PADL:PADL + HALO], P[0:96, PADL:PADL + HALO], cst[0:96, C_MASK:C_MASK + 1],
                           None, ALU.mult, None, reads=(B.p[pi][0], B.const), writes=(B.p[pi][0],))
                pB = B.p[pi] + [B.ppad[pi]]
                cur, curB = P, pB
                sh = 1
                si = 0
                while sh < win:
                    lo = 2 * sh - 1
                    dst, dstB = sbufs[si], B.s[si]
                    dve_tt(dst[0:96, lo:PW], cur[0:96, lo:PW], cur[0:96, lo - sh:PW - sh], ALU.add,
                           reads=list(curB) + [B.A], writes=(dstB,))
                    cur, curB = dst, [dstB]
                    si ^= 1
                    sh *= 2
                if first_tile:
                    c0 = PADL + HALO
                    dve_tt(cur[0:96, c0:c0 + 16], cur[0:96, c0:c0 + 16],
                           cst[0:96, C_CORR + 16 * g:C_CORR + 16 * g + 16], ALU.mult,
                           reads=list(curB) + [B.const], writes=curB)
                dve_stt(dbuf[0:96, i, :], cur[0:96, PADL:PW], 1.0 / win, P[0:96, PADL:PW], ALU.mult, ALU.subtract,
                        reads=list(curB) + pB + [B.A], writes=(B.d[i],))
        ws, wB = wnext("Q1")
        q_proj("Q1", ws, wB)
        attention(1)
        ws, wB = wnext("PG")
        pgv = sview(ws, 0, 96, [4, 2, 192])
        for g in range(4):
            for oc in range(2):
                i = 2 * g + oc
                for tb in range(NB):
                    sl = blk(tb)
                    ps, psB = ps_next()
                    mm_group(ps[0:96, 0:BS],
                             [(pgv[:, g, kc, oc * 96:(oc + 1) * 96], dbuf[0:96, 2 * g + kc, sl]) for kc in range(2)],
                             reads=(wB, B.d[2 * g], B.d[2 * g + 1]), psB=psB)
                    act(mixb[0:96, i, sl], ps[0:96, 0:BS], AF.Identity, reads=(psB, B.const),
                        writes=(B.mix[i][tb],), scale=cst[0:96, C_PSC + i:C_PSC + i + 1])
        w_out_stage(1, nxt)

    def dma_sp(out, in_, reads, writes, sem):
        def fn(e):
            return e.dma_start(out=out, in_=in_)

        return S.add("sp", fn, reads=reads, writes=writes, dma=sem)

    dma_sp(cst[:, :], cst_d[:, :], (), (B.const,), const_sem)
    dma_sp(ident[:, :], ident_d[:, :], (), (B.const,), const_sem)
    memset("dve", ones[:, :], 1.0, (B.const,))

    def load_transpose(src_d, row0, nrows, dst_fn, dst_bufs_fn, extra=()):
        nblk = (nrows + 127) // 128
        for b_ in range(nblk):
            rows = min(128, nrows - b_ * 128)
            xi = state["xio"] % 3
            state["xio"] += 1
            dma_sp(xio[xi][0:rows, :], src_d[row0 + b_ * 128:row0 + b_ * 128 + rows, :], (), (B.xio[xi],),
                   xio_sem[xi])
            for half in range(2):
                ps, psB = ps_next()

                def fn(e, ps=ps, xi=xi, rows=rows, half=half):
                    ins = None
                    for cc in range(4):
                        c = half * 4 + cc
                        ins = e.transpose(ps[:, cc * 128:cc * 128 + rows], xio[xi][0:rows, c * 128:(c + 1) * 128],
                                          ident[0:rows, 0:rows])
                    return ins

                S.add("pe", fn, reads=(B.xio[xi], B.const), writes=(psB,))
                src = ps[:, :].rearrange("p (a b) -> p a b", a=4)[:, :, 0:rows]
                dst = dst_fn(half * 4, b_ * 128, rows)
                act(dst, src, AF.Copy, reads=(psB,) + tuple(extra), writes=dst_bufs_fn(half * 4, b_ * 128, rows))

    load_transpose(mem_d, 0, NMEM, lambda c0, t0, n: memT[:, c0:c0 + 4, t0:t0 + n],
                   lambda c0, t0, n: (B.memT,), extra=(B.A,))
    for l in range(2):
        norm_block(lambda sl: memT[:, :, sl], [B.memT] * 8, slice(0, NMEM), C_MEM + 8 * l,
                   lambda c, sl: hT[:, c, sl], lambda c: (B.h[c][0],))
        ws, wB = wnext(f"KVK{l}")
        wv = sview(ws, 0, 128, [8, 256])
        for j in range(2):
            ps, psB = ps_next()
            mm_group(ps[:, 0:NMEM], [(wv[:, k, j * 128:(j + 1) * 128], hT[:, k, 0:NMEM]) for k in range(8)],
                     reads=[wB] + hB(0), psB=psB)
            act(kT[:, l, j, :], ps[:, 0:NMEM], AF.Copy, reads=(psB,), writes=(B.kv,))
        ws, wB = wnext(f"KVV{l}")
        wv = sview(ws, 0, 128, [8, 256])
        for mc in range(2):
            ps, psB = ps_next()
            mm_group(ps[:, 0:256], [(hT[:, k, mc * 128:(mc + 1) * 128], wv[:, k, :]) for k in range(8)],
                     reads=[wB] + hB(0), psB=psB)
            act(vS[:, l, mc, :], ps[:, 0:256], AF.Copy, reads=(psB,), writes=(B.kv,))

    def xbufs(c0, t0, n):
        tbs = sorted(set([t0 // BS, (t0 + n - 1) // BS]))
        return [B.x[c][tb] for c in range(c0, c0 + 4) for tb in tbs]

    def stage_gcol(kind, l, which):
        if kind == "ffn":
            return (C_FFN1 if which == 0 else C_FFN2) + 8 * l
        return C_MIX + 8 * l

    NIB = (TT + 127) // 128
    NOB = OWN // 128
    in_slot = {}

    def in_load(t, b_):
        rows = min(128, TT - b_ * 128)
        xi = state["xio"] % 3
        state["xio"] += 1
        in_slot[(t, b_)] = xi
        r0 = t * OWN + b_ * 128
        dma_sp(xio[xi][0:rows, :], x_d[r0:r0 + rows, :], (), (B.xio[xi],), xio_sem[xi])

    def in_xpose(t, b_):
        rows = min(128, TT - b_ * 128)
        xi = in_slot[(t, b_)]
        for half in range(2):
            ps, psB = ps_next()

            def fn(e, ps=ps, xi=xi, rows=rows, half=half):
                ins = None
                for cc in range(4):
                    c = half * 4 + cc
                    ins = e.transpose(ps[:, cc * 128:cc * 128 + rows], xio[xi][0:rows, c * 128:(c + 1) * 128],
                                      ident[0:rows, 0:rows])
                return ins

            S.add("pe", fn, reads=(B.xio[xi], B.const), writes=(psB,))
            src = ps[:, :].rearrange("p (a b) -> p a b", a=4)[:, :, 0:rows]
            act(xT[:, half * 4:half * 4 + 4, b_ * 128:b_ * 128 + rows], src, AF.Copy, reads=(psB,),
                writes=xbufs(half * 4, b_ * 128, rows))

    def out_block(t, ob, srcT, srcB):
        col0 = HALO + ob * 128
        tbs = sorted(set([col0 // BS, (col0 + 127) // BS]))
        xi = state["xo"] % 2
        state["xo"] += 1
        for half in range(2):
            ps, psB = ps_next()

            def fn(e, ps=ps, half=half, col0=col0, srcT=srcT):
                ins = None
                for cc in range(4):
                    c = half * 4 + cc
                    ins = e.transpose(ps[:, cc * 128:(cc + 1) * 128], srcT[:, c, col0:col0 + 128], ident[:, :])
                return ins

            rd = [srcB(c, tb) for c in range(half * 4, half * 4 + 4) for tb in tbs] + [B.const]
            S.add("pe", fn, reads=rd, writes=(psB,))
            act(xo[xi][:, half * 512:(half + 1) * 512], ps[:, :], AF.Copy, reads=(psB,), writes=(B.xo[xi][half],))

        def fn2(e, dst=y_d[t * OWN + ob * 128:t * OWN + (ob + 1) * 128, :], src=xo[xi][:, :]):
            return e.dma_start(out=dst, in_=src)

        S.add("act", fn2, reads=B.xo[xi], writes=(), dma=xo_sem[xi])

    if final_norm:
        srcT, srcB = yT, (lambda c, tb: B.yT[c][tb])
    else:
        srcT, srcB = xT, (lambda c, tb: B.x[c][tb])

    def first_pend():
        if stage_list:
            set_pend(stage_gcol(*stage_list[0]))
        elif final_norm:
            set_pend(C_FIN, "y")
        else:
            set_pend(None)

    for b_ in range(3):
        in_load(0, b_)
    for t in range(NT):
        seq = []
        outs = [("out", ob) for ob in range(NOB)] if t > 0 else []
        for b_ in range(NIB):
            seq.append(("in", b_))
            if b_ == 1:
                seq.append(("flush", 0))
            if b_ >= 1 and outs:
                seq.append(outs.pop(0))
            if b_ == 4:
                seq.append(("norm", 0))
            if b_ == 7:
                seq.append(("norm", 1))
        seq += outs
        for (kind, v) in seq:
            if kind == "in":
                in_xpose(t, v)
                if v + 3 < NIB:
                    in_load(t, v + 3)
                elif t + 1 < NT:
                    in_load(t + 1, v + 3 - NIB)
            elif kind == "flush":
                if t > 0 and final_norm:
                    for tb in range(NB):
                        need_h(tb)
                first_pend()
            elif kind == "out":
                out_block(t - 1, v, srcT, srcB)
            elif kind == "norm":
                if not (not stage_list and final_norm):
                    need_h(v)
        for si, (kind, l, which) in enumerate(stage_list):
            if si + 1 < len(stage_list):
                nxt = (stage_gcol(*stage_list[si + 1]), "h")
            elif final_norm:
                nxt = (C_FIN, "y")
            else:
                nxt = (None, "h")
            if kind == "ffn":
                ffn(l, which, nxt)
            elif l == 0:
                mixer_conv(t == 0, nxt)
            else:
                mixer_pool(t == 0, nxt)
    if final_norm:
        for tb in range(NB):
            need_h(tb)
    for ob in range(NOB):
        out_block(NT - 1, ob, srcT, srcB)
    assert wstate["next"] == len(units), (wstate["next"], len(units))

    for k in list(sems.keys()):
        sems[k] = nc.alloc_semaphore(k)
    for e in ("pe", "act", "dve", "pool"):
        sems[e] = nc.alloc_semaphore("t_" + e)

    def replay(name, eng):
        for waits, fn, inc in S.ops[name]:
            for k, v in waits:
                eng.wait_ge(sems[k], v)
            ins = fn(eng)
            ins.then_inc(sems[inc[0]], inc[1])

    with nc.Block() as block:
        @block.tensor
        def _(e):
            replay("pe", e)

        @block.scalar
        def _(e):
            replay("act", e)

        @block.vector
        def _(e):
            replay("dve", e)

        @block.gpsimd
        def _(e):
            replay("pool", e)

        @block.sync
        def _(e):
            replay("sp", e)
            for ds in out_sems:
                if ds.count:
                    e.wait_ge(sems[ds.key], ds.count)

    stats = {k: len(v) for k, v in S.ops.items()}
    return nc, stats


_CACHE = {}


def _get_program(NT, nstages=6, final_norm=True):
    key = (NT, nstages, final_norm)
    if key not in _CACHE:
        _CACHE[key] = build_program(NT, nstages, final_norm)[0]
    return _CACHE[key]


def _const_pack(inp, mask_val, corr_on):
    c = np.zeros((128, NCST), np.float32)

    def put(col, vec2d):
        for i in range(vec2d.shape[0]):
            c[:, col + 8 * i:col + 8 * i + 8] = vec2d[i].reshape(8, 128).T

    put(C_FFN1, inp["ffn1_norm"])
    put(C_MIX, inp["mix_norm"])
    put(C_MEM, inp["mem_norm"])
    put(C_FFN2, inp["ffn2_norm"])
    put(C_FIN, inp["final_norm"][None, :])
    cw = inp["conv_w"][0]
    for ch in range(6):
        for tap in range(3):
            c[:, C_CONV + 3 * ch + tap] = cw[tap, ch * 128:(ch + 1) * 128]
    ps = inp["pool_scale"][0]
    for i in range(8):
        c[0:96, C_PSC + i] = ps[i * 96:(i + 1) * 96]
    c[:, C_MASK] = mask_val
    for g in range(4):
        w = 2 << g
        for t in range(16):
            c[:, C_CORR + 16 * g + t] = (float(w) / min(t + 1, w)) if corr_on else 1.0
    return c


def _prep_inputs(inputs):
    inp = {k: np.ascontiguousarray(np.asarray(v), dtype=np.float32) for k, v in inputs.items()}
    return inp


_WKEYS = ["ffn1_w_gu", "ffn2_w_gu", "ffn1_w_down", "ffn2_w_down", "w_kv", "w_out",
          "conv_w_in", "pool_w_in", "pool_w_group"]


def _core_x(inp, core):
    b, half = core // 2, core % 2
    x = inp["x"]
    xs = np.zeros((4096 + HALO, D), np.float32)
    if half == 0:
        xs[HALO:] = x[b, 0:4096]
    else:
        xs[:] = x[b, 4096 - HALO:8192]
    return xs


def kernel(**inputs):
    inp = _prep_inputs(inputs)
    NT = 4
    nc = _get_program(NT)
    ident = np.eye(128, dtype=np.float32)
    in_maps = []
    for core in range(NCORE):
        b, half = core // 2, core % 2
        m = {"x": _core_x(inp, core), "mem": inp["mem"][b], "ident": ident,
             "cst": _const_pack(inp, 0.0 if half == 0 else 1.0, half == 0)}
        for k in _WKEYS:
            m[k] = inp[k]
        in_maps.append(m)
    res = run_bass_kernel_spmd(nc, in_maps, core_ids=list(range(NCORE)))
    out = np.empty((4, 8192, D), np.float32)
    for core in range(NCORE):
        b, half = core // 2, core % 2
        out[b, half * 4096:(half + 1) * 4096] = res.results[core]["y"]
    return out
```

```python
import numpy as np
import concourse.bass as bass
import concourse.mybir as mybir
from concourse.bass_utils import run_bass_kernel_spmd

F32 = mybir.dt.float32
BF16 = mybir.dt.bfloat16
AF = mybir.ActivationFunctionType
ALU = mybir.AluOpType

D = 1024
DFF = 2816
NMEM = 256
MIXW = 768
EPS = 1e-6
OWN = 1024
HALO = 20
TT = OWN + HALO
NB = 3
BS = TT // NB
NCORE = 8
NSLOT = 7
SLOT_ELEMS = 2816
PADL = 16
PW = PADL + TT

C_FFN1 = 0
C_MIX = 16
C_MEM = 32
C_FFN2 = 48
C_FIN = 64
C_CONV = 72
C_PSC = 90
C_MASK = 98
C_CORR = 99
NCST = 164

ENGS = ("pe", "act", "dve", "pool", "sp")


class Buf:
    __slots__ = ("w", "r", "name")

    def __init__(self, name=""):
        self.w = {}
        self.r = {}
        self.name = name


class DmaSem:
    def __init__(self, key):
        self.key = key
        self.count = 0


class Sched:
    def __init__(self):
        self.ops = {e: [] for e in ENGS}
        self.cnt = {e: 0 for e in ENGS}
        self.waited = {e: {} for e in ENGS}

    def add(self, eng, fn, reads=(), writes=(), dma=None):
        deps = {}

        def dep(key, val, raw):
            if key == eng:
                if eng == "pe":
                    return
            if val > deps.get(key, 0):
                deps[key] = val

        for b in reads:
            for k, v in b.w.items():
                dep(k, v, True)
        for b in writes:
            for k, v in b.w.items():
                dep(k, v, False)
            for k, v in b.r.items():
                dep(k, v, False)
        wd = self.waited[eng]
        waits = []
        for k, v in deps.items():
            if wd.get(k, 0) < v:
                waits.append((k, v))
                wd[k] = v
        if dma is not None:
            dma.count += 16
            tok = (dma.key, dma.count)
            inc = (dma.key, 16)
        else:
            self.cnt[eng] += 1
            tok = (eng, self.cnt[eng])
            inc = (eng, 1)
        self.ops[eng].append((waits, fn, inc))
        for b in reads:
            if b.r.get(tok[0], 0) < tok[1]:
                b.r[tok[0]] = tok[1]
        for b in writes:
            if b.w.get(tok[0], 0) < tok[1]:
                b.w[tok[0]] = tok[1]
        return tok


def build_program(NT, nstages=6, final_norm=True):
    nc = bass.Bass("TRN2", target_bir_lowering=False)
    S = Sched()
    NROWS = NT * OWN + HALO

    def din(name, shape):
        return nc.dram_tensor(name, list(shape), F32, kind="ExternalInput").ap()

    x_d = din("x", (NROWS, D))
    mem_d = din("mem", (NMEM, D))
    ident_d = din("ident", (128, 128))
    cst_d = din("cst", (128, NCST))
    w_gu_d = [din("ffn1_w_gu", (2, D, 2 * DFF)), din("ffn2_w_gu", (2, D, 2 * DFF))]
    w_dn_d = [din("ffn1_w_down", (2, DFF, D)), din("ffn2_w_down", (2, DFF, D))]
    w_kv_d = din("w_kv", (2, D, 512))
    w_out_d = din("w_out", (2, D, D))
    conv_in_d = din("conv_w_in", (1, D, 2560))
    pool_in_d = din("pool_w_in", (1, D, D))
    pool_g_d = din("pool_w_group", (1, 4, 192, 192))
    y_d = nc.dram_tensor("y", [NT * OWN, D], F32, kind="ExternalOutput").ap()

    sb = nc.alloc_sbuf_tensor
    xT = sb("xT", [128, 8, TT], F32)
    hT = sb("hT", [128, 8, TT], BF16)
    A = sb("A", [128, 22 * TT], BF16)
    mixb = sb("mixb", [128, 8, TT], BF16)
    attb = sb("attb", [128, 2, TT], BF16)
    qb = sb("qb", [128, 2, TT], BF16)
    slots = [sb(f"ws{i}", [128, SLOT_ELEMS], BF16) for i in range(NSLOT)]
    ebuf = [sb(f"eb{i}", [128, 4, BS], BF16) for i in range(2)]
    rcb = [sb(f"rc{i}", [128, BS], F32) for i in range(2)]
    sqb = sb("sqb", [128, 8, BS], BF16)
    msb = [sb(f"ms{i}", [128, BS], F32) for i in range(2)]
    rsb = [sb(f"rs{i}", [128, BS], F32) for i in range(2)]
    tmpb = [sb(f"tmp{i}", [128, BS], F32) for i in range(3)]
    xio = [sb(f"xio{i}", [128, D], F32) for i in range(3)]
    xo = [sb(f"xo{i}", [128, D], F32) for i in range(2)]
    kT = sb("kT", [128, 2, 2, NMEM], BF16)
    vS = sb("vS", [128, 2, 2, 256], BF16)
    ident = sb("ident_s", [128, 128], F32)
    ones = sb("ones_s", [128, 128], BF16)
    cst = sb("cst_s", [128, NCST], F32)
    psum = [nc.alloc_psum_tensor(f"ps{i}", [128, 512], F32) for i in range(8)]

    def A_view(off_bf16, shape, dtype):
        n = int(np.prod(shape[1:]))
        if dtype == F32:
            ap = A[:, off_bf16:off_bf16 + 2 * n].bitcast(F32)
        else:
            ap = A[:, off_bf16:off_bf16 + n]
        if len(shape) == 3:
            ap = ap.rearrange("p (a b) -> p a b", a=shape[1])
        return ap

    aT = A_view(0, [128, 22, TT], BF16)
    yT = A_view(0, [128, 8, TT], F32)
    memT = mixb[:, 0:4, :].rearrange("p a b -> p (a b)")[:, 0:2 * 8 * NMEM].bitcast(F32).rearrange(
        "p (a b) -> p a b", a=8)
    memh = attb[:, :, :].rearrange("p a b -> p (a b)")[:, 0:8 * NMEM].rearrange("p (a b) -> p a b", a=8)
    ubuf = [A_view(i * 2 * (TT + 2), [128, TT + 2], F32) for i in range(2)]
    o1 = 2 * 2 * (TT + 2)
    ybuf = [A_view(o1 + i * 2 * TT, [128, TT], F32) for i in range(2)]
    pbuf = [A_view(i * 2 * PW, [128, PW], F32) for i in range(4)]
    sbufs = [A_view((4 + i) * 2 * PW, [128, PW], F32) for i in range(2)]
    o2 = 6 * 2 * PW
    dbuf = A_view(o2, [128, 8, TT], BF16)

    class NS:
        pass

    B = NS()
    B.x = [[Buf() for _ in range(NB)] for _ in range(8)]
    B.h = [[Buf() for _ in range(NB)] for _ in range(8)]
    B.a = [[Buf() for _ in range(NB)] for _ in range(22)]
    B.A = Buf()
    B.mix = [[Buf() for _ in range(NB)] for _ in range(8)]
    B.att = [[Buf() for _ in range(NB)] for _ in range(2)]
    B.q = [[Buf() for _ in range(NB)] for _ in range(2)]
    B.slot = [Buf() for _ in range(NSLOT)]
    B.e = [[Buf() for _ in range(4)] for _ in range(2)]
    B.rc = [Buf() for _ in range(2)]
    B.sq = [Buf() for _ in range(2)]
    B.ms = [Buf() for _ in range(2)]
    B.rs = [Buf() for _ in range(2)]
    B.tmp = [Buf() for _ in range(3)]
    B.xio = [Buf() for _ in range(3)]
    B.xo = [[Buf() for _ in range(2)] for _ in range(2)]
    B.kv = Buf()
    B.const = Buf()
    B.ps = [Buf() for _ in range(8)]
    B.u = [[Buf() for _ in range(NB)] for _ in range(2)]
    B.upad = [Buf() for _ in range(2)]
    B.y = [Buf() for _ in range(2)]
    B.p = [[Buf() for _ in range(NB)] for _ in range(4)]
    B.ppad = [Buf() for _ in range(4)]
    B.s = [Buf() for _ in range(2)]
    B.d = [Buf() for _ in range(8)]
    B.yT = [[Buf() for _ in range(NB)] for _ in range(8)]
    B.memT = Buf()
    B.memh = [Buf() for _ in range(8)]

    def hB(tb):
        return [B.h[c][tb] for c in range(8)]

    sems = {}

    def new_dma_sem(name):
        sems[name] = None
        return DmaSem(name)

    slot_sem = [new_dma_sem(f"d_slot{i}") for i in range(NSLOT)]
    xio_sem = [new_dma_sem(f"d_xio{i}") for i in range(3)]
    const_sem = new_dma_sem("d_const")
    xo_sem = [new_dma_sem(f"d_xo{i}") for i in range(2)]
    out_sems = xo_sem

    state = {"ps": 0, "tmp": 0, "nrm": 0, "xio": 0, "xo": 0, "e": 0}

    def ps_next():
        i = state["ps"]
        state["ps"] = (i + 1) % 8
        return psum[i], B.ps[i]

    def blk(tb):
        return slice(tb * BS, (tb + 1) * BS)

    units = []
    wstate = {"emitted": 0, "next": 0}

    def sview(slot, off, P, shape):
        n = int(np.prod(shape))
        ap = slot[0:P, off:off + n]
        if len(shape) == 2:
            ap = ap.rearrange("p (a b) -> p a b", a=shape[0])
        elif len(shape) == 3:
            ap = ap.rearrange("p (a b c) -> p a b c", a=shape[0], b=shape[1])
        return ap

    def kview(dram2d, r0, nk, c0, ncol, P=128):
        return dram2d[r0:r0 + nk * P, c0:c0 + ncol].rearrange("(k p) n -> p k n", p=P)

    def emit_unit_dma(idx):
        name, parts = units[idx]
        s = idx % NSLOT
        for (dst_fn, src) in parts:
            dst = dst_fn(slots[s])

            def fn(e, dst=dst, src=src):
                return e.dma_start(out=dst, in_=src)

            S.add("pool", fn, reads=(), writes=(B.slot[s],), dma=slot_sem[s])

    def wnext(name):
        i = wstate["next"]
        assert units[i][0] == name, (units[i][0], name)
        lim = min(len(units), i + NSLOT - 1)
        while wstate["emitted"] < lim:
            emit_unit_dma(wstate["emitted"])
            wstate["emitted"] += 1
        wstate["next"] = i + 1
        s = i % NSLOT
        return slots[s], B.slot[s]

    def U(name, parts):
        units.append((name, parts))

    def list_ffn_units(l, which, kv=False):
        gu = w_gu_d[which][l]
        dn = w_dn_d[which][l]
        for fp in range(11):
            U(f"G{l}{which}_{fp}", [(lambda s: sview(s, 0, 128, [8, 256]), kview(gu, 0, 8, 256 * fp, 256))])
            U(f"U{l}{which}_{fp}", [(lambda s: sview(s, 0, 128, [8, 256]), kview(gu, 0, 8, DFF + 256 * fp, 256))])
            if kv and fp in KV_AFTER_FP:
                kv_unit(KV_AFTER_FP[fp])
        for op in range(4):
            for kh in range(2):
                U(f"D{l}{which}_{op}_{kh}",
                  [(lambda s: sview(s, 0, 128, [11, 256]), kview(dn, kh * 1408, 11, 256 * op, 256))])

    def list_mix_units(l):
        if l == 0:
            w = conv_in_d[0]
            for c in range(6):
                U(f"GCV_{c}", [
                    (lambda s: sview(s, 0, 128, [8, 256])[:, :, 0:128], kview(w, 0, 8, MIXW + 128 * c, 128)),
                    (lambda s: sview(s, 0, 128, [8, 256])[:, :, 128:256], kview(w, 0, 8, 2 * MIXW + 128 * c, 128)),
                ])
                if c % 2 == 1:
                    U(f"GB_{c // 2}", [(lambda s: sview(s, 0, 128, [8, 256]), kview(w, 0, 8, 256 * (c // 2), 256))])
            U("Q0", [(lambda s: sview(s, 0, 128, [8, 256]), kview(w, 0, 8, 3 * MIXW, 256))])
            wo = w_out_d[0]
            for op in range(4):
                U(f"WO0_{op}", [(lambda s: sview(s, 0, 128, [8, 256]), kview(wo, 0, 8, 256 * op, 256))])
        else:
            w = pool_in_d[0]
            for g in range(4):
                U(f"PW_{g}", [(lambda s: sview(s, 0, 128, [8, 192]), kview(w, 0, 8, 192 * g, 192))])
            U("Q1", [(lambda s: sview(s, 0, 128, [8, 256]), kview(w, 0, 8, MIXW, 256))])
            pg = pool_g_d[0].rearrange("g (kc p) n -> p g kc n", p=96)
            U("PG", [(lambda s: sview(s, 0, 96, [4, 2, 192]), pg)])
            wo = w_out_d[1]
            for op in range(4):
                U(f"WO1_{op}", [
                    (lambda s: sview(s, 0, 96, [8, 256]), kview(wo, 0, 8, 256 * op, 256, P=96)),
                    (lambda s: sview(s, 2048, 128, [2, 256]), kview(wo, MIXW, 2, 256 * op, 256)),
                ])

    def kv_unit(name):
        l = int(name[3])
        c0 = 0 if name[2] == "K" else 256
        U(name, [(lambda s: sview(s, 0, 128, [8, 256]), kview(w_kv_d[l], 0, 8, c0, 256))])

    stage_list = []
    for l in range(2):
        stage_list += [("ffn", l, 0), ("mix", l, 0), ("ffn", l, 1)]
    stage_list = stage_list[:nstages]
    KV_AFTER_FP = {3: "KVK0", 4: "KVV0", 6: "KVK1", 7: "KVV1"}
    defer_prologue = bool(stage_list) and stage_list[0][0] == "ffn"
    if not defer_prologue:
        for nm in ("KVK0", "KVV0", "KVK1", "KVV1"):
            kv_unit(nm)
    for t in range(NT):
        for si, (kind, l, which) in enumerate(stage_list):
            if kind == "ffn":
                list_ffn_units(l, which, kv=(defer_prologue and t == 0 and si == 0))
            else:
                list_mix_units(l)

    def mm_group(out_ap, pairs, reads, psB):
        n = len(pairs)

        def fn(e):
            ins = None
            for i, (l, r) in enumerate(pairs):
                ins = e.matmul(out_ap, l, r, start=(i == 0), stop=(i == n - 1))
            return ins

        return S.add("pe", fn, reads=reads, writes=(psB,))

    def act(out, in_, func, reads, writes, scale=None, bias=None):
        kw = {}
        if scale is not None:
            kw["scale"] = scale
        if bias is not None:
            kw["bias"] = bias

        def fn(e):
            return e.activation(out=out, in_=in_, func=func, **kw)

        return S.add("act", fn, reads=reads, writes=writes)

    def dve_tt(out, in0, in1, op, reads, writes, eng="dve"):
        def fn(e):
            return e.tensor_tensor(out=out, in0=in0, in1=in1, op=op)

        return S.add(eng, fn, reads=reads, writes=writes)

    def dve_ts(out, in0, s1, s2, op0, op1, reads, writes, eng="dve"):
        def fn(e):
            if op1 is None:
                return e.tensor_scalar(out=out, in0=in0, scalar1=s1, scalar2=None, op0=op0)
            return e.tensor_scalar(out=out, in0=in0, scalar1=s1, scalar2=s2, op0=op0, op1=op1)

        return S.add(eng, fn, reads=reads, writes=writes)

    def dve_stt(out, in0, scalar, in1, op0, op1, reads, writes):
        def fn(e):
            return e.scalar_tensor_tensor(out=out, in0=in0, scalar=scalar, in1=in1, op0=op0, op1=op1)

        return S.add("dve", fn, reads=reads, writes=writes)

    def memset(eng, ap, val, writes, reads=()):
        def fn(e):
            return e.memset(ap, val)

        return S.add(eng, fn, reads=reads, writes=writes)

    def norm_block(src_fn, sbufs_r, sl, gcol, dst_fn, dst_bufs_fn, extra=()):
        n = sl.stop - sl.start
        i2 = state["nrm"] % 2
        state["nrm"] += 1
        src = src_fn(sl)
        for hf in range(2):
            act(sqb[:, 4 * hf:4 * hf + 4, 0:n], src[:, 4 * hf:4 * hf + 4, :], AF.Square,
                reads=sbufs_r[4 * hf:4 * hf + 4], writes=(B.sq[hf],))
        ps, psB = ps_next()
        mm_group(ps[:, 0:n], [(ones[:, :], sqb[:, c, 0:n]) for c in range(8)],
                 reads=(B.sq[0], B.sq[1], B.const), psB=psB)
        act(msb[i2][:, 0:n], ps[:, 0:n], AF.Ln, reads=(psB,), writes=(B.ms[i2],), scale=1.0 / D, bias=EPS)
        act(rsb[i2][:, 0:n], msb[i2][:, 0:n], AF.Exp, reads=(B.ms[i2],), writes=(B.rs[i2],), scale=-0.5)
        for c in range(8):
            dve_stt(dst_fn(c, sl), src[:, c, :], cst[:, gcol + c:gcol + c + 1], rsb[i2][:, 0:n],
                    ALU.mult, ALU.mult,
                    reads=(sbufs_r[c], B.rs[i2], B.const) + tuple(extra), writes=dst_bufs_fn(c))

    tile_blocks = [blk(tb) for tb in range(NB)]
    fillers = []

    def run_filler():
        if fillers:
            fillers.pop(0)()

    def flush_fillers():
        while fillers:
            fillers.pop(0)()

    def fence_A():
        for e_ in ("pe", "act", "dve", "pool"):
            if S.cnt[e_]:
                B.A.w[e_] = S.cnt[e_]

    pend = {"gcol": None, "dst": "h", "done": set()}

    def set_pend(gcol, dst="h"):
        pend["gcol"] = gcol
        pend["dst"] = dst
        pend["done"] = set()

    def need_h(tb):
        if pend["gcol"] is None or tb in pend["done"]:
            return
        pend["done"].add(tb)
        xb = [B.x[c][tb] for c in range(8)]
        if pend["dst"] == "h":
            norm_block(lambda sl: xT[:, :, sl], xb, blk(tb), pend["gcol"],
                       lambda c, sl: hT[:, c, sl], lambda c: (B.h[c][tb],))
        else:
            fence_A()
            norm_block(lambda sl: xT[:, :, sl], xb, blk(tb), pend["gcol"],
                       lambda c, sl: yT[:, c, sl], lambda c: (B.yT[c][tb],), extra=(B.A,))

    def ffn(l, which, nxt):
        fence_A()
        for fp in range(11):
            gs, gB = wnext(f"G{l}{which}_{fp}")
            us, uB = wnext(f"U{l}{which}_{fp}")
            gv = sview(gs, 0, 128, [8, 256])
            uv = sview(us, 0, 128, [8, 256])
            for tb in range(NB):
                for f in (2 * fp, 2 * fp + 1):
                    cs = slice((f % 2) * 128, (f % 2) * 128 + 128)
                    sl = blk(tb)
                    need_h(tb)
                    pg, pgB = ps_next()
                    pu, puB = ps_next()
                    mm_group(pg[:, 0:BS], [(gv[:, k, cs], hT[:, k, sl]) for k in range(8)],
                             reads=[gB] + hB(tb), psB=pgB)
                    mm_group(pu[:, 0:BS], [(uv[:, k, cs], hT[:, k, sl]) for k in range(8)],
                             reads=[uB] + hB(tb), psB=puB)
                    ti = state["tmp"] % 3
                    state["tmp"] += 1
                    act(tmpb[ti][:, :], pg[:, 0:BS], AF.Silu, reads=(pgB,), writes=(B.tmp[ti],))
                    dve_tt(aT[:, f, sl], tmpb[ti][:, :], pu[:, 0:BS], ALU.mult,
                           reads=(B.tmp[ti], puB, B.A), writes=(B.a[f][tb],))
                    if fp == 0 and tb == 0 and f == 0:
                        need_h(NB - 1)
            run_filler()
        for op in range(4):
            d0, d0B = wnext(f"D{l}{which}_{op}_0")
            d1, d1B = wnext(f"D{l}{which}_{op}_1")
            dv = [sview(d0, 0, 128, [11, 256]), sview(d1, 0, 128, [11, 256])]
            order = ([(o, tb) for o in (2 * op, 2 * op + 1) for tb in range(NB)] if op < 3 else
                     [(o, tb) for tb in range(NB) for o in (2 * op, 2 * op + 1)])
            for (o, tb) in order:
                cs = slice((o % 2) * 128, (o % 2) * 128 + 128)
                sl = blk(tb)
                ps, psB = ps_next()
                mm_group(ps[:, 0:BS], [(dv[k // 11][:, k % 11, cs], aT[:, k, sl]) for k in range(22)],
                         reads=[d0B, d1B] + [B.a[k][tb] for k in range(22)], psB=psB)
                dve_stt(xT[:, o, sl], ps[:, 0:BS], 0.5, xT[:, o, sl], ALU.mult, ALU.add,
                        reads=(psB, B.x[o][tb]), writes=(B.x[o][tb],))
                if o == 7:
                    if tb == 0:
                        set_pend(*nxt)
                    else:
                        need_h(tb - 1)
        need_h(NB - 2)

    def q_proj(name, ws, wB):
        wv = sview(ws, 0, 128, [8, 256])
        for j in range(2):
            cs = slice(j * 128, j * 128 + 128)
            for tb in range(NB):
                sl = blk(tb)
                ps, psB = ps_next()
                mm_group(ps[:, 0:BS], [(wv[:, k, cs], hT[:, k, sl]) for k in range(8)],
                         reads=[wB] + hB(tb), psB=psB)
                act(qb[:, j, sl], ps[:, 0:BS], AF.Copy, reads=(psB,), writes=(B.q[j][tb],))

    def attention(l):
        steps = [(tb, j) for tb in range(NB) for j in range(2)]
        eis = {}

        def scores(tb, j):
            sl = blk(tb)
            ei = state["e"] % 2
            state["e"] += 1
            eis[(tb, j)] = ei
            for hh in range(2):
                r = slice(64 * hh, 64 * hh + 64)
                for mc in range(2):
                    ps, psB = ps_next()
                    mm_group(ps[:, 0:BS], [(kT[r, l, j, mc * 128:(mc + 1) * 128], qb[r, j, sl])],
                             reads=(B.kv, B.q[j][tb]), psB=psB)
                    act(ebuf[ei][:, hh * 2 + mc, :], ps[:, 0:BS], AF.Exp, reads=(psB,),
                        writes=(B.e[ei][hh * 2 + mc],), scale=0.125)

        def pv(tb, j):
            sl = blk(tb)
            ei = eis[(tb, j)]
            ppv, ppvB = ps_next()
            psm, psmB = ps_next()
            for hh in range(2):
                r = slice(64 * hh, 64 * hh + 64)
                h = 2 * j + hh
                mm_group(ppv[r, 0:BS],
                         [(vS[:, l, mc, h * 64:(h + 1) * 64], ebuf[ei][:, hh * 2 + mc, :]) for mc in range(2)],
                         reads=[B.kv] + B.e[ei][2 * hh:2 * hh + 2], psB=ppvB)
                mm_group(psm[r, 0:BS],
                         [(ones[:, 0:64], ebuf[ei][:, hh * 2 + mc, :]) for mc in range(2)],
                         reads=[B.const] + B.e[ei][2 * hh:2 * hh + 2], psB=psmB)
            act(rcb[ei][:, :], psm[:, 0:BS], AF.Ln, reads=(psmB,), writes=(B.rc[ei],))
            act(rcb[ei][:, :], rcb[ei][:, :], AF.Exp, reads=(B.rc[ei],), writes=(B.rc[ei],), scale=-1.0)
            dve_tt(attb[:, j, sl], ppv[:, 0:BS], rcb[ei][:, :], ALU.mult,
                   reads=(ppvB, B.rc[ei], B.A), writes=(B.att[j][tb],))

        scores(*steps[0])
        for i, st in enumerate(steps):
            if i + 1 < len(steps):
                scores(*steps[i + 1])
            pv(*st)
            run_filler()

    def w_out_stage(l, nxt):
        for op in range(4):
            ws, wB = wnext(f"WO{l}_{op}")
            if l == 0:
                wv = sview(ws, 0, 128, [8, 256])
            else:
                wa = sview(ws, 0, 96, [8, 256])
                wb_ = sview(ws, 2048, 128, [2, 256])
            order = ([(o, tb) for o in (2 * op, 2 * op + 1) for tb in range(NB)] if op < 3 else
                     [(o, tb) for tb in range(NB) for o in (2 * op, 2 * op + 1)])
            for (o, tb) in order:
                cs = slice((o % 2) * 128, (o % 2) * 128 + 128)
                if True:
                    sl = blk(tb)
                    ps, psB = ps_next()
                    if l == 0:
                        pairs = [(wv[:, k, cs], mixb[:, k, sl]) for k in range(6)]
                        pairs += [(wv[:, 6 + j, cs], attb[:, j, sl]) for j in range(2)]
                        rd = [wB] + [B.mix[k][tb] for k in range(6)] + [B.att[j][tb] for j in range(2)]
                    else:
                        pairs = [(wa[:, k, cs], mixb[0:96, k, sl]) for k in range(8)]
                        pairs += [(wb_[:, j, cs], attb[:, j, sl]) for j in range(2)]
                        rd = [wB] + [B.mix[k][tb] for k in range(8)] + [B.att[j][tb] for j in range(2)]
                    mm_group(ps[:, 0:BS], pairs, reads=rd, psB=psB)
                    dve_tt(xT[:, o, sl], ps[:, 0:BS], xT[:, o, sl], ALU.add,
                           reads=(psB, B.x[o][tb]), writes=(B.x[o][tb],))
                    if o == 7:
                        if tb == 0:
                            set_pend(*nxt)
                        else:
                            need_h(tb - 1)
        need_h(NB - 2)

    def mixer_conv(first_tile, nxt):
        flush_fillers()
        fence_A()
        for i in range(2):
            memset("dve", ubuf[i][:, 0:2], 0.0, (B.upad[i],), reads=(B.A,))
        for c in range(6):
            ws, wB = wnext(f"GCV_{c}")
            wv = sview(ws, 0, 128, [8, 256])
            ui = c % 2
            for tb in range(NB):
                sl = blk(tb)
                need_h(tb)
                p1, p1B = ps_next()
                p2, p2B = ps_next()
                mm_group(p1[:, 0:BS], [(wv[:, k, 0:128], hT[:, k, sl]) for k in range(8)],
                         reads=[wB] + hB(tb), psB=p1B)
                mm_group(p2[:, 0:BS], [(wv[:, k, 128:256], hT[:, k, sl]) for k in range(8)],
                         reads=[wB] + hB(tb), psB=p2B)
                ti = state["tmp"] % 3
                state["tmp"] += 1
                act(tmpb[ti][:, :], p2[:, 0:BS], AF.Copy, reads=(p2B,), writes=(B.tmp[ti],))
                dve_tt(ubuf[ui][:, 2 + tb * BS:2 + (tb + 1) * BS], p1[:, 0:BS], tmpb[ti][:, :], ALU.mult,
                       reads=(p1B, B.tmp[ti], B.A), writes=(B.u[ui][tb],))
                if c == 0 and tb == 0:
                    need_h(NB - 1)
            if first_tile:
                dve_ts(ubuf[ui][:, 2:2 + HALO], ubuf[ui][:, 2:2 + HALO], cst[:, C_MASK:C_MASK + 1], None,
                       ALU.mult, None, reads=(B.u[ui][0], B.const), writes=(B.u[ui][0],))
            cw = lambda tap: cst[:, C_CONV + 3 * c + tap:C_CONV + 3 * c + tap + 1]
            dve_ts(ybuf[ui][:, :], ubuf[ui][:, 2:2 + TT], cw(2), None, ALU.mult, None,
                   reads=B.u[ui] + [B.upad[ui], B.const, B.A], writes=(B.y[ui],))
            dve_stt(ybuf[ui][:, :], ubuf[ui][:, 1:1 + TT], cw(1), ybuf[ui][:, :], ALU.mult, ALU.add,
                    reads=B.u[ui] + [B.upad[ui], B.y[ui], B.const], writes=(B.y[ui],))
            dve_stt(ybuf[ui][:, :], ubuf[ui][:, 0:TT], cw(0), ybuf[ui][:, :], ALU.mult, ALU.add,
                    reads=B.u[ui] + [B.upad[ui], B.y[ui], B.const], writes=(B.y[ui],))
            if c % 2 == 1:
                ws2, wB2 = wnext(f"GB_{c // 2}")
                wv2 = sview(ws2, 0, 128, [8, 256])
                for cc in (c - 1, c):
                    cs = slice((cc % 2) * 128, (cc % 2) * 128 + 128)
                    for tb in range(NB):
                        sl = blk(tb)
                        ps, psB = ps_next()
                        mm_group(ps[:, 0:BS], [(wv2[:, k, cs], hT[:, k, sl]) for k in range(8)],
                                 reads=[wB2] + hB(tb), psB=psB)
                        dve_tt(mixb[:, cc, sl], ps[:, 0:BS], ybuf[cc % 2][:, sl], ALU.mult,
                               reads=(psB, B.y[cc % 2], B.A), writes=(B.mix[cc][tb],))
        ws, wB = wnext("Q0")
        q_proj("Q0", ws, wB)
        attention(0)
        w_out_stage(0, nxt)

    def mixer_pool(first_tile, nxt):
        fence_A()
        for i in range(4):
            memset("dve", pbuf[i][:, 0:PADL], 0.0, (B.ppad[i],), reads=(B.A,))
        for gp in range(2):
            wvs = []
            for g in (2 * gp, 2 * gp + 1):
                ws, wB = wnext(f"PW_{g}")
                wvs.append((sview(ws, 0, 128, [8, 192]), wB))
            chunks = [4 * gp + ii for ii in range(4)]
            for tb in range(NB):
                sl = blk(tb)
                need_h(tb)
                for i in chunks:
                    wv, wB = wvs[(i // 2) % 2]
                    pi = i % 4
                    cs = slice((i % 2) * 96, (i % 2) * 96 + 96)
                    ps, psB = ps_next()
                    mm_group(ps[0:96, 0:BS], [(wv[:, k, cs], hT[:, k, sl]) for k in range(8)],
                             reads=[wB] + hB(tb), psB=psB)
                    act(pbuf[pi][0:96, PADL + tb * BS:PADL + (tb + 1) * BS], ps[0:96, 0:BS], AF.Copy,
                        reads=(psB, B.A), writes=(B.p[pi][tb],))
                    if gp == 0 and tb == 0 and i == chunks[1]:
                        need_h(NB - 1)
            def chain(i):
                g = i // 2
                win = 2 << g
                pi = i % 4
                P = pbuf[pi]
                if first_tile:
                    dve_ts(P[0:96, PADL:PADL + HALO], P[0:96, PADL:PADL + HALO], cst[0:96, C_MASK:C_MASK + 1],
                           None, ALU.mult, None, reads=(B.p[pi][0], B.const), writes=(B.p[pi][0],))
                pB = B.p[pi] + [B.ppad[pi]]
                cur, curB = P, pB
                sh = 1
                si = 0
                while sh < win:
                    lo = 2 * sh - 1
                    dst, dstB = sbufs[si], B.s[si]
                    dve_tt(dst[0:96, lo:PW], cur[0:96, lo:PW], cur[0:96, lo - sh:PW - sh], ALU.add,
                           reads=list(curB) + [B.A], writes=(dstB,))
                    cur, curB = dst, [dstB]
                    si ^= 1
                    sh *= 2
                if first_tile:
                    c0 = PADL + HALO
                    dve_tt(cur[0:96, c0:c0 + 16], cur[0:96, c0:c0 + 16],
                           cst[0:96, C_CORR + 16 * g:C_CORR + 16 * g + 16], ALU.mult,
                           reads=list(curB) + [B.const], writes=curB)
                dve_stt(dbuf[0:96, i, :], cur[0:96, PADL:PW], 1.0 / win, P[0:96, PADL:PW], ALU.mult, ALU.subtract,
                        reads=list(curB) + pB + [B.A], writes=(B.d[i],))

            for i in chunks:
                if gp == 0:
                    chain(i)
                else:
                    fillers.append(lambda i=i: chain(i))
        ws, wB = wnext("Q1")
        q_proj("Q1", ws, wB)
        attention(1)
        flush_fillers()
        ws, wB = wnext("PG")
        pgv = sview(ws, 0, 96, [4, 2, 192])
        for g in range(4):
            for oc in range(2):
                i = 2 * g + oc
                for tb in range(NB):
                    sl = blk(tb)
                    ps, psB = ps_next()
                    mm_group(ps[0:96, 0:BS],
                             [(pgv[:, g, kc, oc * 96:(oc + 1) * 96], dbuf[0:96, 2 * g + kc, sl]) for kc in range(2)],
                             reads=(wB, B.d[2 * g], B.d[2 * g + 1]), psB=psB)
                    act(mixb[0:96, i, sl], ps[0:96, 0:BS], AF.Identity, reads=(psB, B.const),
                        writes=(B.mix[i][tb],), scale=cst[0:96, C_PSC + i:C_PSC + i + 1])
        w_out_stage(1, nxt)

    def dma_sp(out, in_, reads, writes, sem):
        def fn(e):
            return e.dma_start(out=out, in_=in_)

        return S.add("sp", fn, reads=reads, writes=writes, dma=sem)

    dma_sp(cst[:, :], cst_d[:, :], (), (B.const,), const_sem)
    dma_sp(ident[:, :], ident_d[:, :], (), (B.const,), const_sem)
    memset("dve", ones[:, :], 1.0, (B.const,))

    def load_transpose(src_d, row0, nrows, dst_fn, dst_bufs_fn, extra=(), blocks=None):
        nblk = (nrows + 127) // 128
        for b_ in (range(nblk) if blocks is None else blocks):
            rows = min(128, nrows - b_ * 128)
            xi = b_ % 2
            stg = xo[xi]
            dma_sp(stg[0:rows, :], src_d[row0 + b_ * 128:row0 + b_ * 128 + rows, :], (), B.xo[xi],
                   xo_sem[xi])
            for half in range(2):
                ps, psB = ps_next()

                def fn(e, ps=ps, stg=stg, rows=rows, half=half):
                    ins = None
                    for cc in range(4):
                        c = half * 4 + cc
                        ins = e.transpose(ps[:, cc * 128:cc * 128 + rows], stg[0:rows, c * 128:(c + 1) * 128],
                                          ident[0:rows, 0:rows])
                    return ins

                S.add("pe", fn, reads=list(B.xo[xi]) + [B.const], writes=(psB,))
                src = ps[:, :].rearrange("p (a b) -> p a b", a=4)[:, :, 0:rows]
                dst = dst_fn(half * 4, b_ * 128, rows)
                act(dst, src, AF.Copy, reads=(psB,) + tuple(extra), writes=dst_bufs_fn(half * 4, b_ * 128, rows))

    def pro_xpose(b_):
        load_transpose(mem_d, 0, NMEM, lambda c0, t0, n: memT[:, c0:c0 + 4, t0:t0 + n],
                       lambda c0, t0, n: (B.memT,), blocks=[b_])

    def pro_norm(l):
        norm_block(lambda sl: memT[:, :, sl], [B.memT] * 8, slice(0, NMEM), C_MEM + 8 * l,
                   lambda c, sl: memh[:, c, sl], lambda c: (B.memh[c],))

    def pro_k(l):
        ws, wB = wnext(f"KVK{l}")
        wv = sview(ws, 0, 128, [8, 256])
        for j in range(2):
            ps, psB = ps_next()
            mm_group(ps[:, 0:NMEM], [(wv[:, k, j * 128:(j + 1) * 128], memh[:, k, 0:NMEM]) for k in range(8)],
                     reads=[wB] + B.memh, psB=psB)
            act(kT[:, l, j, :], ps[:, 0:NMEM], AF.Copy, reads=(psB,), writes=(B.kv,))

    def pro_v(l):
        ws, wB = wnext(f"KVV{l}")
        wv = sview(ws, 0, 128, [8, 256])
        for mc in range(2):
            ps, psB = ps_next()
            mm_group(ps[:, 0:256], [(memh[:, k, mc * 128:(mc + 1) * 128], wv[:, k, :]) for k in range(8)],
                     reads=[wB] + B.memh, psB=psB)
            act(vS[:, l, mc, :], ps[:, 0:256], AF.Copy, reads=(psB,), writes=(B.kv,))

    pro_steps = [lambda: pro_xpose(0), lambda: pro_xpose(1), lambda: pro_norm(0), lambda: pro_k(0),
                 lambda: pro_v(0), lambda: pro_norm(1), lambda: pro_k(1), lambda: pro_v(1)]
    if defer_prologue:
        fillers.extend(pro_steps)
    else:
        for st_ in pro_steps:
            st_()

    def xbufs(c0, t0, n):
        tbs = sorted(set([t0 // BS, (t0 + n - 1) // BS]))
        return [B.x[c][tb] for c in range(c0, c0 + 4) for tb in tbs]

    def stage_gcol(kind, l, which):
        if kind == "ffn":
            return (C_FFN1 if which == 0 else C_FFN2) + 8 * l
        return C_MIX + 8 * l

    NIB = (TT + 127) // 128
    NOB = OWN // 128
    in_slot = {}

    def in_load(t, b_):
        rows = min(128, TT - b_ * 128)
        xi = state["xio"] % 3
        state["xio"] += 1
        in_slot[(t, b_)] = xi
        r0 = t * OWN + b_ * 128
        dma_sp(xio[xi][0:rows, :], x_d[r0:r0 + rows, :], (), (B.xio[xi],), xio_sem[xi])

    def in_xpose(t, b_):
        rows = min(128, TT - b_ * 128)
        xi = in_slot[(t, b_)]
        for half in range(2):
            ps, psB = ps_next()

            def fn(e, ps=ps, xi=xi, rows=rows, half=half):
                ins = None
                for cc in range(4):
                    c = half * 4 + cc
                    ins = e.transpose(ps[:, cc * 128:cc * 128 + rows], xio[xi][0:rows, c * 128:(c + 1) * 128],
                                      ident[0:rows, 0:rows])
                return ins

            S.add("pe", fn, reads=(B.xio[xi], B.const), writes=(psB,))
            src = ps[:, :].rearrange("p (a b) -> p a b", a=4)[:, :, 0:rows]
            act(xT[:, half * 4:half * 4 + 4, b_ * 128:b_ * 128 + rows], src, AF.Copy, reads=(psB,),
                writes=xbufs(half * 4, b_ * 128, rows))

    def out_block(t, ob, srcT, srcB):
        col0 = HALO + ob * 128
        tbs = sorted(set([col0 // BS, (col0 + 127) // BS]))
        xi = state["xo"] % 2
        state["xo"] += 1
        for half in range(2):
            ps, psB = ps_next()

            def fn(e, ps=ps, half=half, col0=col0, srcT=srcT):
                ins = None
                for cc in range(4):
                    c = half * 4 + cc
                    ins = e.transpose(ps[:, cc * 128:(cc + 1) * 128], srcT[:, c, col0:col0 + 128], ident[:, :])
                return ins

            rd = [srcB(c, tb) for c in range(half * 4, half * 4 + 4) for tb in tbs] + [B.const]
            S.add("pe", fn, reads=rd, writes=(psB,))
            act(xo[xi][:, half * 512:(half + 1) * 512], ps[:, :], AF.Copy, reads=(psB,), writes=(B.xo[xi][half],))

        def fn2(e, dst=y_d[t * OWN + ob * 128:t * OWN + (ob + 1) * 128, :], src=xo[xi][:, :]):
            return e.dma_start(out=dst, in_=src)

        S.add("act", fn2, reads=B.xo[xi], writes=(), dma=xo_sem[xi])

    if final_norm:
        srcT, srcB = yT, (lambda c, tb: B.yT[c][tb])
    else:
        srcT, srcB = xT, (lambda c, tb: B.x[c][tb])

    def first_pend():
        if stage_list:
            set_pend(stage_gcol(*stage_list[0]))
        elif final_norm:
            set_pend(C_FIN, "y")
        else:
            set_pend(None)

    for b_ in range(3):
        in_load(0, b_)
    for t in range(NT):
        seq = []
        outs = [("out", ob) for ob in range(NOB)] if t > 0 else []
        for b_ in range(NIB):
            seq.append(("in", b_))
            if b_ == 1:
                seq.append(("flush", 0))
            if b_ >= 1 and outs:
                seq.append(outs.pop(0))
            if b_ == 4:
                seq.append(("norm", 0))
            if b_ == 7:
                seq.append(("norm", 1))
        seq += outs
        for (kind, v) in seq:
            if kind == "in":
                in_xpose(t, v)
                if v + 3 < NIB:
                    in_load(t, v + 3)
                elif t + 1 < NT:
                    in_load(t + 1, v + 3 - NIB)
            elif kind == "flush":
                if t > 0 and final_norm:
                    for tb in range(NB):
                        need_h(tb)
                first_pend()
            elif kind == "out":
                out_block(t - 1, v, srcT, srcB)
            elif kind == "norm":
                if not (not stage_list and final_norm):
                    need_h(v)
        for si, (kind, l, which) in enumerate(stage_list):
            if si + 1 < len(stage_list):
                nxt = (stage_gcol(*stage_list[si + 1]), "h")
            elif final_norm:
                nxt = (C_FIN, "y")
            else:
                nxt = (None, "h")
            if kind == "ffn":
                ffn(l, which, nxt)
            elif l == 0:
                mixer_conv(t == 0, nxt)
            else:
                mixer_pool(t == 0, nxt)
    if final_norm:
        for tb in range(NB):
            need_h(tb)
    for ob in range(NOB):
        out_block(NT - 1, ob, srcT, srcB)
    assert wstate["next"] == len(units), (wstate["next"], len(units))

    for k in list(sems.keys()):
        sems[k] = nc.alloc_semaphore(k)
    for e in ("pe", "act", "dve", "pool"):
        sems[e] = nc.alloc_semaphore("t_" + e)

    def replay(name, eng):
        for waits, fn, inc in S.ops[name]:
            for k, v in waits:
                eng.wait_ge(sems[k], v)
            ins = fn(eng)
            ins.then_inc(sems[inc[0]], inc[1])

    with nc.Block() as block:
        @block.tensor
        def _(e):
            replay("pe", e)

        @block.scalar
        def _(e):
            replay("act", e)

        @block.vector
        def _(e):
            replay("dve", e)

        @block.gpsimd
        def _(e):
            replay("pool", e)

        @block.sync
        def _(e):
            replay("sp", e)
            for ds in out_sems:
                if ds.count:
                    e.wait_ge(sems[ds.key], ds.count)

    stats = {k: len(v) for k, v in S.ops.items()}
    return nc, stats


_CACHE = {}


def _get_program(NT, nstages=6, final_norm=True):
    key = (NT, nstages, final_norm)
    if key not in _CACHE:
        _CACHE[key] = build_program(NT, nstages, final_norm)[0]
    return _CACHE[key]


def _const_pack(inp, mask_val, corr_on):
    c = np.zeros((128, NCST), np.float32)

    def put(col, vec2d):
        for i in range(vec2d.shape[0]):
            c[:, col + 8 * i:col + 8 * i + 8] = vec2d[i].reshape(8, 128).T

    put(C_FFN1, inp["ffn1_norm"])
    put(C_MIX, inp["mix_norm"])
    put(C_MEM, inp["mem_norm"])
    put(C_FFN2, inp["ffn2_norm"])
    put(C_FIN, inp["final_norm"][None, :])
    cw = inp["conv_w"][0]
    for ch in range(6):
        for tap in range(3):
            c[:, C_CONV + 3 * ch + tap] = cw[tap, ch * 128:(ch + 1) * 128]
    ps = inp["pool_scale"][0]
    for i in range(8):
        c[0:96, C_PSC + i] = ps[i * 96:(i + 1) * 96]
    c[:, C_MASK] = mask_val
    for g in range(4):
        w = 2 << g
        for t in range(16):
            c[:, C_CORR + 16 * g + t] = (float(w) / min(t + 1, w)) if corr_on else 1.0
    return c


def _prep_inputs(inputs):
    inp = {k: np.ascontiguousarray(np.asarray(v), dtype=np.float32) for k, v in inputs.items()}
    return inp


_WKEYS = ["ffn1_w_gu", "ffn2_w_gu", "ffn1_w_down", "ffn2_w_down", "w_kv", "w_out",
          "conv_w_in", "pool_w_in", "pool_w_group"]


def _core_x(inp, core):
    b, half = core // 2, core % 2
    x = inp["x"]
    xs = np.zeros((4096 + HALO, D), np.float32)
    if half == 0:
        xs[HALO:] = x[b, 0:4096]
    else:
        xs[:] = x[b, 4096 - HALO:8192]
    return xs


def kernel(**inputs):
    inp = _prep_inputs(inputs)
    NT = 4
    nc = _get_program(NT)
    ident = np.eye(128, dtype=np.float32)
    in_maps = []
    for core in range(NCORE):
        b, half = core // 2, core % 2
        m = {"x": _core_x(inp, core), "mem": inp["mem"][b], "ident": ident,
             "cst": _const_pack(inp, 0.0 if half == 0 else 1.0, half == 0)}
        for k in _WKEYS:
            m[k] = inp[k]
        in_maps.append(m)
    res = run_bass_kernel_spmd(nc, in_maps, core_ids=list(range(NCORE)))
    out = np.empty((4, 8192, D), np.float32)
    for core in range(NCORE):
        b, half = core // 2, core % 2
        out[b, half * 4096:(half + 1) * 4096] = res.results[core]["y"]
    return out
```
